# Optimizing a Trainium2 kernel written in Bass

```python
import math
import jax, jax.numpy as jnp
from jax import lax
import numpy as np

D_MODEL = 2048
BATCH = 8
SEQ = 2048
DEPTH = 4
DEC_BATCH = 8
DEC_SEQ = 32
PAST_LEN = 2048

CHUNK = 64
N_META = 16
D_MIX = D_MODEL
D_A = D_MIX // 2
N_BLOCKS_A = 16
BS_A = D_A // N_BLOCKS_A
CONV_W = 4
RG_C = 8.0
N_HEADS_B = 8
DK = 64
DV = 2 * DK
D_B = N_HEADS_B * DV
QK_W = N_HEADS_B * 2 * DK
IN_COLS = 2 * D_A + 2 * QK_W + 2 * D_B
NUM_BUCKETS = 32
REL_MAX_DIST = 1024
QBLOCK = 128
EPS = 1e-6

kernel_name = "hymba_rglru_diffattn_stream_step"


def rms_norm(x, g):
    xf = x.astype(jnp.float32)
    y = xf * lax.rsqrt(jnp.mean(xf * xf, axis=-1, keepdims=True) + EPS)
    return (y * g.astype(jnp.float32)).astype(x.dtype)


def rel_bucket(rel):
    half = NUM_BUCKETS // 2
    max_exact = half // 2
    ret = jnp.where(rel > 0, half, 0).astype(jnp.int32)
    n = jnp.abs(rel).astype(jnp.int32)
    nf = jnp.maximum(n, 1).astype(jnp.float32)
    large = max_exact + (jnp.log(nf / max_exact) / math.log(REL_MAX_DIST / max_exact)
                         * (half - max_exact)).astype(jnp.int32)
    large = jnp.minimum(large, half - 1)
    return ret + jnp.where(n < max_exact, n, large)


def rel_bias_block(qpos, kpos, rel_bias):
    b = rel_bias.astype(jnp.float32)[rel_bucket(kpos[None, :] - qpos[:, None])]
    return jnp.transpose(b, (2, 0, 1))


def chunk_id(pos):
    return jnp.where(pos < N_META, -1, (pos - N_META) // CHUNK)


def diff_attend(q1, q2, k1, k2, v, bias, mask, lam):
    scale = DK ** -0.5
    l1 = jnp.einsum('bqhd,bkhd->bhqk', q1, k1).astype(jnp.float32) * scale + bias
    l2 = jnp.einsum('bqhd,bkhd->bhqk', q2, k2).astype(jnp.float32) * scale + bias
    if mask is not None:
        l1 = jnp.where(mask, l1, -jnp.inf)
        l2 = jnp.where(mask, l2, -jnp.inf)
    p = jax.nn.softmax(l1, axis=-1) - lam * jax.nn.softmax(l2, axis=-1)
    return jnp.einsum('bhqk,bkhd->bqhd', p.astype(v.dtype), v)


def diff_attn_prompt(q1, q2, k1, k2, v, rel_bias, lam):
    B, L = q1.shape[0], q1.shape[1]
    nblk = -(-L // QBLOCK)
    Lp = nblk * QBLOCK
    pad = ((0, 0), (0, Lp - L), (0, 0), (0, 0))
    q1p = jnp.pad(q1, pad)
    q2p = jnp.pad(q2, pad)
    kpos = jnp.arange(L)
    kchunk = chunk_id(kpos)

    def block(i):
        s = i * QBLOCK
        qpos = s + jnp.arange(QBLOCK)
        qb1 = lax.dynamic_slice_in_dim(q1p, s, QBLOCK, axis=1)
        qb2 = lax.dynamic_slice_in_dim(q2p, s, QBLOCK, axis=1)
        bias = rel_bias_block(qpos, kpos, rel_bias)
        mask = kchunk[None, :] <= chunk_id(qpos)[:, None]
        return diff_attend(qb1, qb2, k1, k2, v, bias, mask, lam)

    out = lax.map(block, jnp.arange(nblk))
    out = jnp.moveaxis(out, 0, 1).reshape(B, Lp, N_HEADS_B, DV)
    return out[:, :L]


def diff_attn_sample(q1, q2, k1_all, k2_all, v_all, rel_bias, lam):
    T = q1.shape[1]
    K = k1_all.shape[1]
    P = K - T
    kpos = jnp.arange(K)
    qpos = P + jnp.arange(T)
    bias = rel_bias_block(qpos, kpos, rel_bias)
    return diff_attend(q1, q2, k1_all, k2_all, v_all, bias, None, lam)


def lin_scan(a, b, h0):
    b = b.at[:, 0].add(a[:, 0] * h0)

    def comb(left, right):
        return (left[0] * right[0], right[0] * left[1] + right[1])

    _, h = lax.associative_scan(comb, (a, b), axis=1)
    return h


def rglru_branch(xa, conv_buf, h0, conv_w, conv_b, wr, br, wi, bi, lam_param):
    B, T = xa.shape[0], xa.shape[1]
    xp = jnp.concatenate([conv_buf.astype(xa.dtype), xa], axis=1)
    xc = conv_b
    for j in range(CONV_W):
        xc = xc + xp[:, j:j + T] * conv_w[j]
    new_buf = xp[:, -(CONV_W - 1):]
    xb = xc.reshape(B, T, N_BLOCKS_A, BS_A)
    r = jax.nn.sigmoid(jnp.einsum('btnc,ncd->btnd', xb, wr) + br).reshape(B, T, D_A)
    i = jax.nn.sigmoid(jnp.einsum('btnc,ncd->btnd', xb, wi) + bi).reshape(B, T, D_A)
    log_a = -RG_C * r.astype(jnp.float32) * jax.nn.softplus(-lam_param.astype(jnp.float32))
    a = jnp.exp(log_a)
    bterm = jnp.sqrt(-jnp.expm1(2.0 * log_a)) * (i * xc).astype(jnp.float32)
    h = lin_scan(a, bterm, h0.astype(jnp.float32))
    return h.astype(xa.dtype), new_buf, h[:, -1]


def layer_forward(x, conv_buf, h0, k_cache, v_cache, rel_bias, pre_g, post_g, w_in, conv_w, conv_b,
                  wr, br, wi, bi, rglru_lam, lq1, lk1, lq2, lk2, subln_g, w_out, lam_init, is_prompt):
    B, T = x.shape[0], x.shape[1]
    u = rms_norm(x, pre_g)
    proj = jnp.einsum('btd,dc->btc', u, w_in)
    splits = [D_A, 2 * D_A, 2 * D_A + QK_W, 2 * D_A + 2 * QK_W, 2 * D_A + 2 * QK_W + D_B]
    xa, ga, q, k, v, gb = jnp.split(proj, splits, axis=-1)
    ya, new_buf, h_last = rglru_branch(xa, conv_buf, h0, conv_w, conv_b, wr, br, wi, bi, rglru_lam)
    ya = ya * jax.nn.silu(ga)
    q = q.reshape(B, T, N_HEADS_B, 2, DK)
    k_rows = k.reshape(B, T, N_HEADS_B, 2 * DK)
    v_rows = v.reshape(B, T, N_HEADS_B, DV)
    lam = (jnp.exp(jnp.sum(lq1.astype(jnp.float32) * lk1.astype(jnp.float32)))
           - jnp.exp(jnp.sum(lq2.astype(jnp.float32) * lk2.astype(jnp.float32))) + lam_init)
    if is_prompt:
        o = diff_attn_prompt(q[..., 0, :], q[..., 1, :], k_rows[..., :DK], k_rows[..., DK:], v_rows,
                             rel_bias, lam)
    else:
        k_all = jnp.concatenate([k_cache.astype(k_rows.dtype), k_rows], axis=1)
        v_all = jnp.concatenate([v_cache.astype(v_rows.dtype), v_rows], axis=1)
        o = diff_attn_sample(q[..., 0, :], q[..., 1, :], k_all[..., :DK], k_all[..., DK:], v_all,
                             rel_bias, lam)
    o = rms_norm(o, subln_g) * (1.0 - lam_init)
    yb = o.reshape(B, T, D_B) * jax.nn.silu(gb)
    y = jnp.einsum('btc,cd->btd', jnp.concatenate([ya, yb], axis=-1), w_out)
    x = x + rms_norm(y, post_g)
    return x, k_rows, v_rows, new_buf, h_last


def setup_inputs(seed: int = 0) -> dict:
    key = jax.random.key(seed)
    ks = jax.random.split(key, 24)
    f32 = jnp.float32
    nrm = lambda k, shape, s: (jax.random.normal(k, shape, f32) * s)
    u = jax.random.uniform(ks[12], (DEPTH, D_A), f32, 0.9, 0.999)
    s = u ** (1.0 / RG_C)
    rglru_lam = jnp.log(s / (1.0 - s))
    return {
        "x_prompt": nrm(ks[0], (BATCH, SEQ, D_MODEL), 1.0),
        "x_sample": nrm(ks[1], (DEC_BATCH, DEC_SEQ, D_MODEL), 1.0),
        "cache_k": nrm(ks[2], (DEPTH, DEC_BATCH, PAST_LEN, N_HEADS_B, 2 * DK), 1.0),
        "cache_v": nrm(ks[3], (DEPTH, DEC_BATCH, PAST_LEN, N_HEADS_B, DV), 1.0),
        "state_conv": nrm(ks[4], (DEPTH, DEC_BATCH, CONV_W - 1, D_A), 1.0),
        "state_rglru": nrm(ks[5], (DEPTH, DEC_BATCH, D_A), 0.5),
        "meta": nrm(ks[6], (N_META, D_MODEL), 1.0),
        "rel_bias": nrm(ks[7], (NUM_BUCKETS, N_HEADS_B), 0.5),
        "pre_g": 1.0 + nrm(ks[8], (DEPTH, D_MODEL), 0.05),
        "post_g": 1.0 + nrm(ks[9], (DEPTH, D_MODEL), 0.05),
        "w_in": nrm(ks[10], (DEPTH, D_MODEL, IN_COLS), D_MODEL ** -0.5),
        "conv_w": nrm(ks[11], (DEPTH, CONV_W, D_A), CONV_W ** -0.5),
        "conv_b": nrm(ks[13], (DEPTH, D_A), 0.02),
        "gate_r_w": nrm(ks[14], (DEPTH, N_BLOCKS_A, BS_A, BS_A), BS_A ** -0.5),
        "gate_r_b": nrm(ks[15], (DEPTH, N_BLOCKS_A, BS_A), 0.02),
        "gate_i_w": nrm(ks[16], (DEPTH, N_BLOCKS_A, BS_A, BS_A), BS_A ** -0.5),
        "gate_i_b": nrm(ks[17], (DEPTH, N_BLOCKS_A, BS_A), 0.02),
        "rglru_lam": rglru_lam,
        "lam_q1": nrm(ks[18], (DEPTH, DK), 0.1),
        "lam_k1": nrm(ks[19], (DEPTH, DK), 0.1),
        "lam_q2": nrm(ks[20], (DEPTH, DK), 0.1),
        "lam_k2": nrm(ks[21], (DEPTH, DK), 0.1),
        "subln_g": 1.0 + nrm(ks[22], (DEPTH, DV), 0.05),
        "w_out": nrm(ks[23], (DEPTH, D_MIX, D_MODEL), D_MIX ** -0.5),
    }


def reference(x_prompt, x_sample, cache_k, cache_v, state_conv, state_rglru, meta, rel_bias,
              pre_g, post_g, w_in, conv_w, conv_b, gate_r_w, gate_r_b, gate_i_w, gate_i_b,
              rglru_lam, lam_q1, lam_k1, lam_q2, lam_k2, subln_g, w_out):
    B = x_prompt.shape[0]
    hp = jnp.concatenate([jnp.broadcast_to(meta.astype(x_prompt.dtype)[None], (B, N_META, D_MODEL)),
                          x_prompt], axis=1)
    hs = x_sample
    zero_buf = jnp.zeros((B, CONV_W - 1, D_A), x_prompt.dtype)
    zero_h = jnp.zeros((B, D_A), jnp.float32)
    kp_l, vp_l, cp_l, rp_l = [], [], [], []
    ks_l, vs_l, cs_l, rs_l = [], [], [], []
    for l in range(DEPTH):
        lam_init = 0.8 - 0.6 * math.exp(-0.3 * l)
        params = (rel_bias, pre_g[l], post_g[l], w_in[l], conv_w[l], conv_b[l], gate_r_w[l], gate_r_b[l],
                  gate_i_w[l], gate_i_b[l], rglru_lam[l], lam_q1[l], lam_k1[l], lam_q2[l], lam_k2[l],
                  subln_g[l], w_out[l], lam_init)
        hp, kp, vp, cp, rp = layer_forward(hp, zero_buf, zero_h, None, None, *params, True)
        hs, kk, vv, cs, rs = layer_forward(hs, state_conv[l], state_rglru[l], cache_k[l], cache_v[l],
                                           *params, False)
        kp_l.append(kp); vp_l.append(vp); cp_l.append(cp); rp_l.append(rp)
        ks_l.append(kk); vs_l.append(vv); cs_l.append(cs); rs_l.append(rs)
    y_prompt = hp[:, N_META:]
    y_sample = hs
    return (y_prompt, y_sample,
            jnp.stack(kp_l), jnp.stack(vp_l), jnp.stack(cp_l), jnp.stack(rp_l),
            jnp.stack(ks_l), jnp.stack(vs_l), jnp.stack(cs_l), jnp.stack(rs_l))
```

```python
import os
import math
import bisect
import contextlib
import numpy as np
import concourse.bass as bass
import concourse.mybir as mybir
from concourse.bass_utils import run_bass_kernel_spmd

F32 = mybir.dt.float32
BF16 = mybir.dt.bfloat16
AF = mybir.ActivationFunctionType
ALU = mybir.AluOpType
AX = mybir.AxisListType

D = 2048
DEPTH = 4
NT = 2096
SPANS = [(0, 48), (48, 560), (560, 1072), (1072, 1584), (1584, 2096)]
NTB = 17
EPS = 1e-6
OFF = 2080
NREL = 2080 + 2064
NCORES = 8


def tb_rows(tb):
    return (0, 48) if tb == 0 else (48 + 128 * (tb - 1), 128)


def bucket_table():
    rel = np.arange(-OFF, NREL - OFF).astype(np.int64)
    half, max_exact = 16, 8
    ret = np.where(rel > 0, half, 0).astype(np.int32)
    n = np.abs(rel).astype(np.int32)
    nf = np.maximum(n, 1).astype(np.float32)
    lg = (np.log(nf / np.float32(max_exact)) / np.float32(math.log(1024 / max_exact))
          * np.float32(half - max_exact)).astype(np.float32)
    large = max_exact + lg.astype(np.int32)
    large = np.minimum(large, half - 1)
    return ret + np.where(n < max_exact, n, large)


BUCKETS = bucket_table()
_nb = np.nonzero(BUCKETS != 15)[0]
R15 = int(_nb[0]) - OFF - 1
assert -641 <= R15 < -513, R15


class Buf:
    __slots__ = ("w", "r")

    def __init__(self):
        self.w = {}
        self.r = {}


class Eng:
    def __init__(self, nc, es, h, name, nd=0, skip_self=False):
        self.h = h
        self.name = name
        self.sem = es.enter_context(nc.semaphore("s_" + name))
        self.seq = 0
        self.incs = []
        self.last = None
        self.seen = {}
        self.skip_self = skip_self
        self.eager = not skip_self
        self.dsems = [es.enter_context(nc.semaphore("d_%s%d" % (name, i))) for i in range(nd)]
        self.dtot = [0] * nd
        self.rr = 0


def _resolve(tok):
    if tok[0] == 'd':
        return tok[1], tok[2]
    e, seq = tok[1], tok[2]
    i = bisect.bisect_left(e.incs, seq)
    if i == len(e.incs):
        e.last.then_inc(e.sem, 1)
        e.incs.append(e.seq)
    return e.sem, i + 1


def _tmax(d, key, tok):
    o = d.get(key)
    if o is None or o[2] < tok[2]:
        d[key] = tok


def _wait_for(eng, toks):
    waits = {}
    for t in toks:
        if t[0] == 'c' and t[1] is eng and eng.skip_self:
            continue
        sem, val = _resolve(t)
        k = id(sem)
        if k not in waits or waits[k][1] < val:
            waits[k] = (sem, val)
    for k, (sem, val) in waits.items():
        if eng.seen.get(k, 0) < val:
            eng.h.wait_ge(sem, val)
            eng.seen[k] = val


def _flat(bs):
    out = []
    for b in bs:
        if isinstance(b, (tuple, list)):
            out.extend(_flat(b))
        else:
            out.append(b)
    return out


def _deps(reads, writes):
    toks = []
    for b in reads:
        toks.extend(b.w.values())
    for b in writes:
        toks.extend(b.w.values())
        toks.extend(b.r.values())
    return toks


def emit(eng, fn, reads=(), writes=(), mark=False):
    reads, writes = _flat(reads), _flat(writes)
    _wait_for(eng, _deps(reads, writes))
    ins = fn()
    eng.seq += 1
    eng.last = ins
    if eng.eager or mark:
        ins.then_inc(eng.sem, 1)
        eng.incs.append(eng.seq)
    tok = ('c', eng, eng.seq)
    for b in reads:
        _tmax(b.r, id(eng), tok)
    for b in writes:
        _tmax(b.w, id(eng), tok)
    return ins


def emit_dma(eng, fn, reads=(), writes=()):
    reads, writes = _flat(reads), _flat(writes)
    i = eng.rr
    eng.rr = (i + 1) % len(eng.dsems)
    sem = eng.dsems[i]
    _wait_for(eng, _deps(reads, writes))
    if eng.seen.get(id(sem), 0) < eng.dtot[i]:
        eng.h.wait_ge(sem, eng.dtot[i])
        eng.seen[id(sem)] = eng.dtot[i]
    ins = fn()
    eng.dtot[i] += 16
    ins.then_inc(sem, 16)
    tok = ('d', sem, eng.dtot[i])
    for b in reads:
        _tmax(b.r, id(sem), tok)
    for b in writes:
        _tmax(b.w, id(sem), tok)
    return ins


def build(nl):
    nc = bass.Bass("TRN2", target_bir_lowering=False)
    es = contextlib.ExitStack()

    def din(name, shape):
        return nc.dram_tensor(name, list(shape), F32, kind="ExternalInput")

    def dout(name, shape):
        return nc.dram_tensor(name, list(shape), F32, kind="ExternalOutput")

    xp = din("xp", [2048, D]).ap()
    xs = din("xs", [32, D]).ap()
    ck = din("ck", [DEPTH, 2048, 1024]).ap()
    cv = din("cv", [DEPTH, 2048, 1024]).ap()
    scv = din("sc", [DEPTH, 3, 1024]).ap()
    srg = din("sr", [DEPTH, 1024]).ap()
    meta = din("meta", [16, D]).ap()
    relb = din("rel_bias", [32, 8]).ap()
    pre_g = din("pre_g", [DEPTH, D]).ap()
    post_g = din("post_g", [DEPTH, D]).ap()
    w_in = din("w_in", [DEPTH, D, 6144]).ap()
    conv_w = din("conv_w", [DEPTH, 4, 1024]).ap()
    conv_b = din("conv_b", [DEPTH, 1024]).ap()
    grw = din("gate_r_w", [DEPTH, 16, 64, 64]).ap()
    grb = din("gate_r_b", [DEPTH, 1024]).ap()
    giw = din("gate_i_w", [DEPTH, 16, 64, 64]).ap()
    gib = din("gate_i_b", [DEPTH, 1024]).ap()
    rlam = din("rglru_lam", [DEPTH, 1024]).ap()
    lq1 = din("lam_q1", [DEPTH, 64]).ap()
    lk1 = din("lam_k1", [DEPTH, 64]).ap()
    lq2 = din("lam_q2", [DEPTH, 64]).ap()
    lk2 = din("lam_k2", [DEPTH, 64]).ap()
    subg = din("subln_g", [DEPTH, 128]).ap()
    w_out = din("w_out", [DEPTH, D, D]).ap()
    sel = din("sel", [32, NREL]).ap()

    yp = dout("yp", [2048, D]).ap()
    ys = dout("ys", [32, D]).ap()
    kp = dout("kp", [DEPTH, 2064, 1024]).ap()
    vp = dout("vp", [DEPTH, 2064, 1024]).ap()
    cpo = dout("cp", [DEPTH, 3, 1024]).ap()
    rpo = dout("rp", [DEPTH, 1024]).ap()
    kso = dout("ks", [DEPTH, 32, 1024]).ap()
    vso = dout("vs", [DEPTH, 32, 1024]).ap()
    cso = dout("cs", [DEPTH, 3, 1024]).ap()
    rso = dout("rs", [DEPTH, 1024]).ap()

    xscr = [nc.dram_tensor("xscr%d" % i, [NT, D], F32, kind="Internal").ap() for i in range(2)]
    ymscr = nc.dram_tensor("ymscr", [16, 128, NT], BF16, kind="Internal").ap()
    evscr_t = nc.dram_tensor("evscr", [8, NREL], BF16, kind="Internal")
    evscr = evscr_t.ap()

    with es:
        def sb(name, shape, dt):
            return es.enter_context(nc.sbuf_tensor(name, list(shape), dt))

        PE = Eng(nc, es, nc.tensor, "pe", skip_self=True)
        ACT = Eng(nc, es, nc.scalar, "act", nd=6)
        DVE = Eng(nc, es, nc.vector, "dve")
        POOL = Eng(nc, es, nc.gpsimd, "pool", nd=8)
        SP = Eng(nc, es, nc.sync, "sp", nd=12)

        R1 = sb("R1", [128, 16 * NT], BF16)
        uT = R1[:].rearrange("p (c n) -> p c n", c=16)
        WO = R1[:, 0:16 * 2048].rearrange("p (c n) -> p c n", c=16)
        uT_b = [Buf() for _ in SPANS]
        NSLOT = 6
        WS = sb("WS", [128, NSLOT, 16, 128], BF16)
        WS_b = [Buf() for _ in range(NSLOT)]
        XT = [sb("XT%d" % i, [128, D], F32) for i in range(2)]
        XT_b = [(Buf(), Buf()), Buf()]
        YM = [sb("YM%d" % i, [128, NT], BF16) for i in range(2)]
        YM_b = [Buf(), Buf()]
        TK = sb("TK", [128, 1024], F32)
        TMPC = [TK[:, 0:512], TK[:, 512:1024]]
        TMPC_b = [Buf(), Buf()]
        QK = sb("QK", [128, 2 * NT], BF16)
        QT = QK[:, 0:NT]
        QT_b = Buf()
        KT = QK[:, NT:2 * NT]
        KT_b = Buf()
        YMB = [QK[:, i * 2048:(i + 1) * 2048].rearrange("p (c n) -> p c n", c=16) for i in range(2)]
        YMB_b = [(Buf(),), (Buf(),)]
        VH = sb("VH", [128, NTB, 256], BF16)
        VH_b = Buf()
        VM = sb("VM", [16, 128], BF16)
        VM_b = Buf()
        SGB = sb("SGB", [128, NT], F32)
        SGB_b = Buf()
        GB = SGB[:, 0:D]
        GB_b = SGB_b
        KCT = TK[:].bitcast(BF16).rearrange("p (b d) -> p b d", b=16)
        KCT_b = TMPC_b
        KTC = sb("KTC", [128, 2048], BF16)
        KTC_b = Buf()
        VC = sb("VC", [128, 16, 128], BF16)
        VC_b = Buf()
        UB = [VC[:].rearrange("p b d -> p (b d)"), KTC[:, :]]
        UB_b = [VC_b, KTC_b]
        KVS = [sb("KVS%d" % i, [128, 256], F32) for i in range(2)]
        KVS_b = [Buf(), Buf()]
        NE = 7
        ET2 = [sb("ET%d" % i, [128, NE, 256], BF16) for i in range(2)]
        ET_b2 = [[Buf() for _ in range(NE)] for i in range(2)]
        EM2 = [sb("EM%d" % i, [16, 3, 256], BF16) for i in range(2)]
        EM_b2 = [Buf(), Buf()]
        EMM2 = [sb("EMM%d" % i, [16, 16], BF16) for i in range(2)]
        ES2 = [sb("ES%d" % i, [128, 6, 32], BF16) for i in range(2)]
        ES_b2 = [Buf(), Buf()]
        NPT = 4
        PT = [sb("PT%d" % i, [128, 2, 256], BF16) for i in range(NPT)]
        PT_b = [Buf() for _ in range(NPT)]
        NTMP = 10
        TMall = sb("TMall", [128, NTMP, 512], F32)
        TM = [TMall[:, i, :] for i in range(NTMP)]
        TM_b = [Buf() for _ in range(NTMP)]
        XA = sb("XA", [128, NT + 6], F32)
        XA_b = Buf()
        HH = [sb("HH%d" % i, [128, 512], F32) for i in range(2)]
        HH_b = [Buf(), Buf()]
        GW = sb("GW", [128, 16, 128], BF16)
        GW_b = Buf()
        ident = sb("ident", [128, 128], BF16)
        identf = sb("identf", [128, 128], F32)
        ones_b = sb("ones_b", [128, 128], BF16)
        ones_f = sb("ones_f", [128, 128], F32)
        CONST_b = Buf()
        EPSC = sb("EPSC", [128, 1], F32)
        ONEC = sb("ONEC", [128, 1], F32)
        CW = sb("CW", [128, DEPTH * 8, 4], F32)
        CB = sb("CB", [128, DEPTH * 8], F32)
        BR = sb("BR", [128, DEPTH * 8], F32)
        BI = sb("BI", [128, DEPTH * 8], F32)
        CA = sb("CA", [128, DEPTH * 8], F32)
        CA2 = sb("CA2", [128, DEPTH * 8], F32)
        SR0 = sb("SR0", [128, DEPTH * 8], F32)
        SG = sb("SG", [128, DEPTH], F32)
        NLAM = sb("NLAM", [128, DEPTH], F32)
        LT = XT[0][:, 0:4 * DEPTH * 64].rearrange("p (i k) -> p i k", i=4)
        LT2 = sb("LT2", [128, 2, DEPTH], F32)
        F15 = sb("F15", [128, 8], F32)
        RB = sb("RB", [32, 8], F32)
        SS = sb("SS", [128, 16], F32)
        SS_bb = [Buf(), Buf()]
        RSTD = sb("RSTD", [128, 4], F32)
        RSTD_bb = [Buf(), Buf()]
        VEC_b = Buf()

        PS = es.enter_context(nc.psum_tensor("PS", [128, 8, 512], F32))
        PB = [PS[:, i, :] for i in range(8)]
        PB_b = [Buf() for _ in range(8)]
        mmrr = [0]

        def mmbank():
            i = mmrr[0]
            mmrr[0] = (i + 1) % 4
            return i

        PAIRS = [(0, 1), (2, 3)]
        live = [None]
        fprr = [0]

        def free_pair():
            return (6, 7)

        slotc = [0]

        def next_slot():
            v = slotc[0]
            slotc[0] += 1
            return (v // 2) % 2, v % 2

        def bc2(a2):
            return bass.AP(a2.tensor, a2.offset, [list(a2.ap[0]), [0, 2], list(a2.ap[1])])

        tmrr = [0]

        def tmp():
            i = tmrr[0]
            tmrr[0] = (i + 1) % NTMP
            return i

        xscr_b = [[Buf() for _ in range(NTB)] for _ in range(2)]
        ymscr_b = [Buf() for _ in range(16)]
        ev_b = Buf()

        evac_rr = [0]

        def evac_eng():
            evac_rr[0] ^= 1
            return ACT if evac_rr[0] else DVE

        def copy(eng, out, in_, reads, writes):
            if eng is ACT:
                return emit(ACT, lambda: nc.scalar.copy(out, in_), reads, writes)
            return emit(eng, lambda: eng.h.tensor_copy(out, in_), reads, writes)

        emit(POOL, lambda: nc.gpsimd.memset(identf[:], 1.0), (), (CONST_b,))
        emit(POOL, lambda: nc.gpsimd.affine_select(out=identf[:], in_=identf[:], pattern=[[-1, 128]],
                                                   compare_op=ALU.is_equal, fill=0.0, base=0,
                                                   channel_multiplier=1), (), (CONST_b,))
        emit(POOL, lambda: nc.gpsimd.memset(ones_f[:], 1.0), (), (CONST_b,))
        emit(POOL, lambda: nc.gpsimd.memset(EPSC[:], EPS), (), (CONST_b,))
        emit(POOL, lambda: nc.gpsimd.memset(ONEC[:], 1.0), (), (CONST_b,))
        emit(DVE, lambda: nc.vector.tensor_copy(ident[:], identf[:]), (CONST_b,), (CONST_b,))
        emit(DVE, lambda: nc.vector.tensor_copy(ones_b[:], ones_f[:]), (CONST_b,), (CONST_b,))
        emit(DVE, lambda: nc.vector.memset(XA[:], 0.0), (), (XA_b,))

        def small(dst, src):
            emit_dma(SP, lambda: nc.sync.dma_start(out=dst, in_=src, allow_slow_non_contiguous=True),
                     (), (VEC_b,))

        for l in range(nl):
            for k in range(4):
                small(CW[:, l * 8:(l + 1) * 8, k], conv_w[l][k].rearrange("(c p) -> p c", p=128))
        small(CB[:, 0:nl * 8], conv_b[0:nl].rearrange("l (c p) -> p (l c)", p=128))
        small(BI[:, 0:nl * 8], gib[0:nl].rearrange("l (c p) -> p (l c)", p=128))
        small(CA[:, 0:nl * 8], rlam[0:nl].rearrange("l (c p) -> p (l c)", p=128))
        small(SR0[:, 0:nl * 8], srg[0:nl].rearrange("l (c p) -> p (l c)", p=128))
        small(SG[:, :], subg.rearrange("l p -> p l"))
        for i, t in enumerate((lq1, lk1, lq2, lk2)):
            emit_dma(SP, lambda i=i, t=t: nc.sync.dma_start(out=LT[:, i, :], in_=t.rearrange("l k -> (l k)").partition_broadcast(128)), (), (XT_b[0], VEC_b))
        small(F15[:, :], relb[15:16, :].rearrange("o h -> (o h)").partition_broadcast(128))
        small(RB[:, :], relb)

        emit(ACT, lambda: nc.scalar.activation(out=CA[:], in_=CA[:], func=AF.Exp, scale=-1.0), (VEC_b,), (VEC_b,))
        V = (VEC_b,)
        emit(DVE, lambda: nc.vector.tensor_scalar(CA2[:], CA[:], 2.0, None, op0=ALU.add), V, V)
        emit(DVE, lambda: nc.vector.reciprocal(CA2[:], CA2[:]), V, V)
        emit(DVE, lambda: nc.vector.tensor_tensor(CA[:], CA[:], CA2[:], op=ALU.mult), V, V)
        emit(DVE, lambda: nc.vector.tensor_tensor(CA2[:], CA[:], CA[:], op=ALU.mult), V, V)
        emit(DVE, lambda: nc.vector.memset(BR[:], 1.0 / 15), V, V)
        for cf in (1.0 / 13, 1.0 / 11, 1.0 / 9, 1.0 / 7, 1.0 / 5, 1.0 / 3, 1.0):
            emit(DVE, lambda: nc.vector.tensor_tensor(BR[:], BR[:], CA2[:], op=ALU.mult), V, V)
            emit(DVE, lambda cf=cf: nc.vector.tensor_scalar(BR[:], BR[:], cf, None, op0=ALU.add), V, V)
        emit(DVE, lambda: nc.vector.tensor_tensor(CA[:], CA[:], BR[:], op=ALU.mult), V, V)
        emit(DVE, lambda: nc.vector.tensor_scalar(CA[:], CA[:], 2.0, None, op0=ALU.mult), V, V)
        emit(DVE, lambda: nc.vector.tensor_scalar(CA2[:], CA[:], -16.0, None, op0=ALU.mult), (VEC_b,), (VEC_b,))
        emit(DVE, lambda: nc.vector.tensor_scalar(CA[:], CA[:], -8.0, None, op0=ALU.mult), (VEC_b,), (VEC_b,))
        small(BR[:, 0:nl * 8], grb[0:nl].rearrange("l (c p) -> p (l c)", p=128))
        emit(DVE, lambda: nc.vector.tensor_scalar(BR[:], BR[:], -1.0, None, op0=ALU.mult), (VEC_b,), (VEC_b,))
        emit(DVE, lambda: nc.vector.tensor_scalar(BI[:], BI[:], -1.0, None, op0=ALU.mult), (VEC_b,), (VEC_b,))
        emit(DVE, lambda: nc.vector.tensor_tensor(LT[:, 0, :], LT[:, 0, :], LT[:, 1, :], op=ALU.mult), (VEC_b, XT_b[0]), (VEC_b, XT_b[0]))
        emit(DVE, lambda: nc.vector.tensor_tensor(LT[:, 2, :], LT[:, 2, :], LT[:, 3, :], op=ALU.mult), (VEC_b, XT_b[0]), (VEC_b, XT_b[0]))
        emit(DVE, lambda: nc.vector.tensor_reduce(out=LT2[:, 0, :], in_=LT[:, 0, :].rearrange("p (l k) -> p l k", k=64),
                                                  axis=AX.X, op=ALU.add), (VEC_b, XT_b[0]), (VEC_b,))
        emit(DVE, lambda: nc.vector.tensor_reduce(out=LT2[:, 1, :], in_=LT[:, 2, :].rearrange("p (l k) -> p l k", k=64),
                                                  axis=AX.X, op=ALU.add), (VEC_b, XT_b[0]), (VEC_b,))
        emit(ACT, lambda: nc.scalar.activation(out=LT2[:], in_=LT2[:], func=AF.Exp), (VEC_b,), (VEC_b,))
        emit(DVE, lambda: nc.vector.tensor_tensor(NLAM[:], LT2[:, 1, :], LT2[:, 0, :], op=ALU.subtract), (VEC_b,), (VEC_b,))
        for l in range(DEPTH):
            li = 0.8 - 0.6 * math.exp(-0.3 * l)
            emit(DVE, lambda l=l, li=li: nc.vector.tensor_scalar(NLAM[:, l:l + 1], NLAM[:, l:l + 1], -li, None, op0=ALU.add),
                 (VEC_b,), (VEC_b,))
            emit(DVE, lambda l=l, li=li: nc.vector.tensor_scalar(SG[:, l:l + 1], SG[:, l:l + 1], 1.0 - li, None, op0=ALU.mult),
                 (VEC_b,), (VEC_b,))
        for ci, c0 in enumerate(range(0, NREL, 512)):
            c1 = min(NREL, c0 + 512)
            b = mmbank()
            ts_ = tmp()
            pp = ci % 2
            emit_dma(SP, lambda: nc.sync.dma_start(out=TM[ts_][0:32, 0:c1 - c0], in_=sel[:, c0:c1]), (), (TM_b[ts_],))
            emit(PE, lambda: nc.tensor.matmul(PB[b][0:8, 0:c1 - c0], lhsT=RB[:, :], rhs=TM[ts_][0:32, 0:c1 - c0],
                                              start=True, stop=True), (VEC_b, TM_b[ts_]), (PB_b[b],), mark=True)
            stg = PT[pp][:].rearrange("p m c -> p (m c)")
            emit(ACT, lambda: nc.scalar.activation(out=stg[0:8, 0:c1 - c0], in_=PB[b][0:8, 0:c1 - c0], func=AF.Exp),
                 (PB_b[b],), (PT_b[pp],))
            emit_dma(SP, lambda: nc.sync.dma_start(out=evscr[:, c0:c1], in_=stg[0:8, 0:c1 - c0]), (PT_b[pp],), (ev_b,))

        def hankel(h, base, npart, ncol):
            return bass.AP(evscr_t, h * NREL + base, [[1, npart], [1, ncol]])

        wq = []
        nun = int(os.environ.get("MK_UNITS", 8))
        WPOS = [{"k": 0, "v": 1, "x": 2, "g": 3, "q": 4, "gb": 5}, {"k": 0, "v": 1, "q": 2, "gb": 3, "x": 4, "g": 5}]
        for l in range(nl):
            for u in range(nun):
                pos = WPOS[(l * nun + u) % 2]
                blk = {"k": 24 + u, "v": 32 + u, "q": 16 + u, "gb": 40 + u, "x": u, "g": 8 + u}
                for role in sorted(pos, key=lambda r: pos[r]):
                    wq.append((l, blk[role]))
        wissued = [False] * len(wq)
        wdone = [False] * len(wq)

        def w_issue(i):
            l, c = wq[i]
            s = i % NSLOT
            emit_dma(POOL, lambda: nc.gpsimd.dma_start(
                out=WS[:, s, :, :], in_=w_in[l][:, c * 128:(c + 1) * 128].rearrange("(c p) n -> p c n", p=128)),
                (), (WS_b[s],))
            wissued[i] = True

        def w_get(l, u, roles):
            pair = l * nun + u
            idx = [NSLOT * pair + WPOS[pair % 2][r] for r in roles]
            assert all(wissued[i] for i in idx), (l, u, roles)
            return [i % NSLOT for i in idx], idx

        def w_release(idxs):
            for i in idxs:
                wdone[i] = True
            lo = 0
            while lo < len(wq) and wissued[lo]:
                lo += 1
            for i in range(lo, min(len(wq), lo + 2 * NSLOT)):
                if not wissued[i] and (i < NSLOT or wdone[i - NSLOT]):
                    w_issue(i)

        def spans_of(c0, c1):
            return [i for i, (a, b) in enumerate(SPANS) if a < c1 and b > c0]

        def proj_fm(slot, c0, c1, b=None):
            if b is None:
                b = mmbank()
            rb = [WS_b[slot]] + [uT_b[i] for i in spans_of(c0, c1)]
            for dc in range(16):
                emit(PE, lambda dc=dc: nc.tensor.matmul(PB[b][:, 0:c1 - c0], lhsT=WS[:, slot, dc, :], rhs=uT[:, dc, c0:c1],
                                                        start=(dc == 0), stop=(dc == 15)), rb, (PB_b[b],), mark=(dc == 15))
            return b

        def xa_idx(c):
            return c + 3 if c < 32 else c + 6

        def load_x(l, tb, xt):
            r0, R = tb_rows(tb)
            if l == 0:
                if tb == 0:
                    emit_dma(SP, lambda: nc.sync.dma_start(out=XT[xt][0:32, :], in_=xs), (), (XT_b[xt],))
                    emit_dma(SP, lambda: nc.sync.dma_start(out=XT[xt][32:48, :], in_=meta), (), (XT_b[xt],))
                else:
                    emit_dma(SP, lambda: nc.sync.dma_start(out=XT[xt][:, :], in_=xp[(tb - 1) * 128:tb * 128, :]),
                             (), (XT_b[xt],))
            else:
                src = xscr[(l - 1) % 2]
                emit_dma(SP, lambda: nc.sync.dma_start(out=XT[xt][0:R, :], in_=src[r0:r0 + R, :]),
                         (xscr_b[(l - 1) % 2][tb],), (XT_b[xt],))

        def phaseA(l):
            emit_dma(SP, lambda: nc.sync.dma_start(out=GB[:], in_=pre_g[l].partition_broadcast(128)), (), (GB_b,))

            def s1(tb):
                r0, R = tb_rows(tb)
                xt = tb % 2
                SS_b, RSTD_b = SS_bb[xt], RSTD_bb[xt]
                ssc = SS[0:R, 8 * xt:8 * xt + 1]
                rsc = RSTD[0:R, 2 * xt:2 * xt + 1]
                load_x(l, tb, xt)
                emit(ACT, lambda: nc.scalar.activation(out=UB[xt][0:R, :], in_=XT[xt][0:R, :], func=AF.Square,
                                                       accum_out=ssc), (XT_b[xt],), (UB_b[xt], SS_b))
                emit(ACT, lambda: nc.scalar.activation(out=rsc, in_=ssc, func=AF.Ln, scale=1.0 / D, bias=EPSC[0:R, 0:1]),
                     (SS_b, CONST_b), (RSTD_b,))
                emit(ACT, lambda: nc.scalar.activation(out=rsc, in_=rsc, func=AF.Exp, scale=-0.5),
                     (RSTD_b,), (RSTD_b,))
                emit(DVE, lambda: nc.vector.scalar_tensor_tensor(out=UB[xt][0:R, :], in0=XT[xt][0:R, :], scalar=rsc,
                                                                 in1=GB[0:R, :], op0=ALU.mult, op1=ALU.mult),
                     (XT_b[xt], RSTD_b, GB_b), (UB_b[xt],))

            def s2(tb):
                r0, R = tb_rows(tb)
                xt = tb % 2
                for g in range(4):
                    slot = g % 2
                    pv = PB[4 + slot][:].bitcast(BF16)
                    for i in range(4):
                        dc = 4 * g + i
                        emit(PE, lambda i=i, dc=dc: nc.tensor.transpose(pv[:, i * 128:i * 128 + R], UB[xt][0:R, dc * 128:(dc + 1) * 128],
                                                                        ident[0:R, 0:R]),
                             (UB_b[xt], CONST_b), (PB_b[4 + slot],), mark=(i == 3))
                    src = pv[:, 0:512].rearrange("p (i n) -> p i n", i=4)[:, :, 0:R]
                    dst = uT[:, 4 * g:4 * g + 4, r0:r0 + R]
                    copy(evac_eng(), dst, src, (PB_b[4 + slot],), [uT_b[i] for i in spans_of(r0, r0 + R)])

            s1(0)
            for tb in range(NTB):
                if tb + 1 < NTB:
                    s1(tb + 1)
                s2(tb)

        def halo(l, j):
            emit_dma(SP, lambda: nc.sync.dma_start(out=XA[:, 0:3], in_=scv[l][:, j * 128:(j + 1) * 128].rearrange("k p -> p k"),
                                                   allow_slow_non_contiguous=True), (), (XA_b,))

        def mixerA(l, j):
            (sx, sg), widx = w_get(l, j, ("x", "g"))
            lj = l * 8 + j
            ym = 0
            stage = {}

            def front(si):
                c0, c1 = SPANS[si]
                W = c1 - c0
                bx, bg = free_pair()
                proj_fm(sx, c0, c1, bx)
                proj_fm(sg, c0, c1, bg)
                segs = [(0, 32), (32, 48)] if si == 0 else [(c0, c1)]
                for (a, b) in segs:
                    copy(DVE, XA[:, xa_idx(a):xa_idx(a) + (b - a)], PB[bx][:, a - c0:b - c0], (PB_b[bx],), (XA_b,))
                tg, txc, txb = [3 * (si % 2) + k for k in range(3)]
                emit(ACT, lambda: nc.scalar.activation(out=TM[tg][:, 0:W], in_=PB[bg][:, 0:W], func=AF.Exp, scale=-1.0),
                     (PB_b[bg],), (TM_b[tg],))
                yield
                emit(ACT, lambda: nc.scalar.activation(out=TM[tg][:, 0:W], in_=TM[tg][:, 0:W], func=AF.Ln, bias=ONEC[:, 0:1]),
                     (TM_b[tg], CONST_b), (TM_b[tg],))
                yield
                emit(ACT, lambda: nc.scalar.activation(out=TM[tg][:, 0:W], in_=TM[tg][:, 0:W], func=AF.Exp, scale=-1.0),
                     (TM_b[tg],), (TM_b[tg],))
                yield
                emit(DVE, lambda: nc.vector.tensor_tensor(TM[tg][:, 0:W], PB[bg][:, 0:W], TM[tg][:, 0:W], op=ALU.mult),
                     (PB_b[bg], TM_b[tg]), (TM_b[tg],))
                for (a, b) in segs:
                    i0 = xa_idx(a)
                    n = b - a
                    o = TM[txc][:, a - c0:b - c0]
                    emit(DVE, lambda: nc.vector.tensor_scalar(o, XA[:, i0 - 3:i0 - 3 + n], CW[:, lj, 0:1], CB[:, lj:lj + 1],
                                                              op0=ALU.mult, op1=ALU.add), (XA_b, VEC_b), (TM_b[txc],))
                    for k in (1, 2, 3):
                        emit(DVE, lambda k=k: nc.vector.scalar_tensor_tensor(out=o, in0=XA[:, i0 - 3 + k:i0 - 3 + k + n],
                                                                             scalar=CW[:, lj, k:k + 1], in1=o,
                                                                             op0=ALU.mult, op1=ALU.add),
                             (XA_b, VEC_b, TM_b[txc]), (TM_b[txc],))
                xcb = TM[txb][:].bitcast(BF16)
                yield
                emit(ACT, lambda: nc.scalar.copy(xcb[:, 0:W], TM[txc][:, 0:W]), (TM_b[txc],), (TM_b[txb],))
                stage[si] = (tg, txc, txb, xcb, segs, W, c0)

            def back(si):
                tg, txc, txb, xcb, segs, W, c0 = stage.pop(si)
                br_, bi_ = free_pair()
                emit(PE, lambda: nc.tensor.matmul(PB[br_][:, 0:W], lhsT=GW[:, j, :], rhs=xcb[:, 0:W], start=True, stop=True),
                     (GW_b, TM_b[txb]), (PB_b[br_],), mark=True)
                emit(PE, lambda: nc.tensor.matmul(PB[bi_][:, 0:W], lhsT=GW[:, 8 + j, :], rhs=xcb[:, 0:W], start=True, stop=True),
                     (GW_b, TM_b[txb]), (PB_b[bi_],), mark=True)
                tr_, ti_, ta_, ts_ = 6, 7, 8, 9
                yield
                emit(ACT, lambda: nc.scalar.activation(out=TM[tr_][:, 0:W], in_=PB[br_][:, 0:W], func=AF.Exp, scale=-1.0,
                                                       bias=BR[:, lj:lj + 1]), (PB_b[br_], VEC_b), (TM_b[tr_],))
                yield
                emit(ACT, lambda: nc.scalar.activation(out=TM[ti_][:, 0:W], in_=PB[bi_][:, 0:W], func=AF.Exp, scale=-1.0,
                                                       bias=BI[:, lj:lj + 1]), (PB_b[bi_], VEC_b), (TM_b[ti_],))
                yield
                ri = TMall[:, tr_:tr_ + 2, 0:W]
                emit(ACT, lambda: nc.scalar.activation(out=ri, in_=ri, func=AF.Ln, bias=ONEC[:, 0:1]),
                     (TM_b[tr_], TM_b[ti_], CONST_b), (TM_b[tr_], TM_b[ti_]))
                yield
                emit(ACT, lambda: nc.scalar.activation(out=ri, in_=ri, func=AF.Exp, scale=-1.0),
                     (TM_b[tr_], TM_b[ti_]), (TM_b[tr_], TM_b[ti_]))
                yield
                emit(ACT, lambda: nc.scalar.activation(out=TM[ta_][:, 0:W], in_=TM[tr_][:, 0:W], func=AF.Exp,
                                                       scale=CA[:, lj:lj + 1]), (TM_b[tr_], VEC_b), (TM_b[ta_],))
                yield
                emit(ACT, lambda: nc.scalar.activation(out=TM[ts_][:, 0:W], in_=TM[tr_][:, 0:W], func=AF.Exp,
                                                       scale=CA2[:, lj:lj + 1]), (TM_b[tr_], VEC_b), (TM_b[ts_],))
                emit(DVE, lambda: nc.vector.tensor_scalar(TM[ts_][:, 0:W], TM[ts_][:, 0:W], 0.99999994, None, op0=ALU.min),
                     (TM_b[ts_],), (TM_b[ts_],))
                yield
                emit(ACT, lambda: nc.scalar.activation(out=TM[ts_][:, 0:W], in_=TM[ts_][:, 0:W], func=AF.Ln, scale=-1.0, bias=ONEC[:, 0:1]),
                     (TM_b[ts_], CONST_b), (TM_b[ts_],))
                yield
                emit(ACT, lambda: nc.scalar.activation(out=TM[ts_][:, 0:W], in_=TM[ts_][:, 0:W], func=AF.Exp, scale=0.5),
                     (TM_b[ts_],), (TM_b[ts_],))
                yield
                emit(DVE, lambda: nc.vector.tensor_tensor(TM[ti_][:, 0:W], TM[ti_][:, 0:W], TM[txc][:, 0:W], op=ALU.mult),
                     (TM_b[ti_], TM_b[txc]), (TM_b[ti_],))
                emit(DVE, lambda: nc.vector.tensor_tensor(TM[ti_][:, 0:W], TM[ti_][:, 0:W], TM[ts_][:, 0:W], op=ALU.mult),
                     (TM_b[ti_], TM_b[ts_]), (TM_b[ti_],))
                hb = si % 2
                for (a, b) in segs:
                    n = b - a
                    lo = a - c0
                    if si == 0 and a == 0:
                        init = SR0[:, lj:lj + 1]
                        ir = (VEC_b,)
                    elif si == 0:
                        init = 0.0
                        ir = ()
                    else:
                        pw = SPANS[si - 1][1] - SPANS[si - 1][0]
                        init = HH[1 - hb][:, pw - 1:pw]
                        ir = (HH_b[1 - hb],)
                    emit(DVE, lambda: nc.vector.tensor_tensor_scan(HH[hb][:, lo:lo + n], TM[ta_][:, lo:lo + n], TM[ti_][:, lo:lo + n],
                                                                   init, op0=ALU.mult, op1=ALU.add),
                         (TM_b[ta_], TM_b[ti_]) + tuple(ir), (HH_b[hb],))
                emit(DVE, lambda: nc.vector.tensor_tensor(YM[ym][:, c0:c0 + W], HH[hb][:, 0:W], TM[tg][:, 0:W], op=ALU.mult),
                     (HH_b[hb], TM_b[tg]), (YM_b[ym],))
                if si == 0:
                    emit_dma(SP, lambda: nc.sync.dma_start(out=rso[l:l + 1, j * 128:(j + 1) * 128].rearrange("o p -> p o"),
                                                           in_=HH[hb][:, 31:32], allow_slow_non_contiguous=True),
                             (HH_b[hb],), ())
                if si == 4:
                    emit_dma(SP, lambda: nc.sync.dma_start(out=rpo[l:l + 1, j * 128:(j + 1) * 128].rearrange("o p -> p o"),
                                                           in_=HH[hb][:, W - 1:W], allow_slow_non_contiguous=True),
                             (HH_b[hb],), ())

            for si in range(6):
                if si < 5:
                    yield from front(si)
                    yield
                if si >= 1:
                    yield from back(si - 1)
                    yield
            w_release(widx)
            emit_dma(SP, lambda: nc.sync.dma_start(out=cso[l][:, j * 128:(j + 1) * 128].rearrange("k p -> p k"),
                                                   in_=XA[:, 32:35], allow_slow_non_contiguous=True), (XA_b,), ())
            emit_dma(SP, lambda: nc.sync.dma_start(out=cpo[l][:, j * 128:(j + 1) * 128].rearrange("k p -> p k"),
                                                   in_=XA[:, NT + 3:NT + 6], allow_slow_non_contiguous=True), (XA_b,), ())
            emit_dma(SP, lambda: nc.sync.dma_start(out=ymscr[j], in_=YM[ym][:, :]), (YM_b[ym],), (ymscr_b[j],))
            if j + 1 < nun:
                halo(l, j + 1)

        def attend(l, h, q0, q1, blocks, ym, pend):
            W = q1 - q0
            nb = len(blocks)
            sl = {}
            LA = 3

            def scores(bi):
                bk = blocks[bi]
                nk, cs = bk["nk"], bk["cs"]
                p, hf = (bi // 2) % 2, bi % 2
                c0 = hf * 256
                for m in (0, 1):
                    emit(PE, lambda m=m: nc.tensor.matmul(PB[2 * p + m][0:nk, c0 + cs:c0 + W], lhsT=bk["kt"][64 * m:64 * m + 64, :],
                                                          rhs=QT[64 * m:64 * m + 64, q0 + cs:q1], start=True, stop=True),
                         tuple(bk["rd"]) + (QT_b,), (PB_b[2 * p + m],), mark=(m == 1))
                sl[bi] = (p, c0)

            def probs(bi):
                bk = blocks[bi]
                nk, cs = bk["nk"], bk["cs"]
                p, c0 = sl.pop(bi)
                pi = bi % NPT
                src = PS[0:nk, 2 * p:2 * p + 2, c0 + cs:c0 + W]
                dst = PT[pi][0:nk, :, cs:W]
                rb = (PB_b[2 * p], PB_b[2 * p + 1])
                if bk["e"][0] == 'c':
                    emit(ACT, lambda: nc.scalar.activation(out=dst, in_=src, func=AF.Exp, scale=0.125, bias=F15[0:nk, h:h + 1]),
                         rb + (VEC_b,), (PT_b[pi],))
                else:
                    emit(ACT, lambda: nc.scalar.activation(out=dst, in_=src, func=AF.Exp, scale=0.125), rb, (PT_b[pi],))
                    emit(DVE, lambda: nc.vector.tensor_tensor(dst, dst, bc2(bk["e"][1]), op=ALU.mult),
                         (PT_b[pi],) + tuple(bk["erd"]), (PT_b[pi],))

            def pv(bi):
                bk = blocks[bi]
                nk, cs = bk["nk"], bk["cs"]
                pi = bi % NPT
                st, sp = (bi == 0), (bi == nb - 1)
                rhs = PT[pi][0:nk, :, cs:W]
                o4 = PB[4].rearrange("p (m c) -> p m c", m=2)[:, :, cs:W]
                o5 = PB[5].rearrange("p (m c) -> p m c", m=2)[:, :, cs:W]
                emit(PE, lambda: nc.tensor.matmul(o4, lhsT=bk["v"], rhs=rhs, start=st, stop=sp),
                     tuple(bk["rd"]) + (PT_b[pi],), (PB_b[4],))
                emit(PE, lambda: nc.tensor.matmul(o5, lhsT=ones_b[0:nk, :], rhs=rhs, start=st, stop=sp),
                     (CONST_b, PT_b[pi]), (PB_b[5],), mark=True)

            def v2(ap512):
                return ap512.rearrange("p (m c) -> p m c", m=2)[:, :, 0:W]

            T1f, T3f, T2 = XT[0][:, 0:512], XT[0][:, 512:1024], XT[1][:, 0:W]
            B1a, B1, B2 = XT_b[0][0], XT_b[0][1], XT_b[1]

            def part1a():
                emit(DVE, lambda: nc.vector.tensor_copy(v2(T1f), v2(PB[4])), (PB_b[4],), (B1a,))
                emit(ACT, lambda: nc.scalar.activation(out=v2(T3f), in_=v2(PB[5]), func=AF.Ln), (PB_b[5],), (B1,))

            def part1():
                emit(ACT, lambda: nc.scalar.activation(out=v2(T3f), in_=v2(T3f), func=AF.Exp, scale=-1.0), (B1,), (B1,))
                emit(DVE, lambda: nc.vector.tensor_tensor(v2(T1f), v2(T1f), v2(T3f), op=ALU.mult), (B1a, B1), (B1a,))
                emit(DVE, lambda: nc.vector.scalar_tensor_tensor(out=T2, in0=T1f[:, 256:256 + W], scalar=NLAM[:, l:l + 1], in1=T1f[:, 0:W],
                                                                 op0=ALU.mult, op1=ALU.add), (B1a, VEC_b), (B2,))

            def part2():
                emit(ACT, lambda: nc.scalar.activation(out=T1f[:, 0:W], in_=T2, func=AF.Square), (B2,), (B1a,))
                p, c0 = 0, 0
                emit(PE, lambda: nc.tensor.matmul(PB[2 * p][:, c0:c0 + W], lhsT=ones_f[:, :], rhs=T1f[:, 0:W], start=True, stop=True),
                     (CONST_b, B1a), (PB_b[2 * p],), mark=True)
                emit(ACT, lambda: nc.scalar.activation(out=T3f[:, 0:W], in_=PB[2 * p][:, c0:c0 + W], func=AF.Ln, scale=1.0 / 128,
                                                       bias=EPSC[:, 0:1]), (PB_b[2 * p], CONST_b), (B1,))
                emit(ACT, lambda: nc.scalar.activation(out=T3f[:, 0:W], in_=T3f[:, 0:W], func=AF.Exp, scale=-0.5), (B1,), (B1,))
                emit(DVE, lambda: nc.vector.scalar_tensor_tensor(out=T2, in0=T2, scalar=SG[:, l:l + 1], in1=T3f[:, 0:W],
                                                                 op0=ALU.mult, op1=ALU.mult), (B1, B2, VEC_b), (B2,))
                emit(DVE, lambda: nc.vector.tensor_tensor(YM[ym][:, q0:q1], T2, SGB[:, q0:q1], op=ALU.mult),
                     (B2, SGB_b), (YM_b[ym],))

            ng = (nb + 1) // 2

            def sgroup(g):
                for bi in range(2 * g, min(2 * g + 2, nb)):
                    scores(bi)

            sgroup(0)
            if ng > 1:
                sgroup(1)
            for g in range(ng):
                vis = list(range(2 * g, min(2 * g + 2, nb)))
                if g == 0 and pend is not None:
                    pend[0]()
                for bi in vis:
                    probs(bi)
                if g == 0 and pend is not None:
                    pend[1]()
                for bi in vis:
                    pv(bi)
                if g == 0 and pend is not None:
                    pend[2]()
                if g + 2 < ng:
                    sgroup(g + 2)
                yield
            return (part1a, part1, part2)

        def e_load(h):
            ET, ET_b, EM, EM_b, EMM, ES, ES_b = ET2[h % 2], ET_b2[h % 2], EM2[h % 2], EM_b2[h % 2], EMM2[h % 2], ES2[h % 2], ES_b2[h % 2]
            for e in range(NE):
                delta = -640 + 128 * e
                emit_dma(SP, lambda e=e, delta=delta: nc.sync.dma_start(out=ET[:, e, :], in_=hankel(h, delta - 255 + OFF, 128, 256)),
                         (ev_b,), (ET_b[e],))
            for k in range(3):
                emit_dma(SP, lambda k=k: nc.sync.dma_start(out=EM[:, k, :], in_=hankel(h, -16 - 256 * k - 255 + OFF, 16, 256)),
                         (ev_b,), (EM_b,))
            emit_dma(SP, lambda: nc.sync.dma_start(out=EMM[:, :], in_=hankel(h, -15 + OFF, 16, 16)), (ev_b,), (EM_b,))
            for jb in range(11, 16):
                emit_dma(SP, lambda jb=jb: nc.sync.dma_start(out=ES[:, jb - 11, :], in_=hankel(h, 128 * jb - 2048 - 31 + OFF, 128, 32)),
                         (ev_b,), (ES_b,))
            emit_dma(SP, lambda: nc.sync.dma_start(out=ES[0:32, 5, :], in_=hankel(h, -31 + OFF, 32, 32)), (ev_b,), (ES_b,))

        def head(l, h):
            (sk, sv, sq_, sg), widx = w_get(l, h, ("k", "v", "q", "gb"))
            ym = 1
            ET, ET_b, EM, EM_b, EMM, ES, ES_b = ET2[h % 2], ET_b2[h % 2], EM2[h % 2], EM_b2[h % 2], EMM2[h % 2], ES2[h % 2], ES_b2[h % 2]
            emit(DVE, lambda: nc.vector.memset(ET[64:128, 5, 192:256], 0.0), (), (ET_b[5],))
            emit(DVE, lambda: nc.vector.memset(ET[64:128, 6, 64:128], 0.0), (), (ET_b[6],))
            hs = int(os.environ.get("MK_HSTOP", 99))
            if hs < 1:
                return
            def cache_load(hh):
                emit_dma(POOL, lambda: nc.gpsimd.dma_start(out=KCT[:], in_=ck[l][:, hh * 128:(hh + 1) * 128].rearrange("(b p) d -> p b d", p=128)),
                         (), tuple(KCT_b))
                emit_dma(POOL, lambda: nc.gpsimd.dma_start(out=VC[:], in_=cv[l][:, hh * 128:(hh + 1) * 128].rearrange("(b p) d -> p b d", p=128)),
                         (), (VC_b,))

            if h == 0:
                cache_load(0)
            if hs < 2:
                return
            for tb in range(int(os.environ.get("MK_TB0", 0)), int(os.environ.get("MK_TBMAX", NTB))):
                r0, R = tb_rows(tb)
                b = mmbank()
                rb = [WS_b[sk], WS_b[sv]] + [uT_b[i] for i in spans_of(r0, r0 + R)]
                for dc in range(16):
                    emit(PE, lambda dc=dc: nc.tensor.matmul(PB[b][0:R, 0:256], lhsT=uT[:, dc, r0:r0 + R], rhs=WS[:, sk:sk + 2, dc, :],
                                                            start=(dc == 0), stop=(dc == 15)), rb, (PB_b[b],), mark=(dc == 15))
                ks = tb % 2
                copy(ACT, KVS[ks][0:R, :], PB[b][0:R, 0:256], (PB_b[b],), (KVS_b[ks],))
                copy(DVE, VH[0:R, tb, :], KVS[ks][0:R, 0:256], (KVS_b[ks],), (VH_b,))
                cols = slice(h * 128, (h + 1) * 128)
                if os.environ.get("MK_NOKVDMA"):
                    continue
                if tb == 0:
                    emit_dma(ACT, lambda: nc.scalar.dma_start(out=kso[l][:, cols], in_=KVS[ks][0:32, 0:128]), (KVS_b[ks],), ())
                    emit_dma(ACT, lambda: nc.scalar.dma_start(out=vso[l][:, cols], in_=KVS[ks][0:32, 128:256]), (KVS_b[ks],), ())
                    emit_dma(ACT, lambda: nc.scalar.dma_start(out=kp[l][0:16, cols], in_=KVS[ks][32:48, 0:128]), (KVS_b[ks],), ())
                    emit_dma(ACT, lambda: nc.scalar.dma_start(out=vp[l][0:16, cols], in_=KVS[ks][32:48, 128:256]), (KVS_b[ks],), ())
                else:
                    p0 = 16 + 128 * (tb - 1)
                    emit_dma(ACT, lambda: nc.scalar.dma_start(out=kp[l][p0:p0 + 128, cols], in_=KVS[ks][:, 0:128]), (KVS_b[ks],), ())
                    emit_dma(ACT, lambda: nc.scalar.dma_start(out=vp[l][p0:p0 + 128, cols], in_=KVS[ks][:, 128:256]), (KVS_b[ks],), ())
                yield
            if hs < 3:
                return
            b = mmbank()
            for dc in range(16):
                emit(PE, lambda dc=dc: nc.tensor.matmul(PB[b][0:16, 0:128], lhsT=uT[:, dc, 32:48], rhs=WS[:, sv, dc, :],
                                                        start=(dc == 0), stop=(dc == 15)), (WS_b[sv], uT_b[0]), (PB_b[b],), mark=(dc == 15))
            copy(DVE, VM[:, :], PB[b][0:16, 0:128], (PB_b[b],), (VM_b,))
            for grp in ([0], [1, 2, 3, 4], [5, 6, 7, 8], [9, 10, 11, 12], [13, 14, 15, 16]):
                b = mmbank()
                pvw = PB[b][:].bitcast(BF16)
                off = 0
                for gi, tb in enumerate(grp):
                    r0, R = tb_rows(tb)
                    emit(PE, lambda tb=tb, R=R, off=off: nc.tensor.transpose(pvw[:, off:off + R], VH[0:R, tb, 0:128], ident[0:R, 0:R]),
                         (VH_b, CONST_b), (PB_b[b],), mark=(gi == len(grp) - 1))
                    off += R
                c0 = tb_rows(grp[0])[0]
                copy(evac_eng(), KT[:, c0:c0 + off], pvw[:, 0:off], (PB_b[b],), (KT_b,) + YMB_b[0] + YMB_b[1])
            yield
            for si, (c0, c1) in enumerate(SPANS):
                b = proj_fm(sq_, c0, c1)
                copy(evac_eng(), QT[:, c0:c1], PB[b][:, 0:c1 - c0], (PB_b[b],), (QT_b,) + YMB_b[0] + YMB_b[1])
                b = proj_fm(sg, c0, c1)
                emit(ACT, lambda b=b: nc.scalar.activation(out=SGB[:, c0:c1], in_=PB[b][:, 0:c1 - c0], func=AF.Exp, scale=-1.0),
                     (PB_b[b],), (SGB_b,))
                emit(ACT, lambda: nc.scalar.activation(out=SGB[:, c0:c1], in_=SGB[:, c0:c1], func=AF.Ln, bias=ONEC[:, 0:1]),
                     (SGB_b, CONST_b), (SGB_b,))
                emit(ACT, lambda: nc.scalar.activation(out=SGB[:, c0:c1], in_=SGB[:, c0:c1], func=AF.Exp, scale=-1.0),
                     (SGB_b,), (SGB_b,))
                emit(DVE, lambda b=b: nc.vector.tensor_tensor(SGB[:, c0:c1], PB[b][:, 0:c1 - c0], SGB[:, c0:c1], op=ALU.mult),
                     (PB_b[b], SGB_b), (SGB_b,))
                yield
            w_release(widx)
            if not (l == nl - 1 and h == nun - 1):
                e_load((h + 1) % nun)
            if hs < 4:
                return
            for g in range(4):
                slot = g % 2
                pv = PB[slot][:].bitcast(BF16)
                mmrr[0] = (slot + 1) % 4
                for i in range(4):
                    jb = 4 * g + i
                    emit(PE, lambda i=i, jb=jb: nc.tensor.transpose(pv[:, i * 128:(i + 1) * 128], KCT[:, jb, :], ident[:, :]),
                         tuple(KCT_b) + (CONST_b,), (PB_b[slot],), mark=(i == 3))
                copy(evac_eng(), KTC[:, 512 * g:512 * (g + 1)], pv[:, 0:512], (PB_b[slot],), (KTC_b,))
            if hs < 5:
                return
            blocks = []
            for jb in range(16):
                const = (128 * jb + 127 - 2048) <= R15
                blocks.append(dict(kt=KTC[:, 128 * jb:128 * (jb + 1)], v=VC[:, jb, :], nk=128, cs=0, rd=[KTC_b, VC_b],
                                   e=('c',) if const else ('h', ES[:, jb - 11, ::-1]), erd=[ES_b]))
            blocks.append(dict(kt=KT[:, 0:32], v=VH[0:32, 0, 128:256], nk=32, cs=0, rd=[KT_b, VH_b], e=('h', ES[0:32, 5, ::-1]), erd=[ES_b]))
            pend = yield from attend(l, h, 0, 32, blocks, ym, None)
            if h + 1 < nun:
                cache_load(h + 1)
            if hs < 6:
                return
            blocks = [dict(kt=KT[:, 32:48], v=VM[:, :], nk=16, cs=0, rd=[KT_b, VM_b], e=('h', EMM[:, ::-1]), erd=[EM_b])]
            pend = yield from attend(l, h, 32, 48, blocks, ym, pend)
            if hs < 7:
                return
            for k in range(8):
                q0 = 48 + 256 * k
                q1 = q0 + 256
                if -1 - 256 * k > R15:
                    blocks = [dict(kt=KT[:, 32:48], v=VM[:, :], nk=16, cs=0, rd=[KT_b, VM_b], e=('h', EM[:, k, ::-1]), erd=[EM_b])]
                else:
                    blocks = [dict(kt=KT[:, 32:48], v=VM[:, :], nk=16, cs=0, rd=[KT_b, VM_b], e=('c',), erd=[])]
                for jb in range(2 * k + 2):
                    delta = 128 * jb - 256 * k
                    kc = 48 + 128 * jb
                    d = dict(kt=KT[:, kc:kc + 128], v=VH[:, 1 + jb, 128:256], nk=128, cs=max(0, delta), rd=[KT_b, VH_b])
                    if delta + 127 <= R15:
                        d["e"] = ('c',)
                        d["erd"] = []
                    else:
                        e = (delta + 640) // 128
                        cs = d["cs"]
                        d["e"] = ('h', ET[:, e, 255 - cs::-1] if cs > 0 else ET[:, e, ::-1])
                        d["erd"] = [ET_b[e]]
                    blocks.append(d)
                pend = yield from attend(l, h, q0, q1, blocks, ym, pend)
            pend[0]()
            pend[1]()
            pend[2]()
            emit_dma(SP, lambda: nc.sync.dma_start(out=ymscr[8 + h], in_=YM[ym][:, :]), (YM_b[ym],), (ymscr_b[8 + h],))

        def load_gates(l):
            emit(DVE, lambda: nc.vector.memset(GW[:], 0.0), (), (GW_b,))
            for gi, gw in enumerate((grw, giw)):
                for half in range(2):
                    src = gw[l].rearrange("(j t) c d -> t c j d", t=2)[half]
                    emit_dma(POOL, lambda src=src, gi=gi, half=half: nc.gpsimd.dma_start(
                        out=GW[half * 64:(half + 1) * 64, gi * 8:(gi + 1) * 8, half * 64:(half + 1) * 64], in_=src), (), (GW_b,))

        def phaseC(l):
            allu = list(uT_b)
            for cc in range(16):
                emit_dma(POOL, lambda cc=cc: nc.gpsimd.dma_start(out=WO[:, cc, :], in_=w_out[l][cc * 128:(cc + 1) * 128, :]), (), allu)
            emit_dma(SP, lambda: nc.sync.dma_start(out=GB[:], in_=post_g[l].partition_broadcast(128)), (), (GB_b,))
            last = (l == nl - 1)
            def c_loads(tb):
                r0, R = tb_rows(tb)
                emit_dma(SP, lambda: nc.sync.dma_start(out=YMB[tb % 2][:, :, 0:R], in_=ymscr[:, :, r0:r0 + R].rearrange("c p n -> p c n")),
                         ymscr_b, YMB_b[tb % 2] + ((QT_b, KT_b) if tb < 2 else ()))
                load_x(l, tb, tb % 2)

            c_loads(0)
            for tb in range(NTB):
                r0, R = tb_rows(tb)
                xt = tb % 2
                yb = tb % 2
                if tb + 1 < NTB:
                    c_loads(tb + 1)
                ab = 4 if tb % 2 == 0 else 0
                SS_b, RSTD_b = SS_bb[xt], RSTD_bb[xt]
                sso = 8 * xt + 4
                rsc = RSTD[0:R, 2 * xt + 1:2 * xt + 2]
                for cc in range(16):
                    for g in range(4):
                        emit(PE, lambda cc=cc, g=g: nc.tensor.matmul(PB[ab + g][0:R, :], lhsT=YMB[yb][:, cc, 0:R],
                                                                     rhs=WO[:, cc, g * 512:(g + 1) * 512],
                                                                     start=(cc == 0), stop=(cc == 15)),
                             list(YMB_b[yb]) + allu, (PB_b[ab + g],), mark=(cc == 15))
                for g in range(4):
                    tc_ = g % 2
                    emit(ACT, lambda g=g, tc_=tc_: nc.scalar.activation(out=TMPC[tc_][0:R, :], in_=PB[ab + g][0:R, :], func=AF.Square,
                                                                        accum_out=SS[0:R, sso + g:sso + g + 1]),
                         (PB_b[ab + g],), (TMPC_b[tc_], SS_b))
                emit(DVE, lambda: nc.vector.tensor_reduce(out=rsc, in_=SS[0:R, sso:sso + 4], axis=AX.X, op=ALU.add),
                     (SS_b,), (RSTD_b,))
                emit(ACT, lambda: nc.scalar.activation(out=rsc, in_=rsc, func=AF.Ln, scale=1.0 / D, bias=EPSC[0:R, 0:1]),
                     (RSTD_b, CONST_b), (RSTD_b,))
                emit(ACT, lambda: nc.scalar.activation(out=rsc, in_=rsc, func=AF.Exp, scale=-0.5),
                     (RSTD_b,), (RSTD_b,))
                for g in range(4):
                    tc_ = g % 2
                    emit(DVE, lambda g=g, tc_=tc_: nc.vector.scalar_tensor_tensor(out=TMPC[tc_][0:R, :], in0=PB[ab + g][0:R, :],
                                                                                  scalar=rsc, in1=GB[0:R, g * 512:(g + 1) * 512],
                                                                                  op0=ALU.mult, op1=ALU.mult),
                         (PB_b[ab + g], RSTD_b, GB_b), (TMPC_b[tc_],))
                    emit(DVE, lambda g=g, tc_=tc_: nc.vector.tensor_tensor(XT[xt][0:R, g * 512:(g + 1) * 512], XT[xt][0:R, g * 512:(g + 1) * 512],
                                                                           TMPC[tc_][0:R, :], op=ALU.add),
                         (TMPC_b[tc_], XT_b[xt]), (XT_b[xt],))
                if last:
                    if tb == 0:
                        emit_dma(SP, lambda: nc.sync.dma_start(out=ys, in_=XT[xt][0:32, :]), (XT_b[xt],), ())
                    else:
                        emit_dma(SP, lambda: nc.sync.dma_start(out=yp[(tb - 1) * 128:tb * 128, :], in_=XT[xt][:, :]), (XT_b[xt],), ())
                else:
                    emit_dma(SP, lambda: nc.sync.dma_start(out=xscr[l % 2][r0:r0 + R, :], in_=XT[xt][0:R, :]),
                             (XT_b[xt],), (xscr_b[l % 2][tb],))

        for i in range(min(NSLOT, len(wq))):
            w_issue(i)
        e_load(0)
        stop = int(os.environ.get("MK_STOP", 9))
        KINT = int(os.environ.get("MK_KINT", 1))
        for l in range(nl):
            if stop >= 1:
                load_gates(l)
                halo(l, 0)
                phaseA(l)
            if stop >= 2:
                for u in range(nun):
                    gH = head(l, u)
                    gA = mixerA(l, u)
                    next(gH)
                    aliveA = True
                    cnt = 0
                    for _ in gH:
                        cnt += 1
                        if aliveA and cnt % KINT == 0:
                            try:
                                next(gA)
                            except StopIteration:
                                aliveA = False
                    if aliveA:
                        for _ in gA:
                            pass
            if stop >= 4:
                phaseC(l)

        for e in (SP, POOL, ACT):
            for i, sem in enumerate(e.dsems):
                if e.dtot[i] > 0:
                    nc.sync.wait_ge(sem, e.dtot[i])
    return nc


_CACHE = {}


def _sel_matrix():
    s = np.zeros((32, NREL), np.float32)
    s[BUCKETS, np.arange(NREL)] = 1.0
    return s


def kernel(x_prompt, x_sample, cache_k, cache_v, state_conv, state_rglru, meta, rel_bias,
           pre_g, post_g, w_in, conv_w, conv_b, gate_r_w, gate_r_b, gate_i_w, gate_i_b,
           rglru_lam, lam_q1, lam_k1, lam_q2, lam_k2, subln_g, w_out):
    nl = int(os.environ.get("MK_LAYERS", DEPTH))
    ncores = int(os.environ.get("MK_CORES", NCORES))
    if nl not in _CACHE:
        _CACHE[nl] = build(nl)
    nc = _CACHE[nl]
    f = lambda a: np.ascontiguousarray(np.asarray(a, dtype=np.float32))
    shared = {
        "meta": f(meta), "rel_bias": f(rel_bias), "pre_g": f(pre_g), "post_g": f(post_g), "w_in": f(w_in),
        "conv_w": f(conv_w), "conv_b": f(conv_b), "gate_r_w": f(gate_r_w),
        "gate_r_b": f(gate_r_b).reshape(DEPTH, 1024), "gate_i_w": f(gate_i_w),
        "gate_i_b": f(gate_i_b).reshape(DEPTH, 1024), "rglru_lam": f(rglru_lam),
        "lam_q1": f(lam_q1), "lam_k1": f(lam_k1), "lam_q2": f(lam_q2), "lam_k2": f(lam_k2),
        "subln_g": f(subln_g), "w_out": f(w_out), "sel": _sel_matrix(),
    }
    in_maps = []
    for c in range(ncores):
        m = dict(shared)
        m["xp"] = f(x_prompt[c])
        m["xs"] = f(x_sample[c])
        m["ck"] = f(np.asarray(cache_k)[:, c].reshape(DEPTH, 2048, 1024))
        m["cv"] = f(np.asarray(cache_v)[:, c].reshape(DEPTH, 2048, 1024))
        m["sc"] = f(np.asarray(state_conv)[:, c])
        m["sr"] = f(np.asarray(state_rglru)[:, c])
        in_maps.append(m)
    res = run_bass_kernel_spmd(nc, in_maps, core_ids=list(range(ncores)))
    R = res.results
    st = lambda k, ax: np.stack([np.asarray(r[k]) for r in R], axis=ax)
    y_prompt = st("yp", 0)
    y_sample = st("ys", 0)
    k_prompt = st("kp", 1).reshape(DEPTH, ncores, 2064, 8, 128)
    v_prompt = st("vp", 1).reshape(DEPTH, ncores, 2064, 8, 128)
    conv_prompt = st("cp", 1)
    rglru_prompt = st("rp", 1)
    k_sample = st("ks", 1).reshape(DEPTH, ncores, 32, 8, 128)
    v_sample = st("vs", 1).reshape(DEPTH, ncores, 32, 8, 128)
    conv_sample = st("cs", 1)
    rglru_sample = st("rs", 1)
    return (y_prompt, y_sample, k_prompt, v_prompt, conv_prompt, rglru_prompt,
            k_sample, v_sample, conv_sample, rglru_sample)
```

```python
import os
import math
import bisect
import contextlib
import numpy as np
import concourse.bass as bass
import concourse.mybir as mybir
from concourse.bass_utils import run_bass_kernel_spmd

F32 = mybir.dt.float32
BF16 = mybir.dt.bfloat16
AF = mybir.ActivationFunctionType
ALU = mybir.AluOpType
AX = mybir.AxisListType

D = 2048
DEPTH = 4
NT = 2096
SPANS = [(0, 48), (48, 560), (560, 1072), (1072, 1584), (1584, 2096)]
NTB = 17
EPS = 1e-6
OFF = 2080
NREL = 2080 + 2064
NCORES = 8


def tb_rows(tb):
    return (0, 48) if tb == 0 else (48 + 128 * (tb - 1), 128)


def bucket_table():
    rel = np.arange(-OFF, NREL - OFF).astype(np.int64)
    half, max_exact = 16, 8
    ret = np.where(rel > 0, half, 0).astype(np.int32)
    n = np.abs(rel).astype(np.int32)
    nf = np.maximum(n, 1).astype(np.float32)
    lg = (np.log(nf / np.float32(max_exact)) / np.float32(math.log(1024 / max_exact))
          * np.float32(half - max_exact)).astype(np.float32)
    large = max_exact + lg.astype(np.int32)
    large = np.minimum(large, half - 1)
    return ret + np.where(n < max_exact, n, large)


BUCKETS = bucket_table()
_nb = np.nonzero(BUCKETS != 15)[0]
R15 = int(_nb[0]) - OFF - 1
assert -641 <= R15 < -513, R15


class Buf:
    __slots__ = ("w", "r")

    def __init__(self):
        self.w = {}
        self.r = {}


class Eng:
    def __init__(self, nc, es, h, name, nd=0, skip_self=False):
        self.h = h
        self.name = name
        self.sem = es.enter_context(nc.semaphore("s_" + name))
        self.seq = 0
        self.incs = []
        self.last = None
        self.seen = {}
        self.skip_self = skip_self
        self.eager = not skip_self
        self.dsems = [es.enter_context(nc.semaphore("d_%s%d" % (name, i))) for i in range(nd)]
        self.dtot = [0] * nd
        self.rr = 0


def _resolve(tok):
    if tok[0] == 'd':
        return tok[1], tok[2]
    e, seq = tok[1], tok[2]
    i = bisect.bisect_left(e.incs, seq)
    if i == len(e.incs):
        e.last.then_inc(e.sem, 1)
        e.incs.append(e.seq)
    return e.sem, i + 1


def _tmax(d, key, tok):
    o = d.get(key)
    if o is None or o[2] < tok[2]:
        d[key] = tok


def _wait_for(eng, toks):
    waits = {}
    for t in toks:
        if t[0] == 'c' and t[1] is eng and eng.skip_self:
            continue
        sem, val = _resolve(t)
        k = id(sem)
        if k not in waits or waits[k][1] < val:
            waits[k] = (sem, val)
    for k, (sem, val) in waits.items():
        if eng.seen.get(k, 0) < val:
            eng.h.wait_ge(sem, val)
            eng.seen[k] = val


def _flat(bs):
    out = []
    for b in bs:
        if isinstance(b, (tuple, list)):
            out.extend(_flat(b))
        else:
            out.append(b)
    return out


def _deps(reads, writes):
    toks = []
    for b in reads:
        toks.extend(b.w.values())
    for b in writes:
        toks.extend(b.w.values())
        toks.extend(b.r.values())
    return toks


def emit(eng, fn, reads=(), writes=(), mark=False):
    reads, writes = _flat(reads), _flat(writes)
    _wait_for(eng, _deps(reads, writes))
    ins = fn()
    eng.seq += 1
    eng.last = ins
    if eng.eager or mark:
        ins.then_inc(eng.sem, 1)
        eng.incs.append(eng.seq)
    tok = ('c', eng, eng.seq)
    for b in reads:
        _tmax(b.r, id(eng), tok)
    for b in writes:
        _tmax(b.w, id(eng), tok)
    return ins


def emit_dma(eng, fn, reads=(), writes=()):
    reads, writes = _flat(reads), _flat(writes)
    i = eng.rr
    eng.rr = (i + 1) % len(eng.dsems)
    sem = eng.dsems[i]
    _wait_for(eng, _deps(reads, writes))
    if eng.seen.get(id(sem), 0) < eng.dtot[i]:
        eng.h.wait_ge(sem, eng.dtot[i])
        eng.seen[id(sem)] = eng.dtot[i]
    ins = fn()
    eng.dtot[i] += 16
    ins.then_inc(sem, 16)
    tok = ('d', sem, eng.dtot[i])
    for b in reads:
        _tmax(b.r, id(sem), tok)
    for b in writes:
        _tmax(b.w, id(sem), tok)
    return ins


def build(nl):
    nc = bass.Bass("TRN2", target_bir_lowering=False)
    es = contextlib.ExitStack()

    def din(name, shape):
        return nc.dram_tensor(name, list(shape), F32, kind="ExternalInput")

    def dout(name, shape):
        return nc.dram_tensor(name, list(shape), F32, kind="ExternalOutput")

    xp = din("xp", [2048, D]).ap()
    xs = din("xs", [32, D]).ap()
    ck = din("ck", [DEPTH, 2048, 1024]).ap()
    cv = din("cv", [DEPTH, 2048, 1024]).ap()
    scv = din("sc", [DEPTH, 3, 1024]).ap()
    srg = din("sr", [DEPTH, 1024]).ap()
    meta = din("meta", [16, D]).ap()
    relb = din("rel_bias", [32, 8]).ap()
    pre_g = din("pre_g", [DEPTH, D]).ap()
    post_g = din("post_g", [DEPTH, D]).ap()
    w_in = din("w_in", [DEPTH, D, 6144]).ap()
    conv_w = din("conv_w", [DEPTH, 4, 1024]).ap()
    conv_b = din("conv_b", [DEPTH, 1024]).ap()
    grw = din("gate_r_w", [DEPTH, 16, 64, 64]).ap()
    grb = din("gate_r_b", [DEPTH, 1024]).ap()
    giw = din("gate_i_w", [DEPTH, 16, 64, 64]).ap()
    gib = din("gate_i_b", [DEPTH, 1024]).ap()
    rlam = din("rglru_lam", [DEPTH, 1024]).ap()
    lq1 = din("lam_q1", [DEPTH, 64]).ap()
    lk1 = din("lam_k1", [DEPTH, 64]).ap()
    lq2 = din("lam_q2", [DEPTH, 64]).ap()
    lk2 = din("lam_k2", [DEPTH, 64]).ap()
    subg = din("subln_g", [DEPTH, 128]).ap()
    w_out = din("w_out", [DEPTH, D, D]).ap()
    sel = din("sel", [32, NREL]).ap()

    yp = dout("yp", [2048, D]).ap()
    ys = dout("ys", [32, D]).ap()
    kp = dout("kp", [DEPTH, 2064, 1024]).ap()
    vp = dout("vp", [DEPTH, 2064, 1024]).ap()
    cpo = dout("cp", [DEPTH, 3, 1024]).ap()
    rpo = dout("rp", [DEPTH, 1024]).ap()
    kso = dout("ks", [DEPTH, 32, 1024]).ap()
    vso = dout("vs", [DEPTH, 32, 1024]).ap()
    cso = dout("cs", [DEPTH, 3, 1024]).ap()
    rso = dout("rs", [DEPTH, 1024]).ap()

    xscr = [nc.dram_tensor("xscr%d" % i, [NT, D], F32, kind="Internal").ap() for i in range(2)]
    ymscr = nc.dram_tensor("ymscr", [16, 128, NT], BF16, kind="Internal").ap()
    evscr_t = nc.dram_tensor("evscr", [8, NREL], BF16, kind="Internal")
    evscr = evscr_t.ap()

    with es:
        def sb(name, shape, dt):
            return es.enter_context(nc.sbuf_tensor(name, list(shape), dt))

        PE = Eng(nc, es, nc.tensor, "pe", skip_self=True)
        ACT = Eng(nc, es, nc.scalar, "act", nd=6)
        DVE = Eng(nc, es, nc.vector, "dve")
        POOL = Eng(nc, es, nc.gpsimd, "pool", nd=8)
        SP = Eng(nc, es, nc.sync, "sp", nd=12)

        R1 = sb("R1", [128, 16 * NT], BF16)
        uT = R1[:].rearrange("p (c n) -> p c n", c=16)
        WO = R1[:, 0:16 * 2048].rearrange("p (c n) -> p c n", c=16)
        uT_b = [Buf() for _ in SPANS]
        NSLOT = 6
        WS = sb("WS", [128, NSLOT, 16, 128], BF16)
        WS_b = [Buf() for _ in range(NSLOT)]
        XT = [sb("XT%d" % i, [128, D], F32) for i in range(2)]
        XT_b = [(Buf(), Buf()), Buf()]
        YM = [sb("YM%d" % i, [128, NT], BF16) for i in range(2)]
        YM_b = [Buf(), Buf()]
        TK = sb("TK", [128, 1024], F32)
        TMPC = [TK[:, 0:512], TK[:, 512:1024]]
        TMPC_b = [Buf(), Buf()]
        QK = sb("QK", [128, 2 * NT], BF16)
        QT = QK[:, 0:NT]
        QT_b = Buf()
        KT = QK[:, NT:2 * NT]
        KT_b = Buf()
        YMB = [QK[:, i * 2048:(i + 1) * 2048].rearrange("p (c n) -> p c n", c=16) for i in range(2)]
        YMB_b = [(Buf(),), (Buf(),)]
        VH = sb("VH", [128, NTB, 256], BF16)
        VH_b = Buf()
        VM = sb("VM", [16, 128], BF16)
        VM_b = Buf()
        SGB = sb("SGB", [128, NT], F32)
        SGB_b = Buf()
        GB = SGB[:, 0:D]
        GB_b = SGB_b
        KCT = TK[:].bitcast(BF16).rearrange("p (b d) -> p b d", b=16)
        KCT_b = TMPC_b
        KTC = sb("KTC", [128, 2048], BF16)
        KTC_b = Buf()
        VC = sb("VC", [128, 16, 128], BF16)
        VC_b = Buf()
        UB = [VC[:].rearrange("p b d -> p (b d)"), KTC[:, :]]
        UB_b = [VC_b, KTC_b]
        KVS = [sb("KVS%d" % i, [128, 256], F32) for i in range(2)]
        KVS_b = [Buf(), Buf()]
        NE = 7
        ET2 = [sb("ET%d" % i, [128, NE, 256], BF16) for i in range(2)]
        ET_b2 = [[Buf() for _ in range(NE)] for i in range(2)]
        EM2 = [sb("EM%d" % i, [16, 3, 256], BF16) for i in range(2)]
        EM_b2 = [Buf(), Buf()]
        EMM2 = [sb("EMM%d" % i, [16, 16], BF16) for i in range(2)]
        ES2 = [sb("ES%d" % i, [128, 6, 32], BF16) for i in range(2)]
        ES_b2 = [Buf(), Buf()]
        NPT = 4
        PT = [sb("PT%d" % i, [128, 2, 256], BF16) for i in range(NPT)]
        PT_b = [Buf() for _ in range(NPT)]
        NTMP = 10
        TMall = sb("TMall", [128, NTMP, 512], F32)
        TM = [TMall[:, i, :] for i in range(NTMP)]
        TM_b = [Buf() for _ in range(NTMP)]
        XA = sb("XA", [128, NT + 6], F32)
        XA_b = Buf()
        HH = [sb("HH%d" % i, [128, 512], F32) for i in range(2)]
        HH_b = [Buf(), Buf()]
        GW = sb("GW", [128, 16, 128], BF16)
        GW_b = Buf()
        ident = sb("ident", [128, 128], BF16)
        identf = sb("identf", [128, 128], F32)
        ones_b = sb("ones_b", [128, 128], BF16)
        ones_f = sb("ones_f", [128, 128], F32)
        CONST_b = Buf()
        EPSC = sb("EPSC", [128, 1], F32)
        ONEC = sb("ONEC", [128, 1], F32)
        CW = sb("CW", [128, DEPTH * 8, 4], F32)
        CB = sb("CB", [128, DEPTH * 8], F32)
        BR = sb("BR", [128, DEPTH * 8], F32)
        BI = sb("BI", [128, DEPTH * 8], F32)
        CA = sb("CA", [128, DEPTH * 8], F32)
        CA2 = sb("CA2", [128, DEPTH * 8], F32)
        SR0 = sb("SR0", [128, DEPTH * 8], F32)
        SG = sb("SG", [128, DEPTH], F32)
        NLAM = sb("NLAM", [128, DEPTH], F32)
        LT = XT[0][:, 0:4 * DEPTH * 64].rearrange("p (i k) -> p i k", i=4)
        LT2 = sb("LT2", [128, 2, DEPTH], F32)
        F15 = sb("F15", [128, 8], F32)
        RB = sb("RB", [32, 8], F32)
        SS = sb("SS", [128, 16], F32)
        SS_bb = [Buf(), Buf()]
        RSTD = sb("RSTD", [128, 4], F32)
        RSTD_bb = [Buf(), Buf()]
        VEC_b = Buf()

        PS = es.enter_context(nc.psum_tensor("PS", [128, 8, 512], F32))
        PB = [PS[:, i, :] for i in range(8)]
        PB_b = [Buf() for _ in range(8)]
        mmrr = [0]

        def mmbank():
            i = mmrr[0]
            mmrr[0] = (i + 1) % 4
            return i

        PAIRS = [(0, 1), (2, 3)]
        live = [None]
        fprr = [0]

        def free_pair():
            return (6, 7)

        slotc = [0]

        def next_slot():
            v = slotc[0]
            slotc[0] += 1
            return (v // 2) % 2, v % 2

        def bc2(a2):
            return bass.AP(a2.tensor, a2.offset, [list(a2.ap[0]), [0, 2], list(a2.ap[1])])

        tmrr = [0]

        def tmp():
            i = tmrr[0]
            tmrr[0] = (i + 1) % NTMP
            return i

        xscr_b = [[Buf() for _ in range(NTB)] for _ in range(2)]
        ymscr_b = [Buf() for _ in range(16)]
        ev_b = Buf()

        evac_rr = [0]

        def evac_eng():
            evac_rr[0] ^= 1
            return ACT if evac_rr[0] else DVE

        def copy(eng, out, in_, reads, writes):
            if eng is ACT:
                return emit(ACT, lambda: nc.scalar.copy(out, in_), reads, writes)
            return emit(eng, lambda: eng.h.tensor_copy(out, in_), reads, writes)

        emit(POOL, lambda: nc.gpsimd.memset(identf[:], 1.0), (), (CONST_b,))
        emit(POOL, lambda: nc.gpsimd.affine_select(out=identf[:], in_=identf[:], pattern=[[-1, 128]],
                                                   compare_op=ALU.is_equal, fill=0.0, base=0,
                                                   channel_multiplier=1), (), (CONST_b,))
        emit(POOL, lambda: nc.gpsimd.memset(ones_f[:], 1.0), (), (CONST_b,))
        emit(POOL, lambda: nc.gpsimd.memset(EPSC[:], EPS), (), (CONST_b,))
        emit(POOL, lambda: nc.gpsimd.memset(ONEC[:], 1.0), (), (CONST_b,))
        emit(DVE, lambda: nc.vector.tensor_copy(ident[:], identf[:]), (CONST_b,), (CONST_b,))
        emit(DVE, lambda: nc.vector.tensor_copy(ones_b[:], ones_f[:]), (CONST_b,), (CONST_b,))
        emit(DVE, lambda: nc.vector.memset(XA[:], 0.0), (), (XA_b,))

        def small(dst, src):
            emit_dma(SP, lambda: nc.sync.dma_start(out=dst, in_=src, allow_slow_non_contiguous=True),
                     (), (VEC_b,))

        for l in range(nl):
            for k in range(4):
                small(CW[:, l * 8:(l + 1) * 8, k], conv_w[l][k].rearrange("(c p) -> p c", p=128))
        small(CB[:, 0:nl * 8], conv_b[0:nl].rearrange("l (c p) -> p (l c)", p=128))
        small(BI[:, 0:nl * 8], gib[0:nl].rearrange("l (c p) -> p (l c)", p=128))
        small(CA[:, 0:nl * 8], rlam[0:nl].rearrange("l (c p) -> p (l c)", p=128))
        small(SR0[:, 0:nl * 8], srg[0:nl].rearrange("l (c p) -> p (l c)", p=128))
        small(SG[:, :], subg.rearrange("l p -> p l"))
        for i, t in enumerate((lq1, lk1, lq2, lk2)):
            emit_dma(SP, lambda i=i, t=t: nc.sync.dma_start(out=LT[:, i, :], in_=t.rearrange("l k -> (l k)").partition_broadcast(128)), (), (XT_b[0], VEC_b))
        small(F15[:, :], relb[15:16, :].rearrange("o h -> (o h)").partition_broadcast(128))
        small(RB[:, :], relb)

        emit(ACT, lambda: nc.scalar.activation(out=CA[:], in_=CA[:], func=AF.Exp, scale=-1.0), (VEC_b,), (VEC_b,))
        V = (VEC_b,)
        emit(DVE, lambda: nc.vector.tensor_scalar(CA2[:], CA[:], 2.0, None, op0=ALU.add), V, V)
        emit(DVE, lambda: nc.vector.reciprocal(CA2[:], CA2[:]), V, V)
        emit(DVE, lambda: nc.vector.tensor_tensor(CA[:], CA[:], CA2[:], op=ALU.mult), V, V)
        emit(DVE, lambda: nc.vector.tensor_tensor(CA2[:], CA[:], CA[:], op=ALU.mult), V, V)
        emit(DVE, lambda: nc.vector.memset(BR[:], 1.0 / 15), V, V)
        for cf in (1.0 / 13, 1.0 / 11, 1.0 / 9, 1.0 / 7, 1.0 / 5, 1.0 / 3, 1.0):
            emit(DVE, lambda: nc.vector.tensor_tensor(BR[:], BR[:], CA2[:], op=ALU.mult), V, V)
            emit(DVE, lambda cf=cf: nc.vector.tensor_scalar(BR[:], BR[:], cf, None, op0=ALU.add), V, V)
        emit(DVE, lambda: nc.vector.tensor_tensor(CA[:], CA[:], BR[:], op=ALU.mult), V, V)
        emit(DVE, lambda: nc.vector.tensor_scalar(CA[:], CA[:], 2.0, None, op0=ALU.mult), V, V)
        emit(DVE, lambda: nc.vector.tensor_scalar(CA2[:], CA[:], -16.0, None, op0=ALU.mult), (VEC_b,), (VEC_b,))
        emit(DVE, lambda: nc.vector.tensor_scalar(CA[:], CA[:], -8.0, None, op0=ALU.mult), (VEC_b,), (VEC_b,))
        small(BR[:, 0:nl * 8], grb[0:nl].rearrange("l (c p) -> p (l c)", p=128))
        emit(DVE, lambda: nc.vector.tensor_scalar(BR[:], BR[:], -1.0, None, op0=ALU.mult), (VEC_b,), (VEC_b,))
        emit(DVE, lambda: nc.vector.tensor_scalar(BI[:], BI[:], -1.0, None, op0=ALU.mult), (VEC_b,), (VEC_b,))
        emit(DVE, lambda: nc.vector.tensor_tensor(LT[:, 0, :], LT[:, 0, :], LT[:, 1, :], op=ALU.mult), (VEC_b, XT_b[0]), (VEC_b, XT_b[0]))
        emit(DVE, lambda: nc.vector.tensor_tensor(LT[:, 2, :], LT[:, 2, :], LT[:, 3, :], op=ALU.mult), (VEC_b, XT_b[0]), (VEC_b, XT_b[0]))
        emit(DVE, lambda: nc.vector.tensor_reduce(out=LT2[:, 0, :], in_=LT[:, 0, :].rearrange("p (l k) -> p l k", k=64),
                                                  axis=AX.X, op=ALU.add), (VEC_b, XT_b[0]), (VEC_b,))
        emit(DVE, lambda: nc.vector.tensor_reduce(out=LT2[:, 1, :], in_=LT[:, 2, :].rearrange("p (l k) -> p l k", k=64),
                                                  axis=AX.X, op=ALU.add), (VEC_b, XT_b[0]), (VEC_b,))
        emit(ACT, lambda: nc.scalar.activation(out=LT2[:], in_=LT2[:], func=AF.Exp), (VEC_b,), (VEC_b,))
        emit(DVE, lambda: nc.vector.tensor_tensor(NLAM[:], LT2[:, 1, :], LT2[:, 0, :], op=ALU.subtract), (VEC_b,), (VEC_b,))
        for l in range(DEPTH):
            li = 0.8 - 0.6 * math.exp(-0.3 * l)
            emit(DVE, lambda l=l, li=li: nc.vector.tensor_scalar(NLAM[:, l:l + 1], NLAM[:, l:l + 1], -li, None, op0=ALU.add),
                 (VEC_b,), (VEC_b,))
            emit(DVE, lambda l=l, li=li: nc.vector.tensor_scalar(SG[:, l:l + 1], SG[:, l:l + 1], 1.0 - li, None, op0=ALU.mult),
                 (VEC_b,), (VEC_b,))
        for ci, c0 in enumerate(range(0, NREL, 512)):
            c1 = min(NREL, c0 + 512)
            b = mmbank()
            ts_ = tmp()
            pp = ci % 2
            emit_dma(SP, lambda: nc.sync.dma_start(out=TM[ts_][0:32, 0:c1 - c0], in_=sel[:, c0:c1]), (), (TM_b[ts_],))
            emit(PE, lambda: nc.tensor.matmul(PB[b][0:8, 0:c1 - c0], lhsT=RB[:, :], rhs=TM[ts_][0:32, 0:c1 - c0],
                                              start=True, stop=True), (VEC_b, TM_b[ts_]), (PB_b[b],), mark=True)
            stg = PT[pp][:].rearrange("p m c -> p (m c)")
            emit(ACT, lambda: nc.scalar.activation(out=stg[0:8, 0:c1 - c0], in_=PB[b][0:8, 0:c1 - c0], func=AF.Exp),
                 (PB_b[b],), (PT_b[pp],))
            emit_dma(SP, lambda: nc.sync.dma_start(out=evscr[:, c0:c1], in_=stg[0:8, 0:c1 - c0]), (PT_b[pp],), (ev_b,))

        def hankel(h, base, npart, ncol):
            return bass.AP(evscr_t, h * NREL + base, [[1, npart], [1, ncol]])

        wq = []
        nun = int(os.environ.get("MK_UNITS", 8))
        WPOS = [{"k": 0, "v": 1, "x": 2, "g": 3, "q": 4, "gb": 5}, {"k": 0, "v": 1, "q": 2, "gb": 3, "x": 4, "g": 5}]
        for l in range(nl):
            for u in range(nun):
                pos = WPOS[(l * nun + u) % 2]
                blk = {"k": 24 + u, "v": 32 + u, "q": 16 + u, "gb": 40 + u, "x": u, "g": 8 + u}
                for role in sorted(pos, key=lambda r: pos[r]):
                    wq.append((l, blk[role]))
        wissued = [False] * len(wq)
        wdone = [False] * len(wq)

        def w_issue(i):
            l, c = wq[i]
            s = i % NSLOT
            emit_dma(POOL, lambda: nc.gpsimd.dma_start(
                out=WS[:, s, :, :], in_=w_in[l][:, c * 128:(c + 1) * 128].rearrange("(c p) n -> p c n", p=128)),
                (), (WS_b[s],))
            wissued[i] = True

        def w_get(l, u, roles):
            pair = l * nun + u
            idx = [NSLOT * pair + WPOS[pair % 2][r] for r in roles]
            assert all(wissued[i] for i in idx), (l, u, roles)
            return [i % NSLOT for i in idx], idx

        def w_release(idxs):
            for i in idxs:
                wdone[i] = True
            lo = 0
            while lo < len(wq) and wissued[lo]:
                lo += 1
            for i in range(lo, min(len(wq), lo + 2 * NSLOT)):
                if not wissued[i] and (i < NSLOT or wdone[i - NSLOT]):
                    w_issue(i)

        def spans_of(c0, c1):
            return [i for i, (a, b) in enumerate(SPANS) if a < c1 and b > c0]

        def proj_fm(slot, c0, c1, b=None):
            if b is None:
                b = mmbank()
            rb = [WS_b[slot]] + [uT_b[i] for i in spans_of(c0, c1)]
            for dc in range(16):
                emit(PE, lambda dc=dc: nc.tensor.matmul(PB[b][:, 0:c1 - c0], lhsT=WS[:, slot, dc, :], rhs=uT[:, dc, c0:c1],
                                                        start=(dc == 0), stop=(dc == 15)), rb, (PB_b[b],), mark=(dc == 15))
            return b

        def xa_idx(c):
            return c + 3 if c < 32 else c + 6

        def load_x(l, tb, xt):
            r0, R = tb_rows(tb)
            if l == 0:
                if tb == 0:
                    emit_dma(SP, lambda: nc.sync.dma_start(out=XT[xt][0:32, :], in_=xs), (), (XT_b[xt],))
                    emit_dma(SP, lambda: nc.sync.dma_start(out=XT[xt][32:48, :], in_=meta), (), (XT_b[xt],))
                else:
                    emit_dma(SP, lambda: nc.sync.dma_start(out=XT[xt][:, :], in_=xp[(tb - 1) * 128:tb * 128, :]),
                             (), (XT_b[xt],))
            else:
                src = xscr[(l - 1) % 2]
                emit_dma(SP, lambda: nc.sync.dma_start(out=XT[xt][0:R, :], in_=src[r0:r0 + R, :]),
                         (xscr_b[(l - 1) % 2][tb],), (XT_b[xt],))

        def phaseA(l):
            emit_dma(SP, lambda: nc.sync.dma_start(out=GB[:], in_=pre_g[l].partition_broadcast(128)), (), (GB_b,))

            def s1(tb):
                r0, R = tb_rows(tb)
                xt = tb % 2
                SS_b, RSTD_b = SS_bb[xt], RSTD_bb[xt]
                ssc = SS[0:R, 8 * xt:8 * xt + 1]
                rsc = RSTD[0:R, 2 * xt:2 * xt + 1]
                load_x(l, tb, xt)
                emit(ACT, lambda: nc.scalar.activation(out=UB[xt][0:R, :], in_=XT[xt][0:R, :], func=AF.Square,
                                                       accum_out=ssc), (XT_b[xt],), (UB_b[xt], SS_b))
                emit(ACT, lambda: nc.scalar.activation(out=rsc, in_=ssc, func=AF.Ln, scale=1.0 / D, bias=EPSC[0:R, 0:1]),
                     (SS_b, CONST_b), (RSTD_b,))
                emit(ACT, lambda: nc.scalar.activation(out=rsc, in_=rsc, func=AF.Exp, scale=-0.5),
                     (RSTD_b,), (RSTD_b,))
                emit(DVE, lambda: nc.vector.scalar_tensor_tensor(out=UB[xt][0:R, :], in0=XT[xt][0:R, :], scalar=rsc,
                                                                 in1=GB[0:R, :], op0=ALU.mult, op1=ALU.mult),
                     (XT_b[xt], RSTD_b, GB_b), (UB_b[xt],))

            def s2(tb):
                r0, R = tb_rows(tb)
                xt = tb % 2
                for g in range(4):
                    slot = g % 2
                    pv = PB[4 + slot][:].bitcast(BF16)
                    for i in range(4):
                        dc = 4 * g + i
                        emit(PE, lambda i=i, dc=dc: nc.tensor.transpose(pv[:, i * 128:i * 128 + R], UB[xt][0:R, dc * 128:(dc + 1) * 128],
                                                                        ident[0:R, 0:R]),
                             (UB_b[xt], CONST_b), (PB_b[4 + slot],), mark=(i == 3))
                    src = pv[:, 0:512].rearrange("p (i n) -> p i n", i=4)[:, :, 0:R]
                    dst = uT[:, 4 * g:4 * g + 4, r0:r0 + R]
                    copy(evac_eng(), dst, src, (PB_b[4 + slot],), [uT_b[i] for i in spans_of(r0, r0 + R)])

            s1(0)
            for tb in range(NTB):
                if tb + 1 < NTB:
                    s1(tb + 1)
                s2(tb)

        def halo(l, j):
            emit_dma(SP, lambda: nc.sync.dma_start(out=XA[:, 0:3], in_=scv[l][:, j * 128:(j + 1) * 128].rearrange("k p -> p k"),
                                                   allow_slow_non_contiguous=True), (), (XA_b,))

        def mixerA(l, j):
            (sx, sg), widx = w_get(l, j, ("x", "g"))
            lj = l * 8 + j
            ym = 0
            stage = {}

            def front(si):
                c0, c1 = SPANS[si]
                W = c1 - c0
                bx, bg = free_pair()
                proj_fm(sx, c0, c1, bx)
                proj_fm(sg, c0, c1, bg)
                segs = [(0, 32), (32, 48)] if si == 0 else [(c0, c1)]
                for (a, b) in segs:
                    copy(DVE, XA[:, xa_idx(a):xa_idx(a) + (b - a)], PB[bx][:, a - c0:b - c0], (PB_b[bx],), (XA_b,))
                tg, txc, txb = [3 * (si % 2) + k for k in range(3)]
                emit(ACT, lambda: nc.scalar.activation(out=TM[tg][:, 0:W], in_=PB[bg][:, 0:W], func=AF.Exp, scale=-1.0),
                     (PB_b[bg],), (TM_b[tg],))
                yield
                emit(ACT, lambda: nc.scalar.activation(out=TM[tg][:, 0:W], in_=TM[tg][:, 0:W], func=AF.Ln, bias=ONEC[:, 0:1]),
                     (TM_b[tg], CONST_b), (TM_b[tg],))
                yield
                emit(ACT, lambda: nc.scalar.activation(out=TM[tg][:, 0:W], in_=TM[tg][:, 0:W], func=AF.Exp, scale=-1.0),
                     (TM_b[tg],), (TM_b[tg],))
                yield
                emit(DVE, lambda: nc.vector.tensor_tensor(TM[tg][:, 0:W], PB[bg][:, 0:W], TM[tg][:, 0:W], op=ALU.mult),
                     (PB_b[bg], TM_b[tg]), (TM_b[tg],))
                for (a, b) in segs:
                    i0 = xa_idx(a)
                    n = b - a
                    o = TM[txc][:, a - c0:b - c0]
                    emit(DVE, lambda: nc.vector.tensor_scalar(o, XA[:, i0 - 3:i0 - 3 + n], CW[:, lj, 0:1], CB[:, lj:lj + 1],
                                                              op0=ALU.mult, op1=ALU.add), (XA_b, VEC_b), (TM_b[txc],))
                    for k in (1, 2, 3):
                        emit(DVE, lambda k=k: nc.vector.scalar_tensor_tensor(out=o, in0=XA[:, i0 - 3 + k:i0 - 3 + k + n],
                                                                             scalar=CW[:, lj, k:k + 1], in1=o,
                                                                             op0=ALU.mult, op1=ALU.add),
                             (XA_b, VEC_b, TM_b[txc]), (TM_b[txc],))
                xcb = TM[txb][:].bitcast(BF16)
                yield
                emit(ACT, lambda: nc.scalar.copy(xcb[:, 0:W], TM[txc][:, 0:W]), (TM_b[txc],), (TM_b[txb],))
                stage[si] = (tg, txc, txb, xcb, segs, W, c0)

            def back(si):
                tg, txc, txb, xcb, segs, W, c0 = stage.pop(si)
                br_, bi_ = free_pair()
                emit(PE, lambda: nc.tensor.matmul(PB[br_][:, 0:W], lhsT=GW[:, j, :], rhs=xcb[:, 0:W], start=True, stop=True),
                     (GW_b, TM_b[txb]), (PB_b[br_],), mark=True)
                emit(PE, lambda: nc.tensor.matmul(PB[bi_][:, 0:W], lhsT=GW[:, 8 + j, :], rhs=xcb[:, 0:W], start=True, stop=True),
                     (GW_b, TM_b[txb]), (PB_b[bi_],), mark=True)
                tr_, ti_, ta_, ts_ = 6, 7, 8, 9
                yield
                emit(ACT, lambda: nc.scalar.activation(out=TM[tr_][:, 0:W], in_=PB[br_][:, 0:W], func=AF.Exp, scale=-1.0,
                                                       bias=BR[:, lj:lj + 1]), (PB_b[br_], VEC_b), (TM_b[tr_],))
                yield
                emit(ACT, lambda: nc.scalar.activation(out=TM[ti_][:, 0:W], in_=PB[bi_][:, 0:W], func=AF.Exp, scale=-1.0,
                                                       bias=BI[:, lj:lj + 1]), (PB_b[bi_], VEC_b), (TM_b[ti_],))
                yield
                ri = TMall[:, tr_:tr_ + 2, 0:W]
                emit(ACT, lambda: nc.scalar.activation(out=ri, in_=ri, func=AF.Ln, bias=ONEC[:, 0:1]),
                     (TM_b[tr_], TM_b[ti_], CONST_b), (TM_b[tr_], TM_b[ti_]))
                yield
                emit(ACT, lambda: nc.scalar.activation(out=ri, in_=ri, func=AF.Exp, scale=-1.0),
                     (TM_b[tr_], TM_b[ti_]), (TM_b[tr_], TM_b[ti_]))
                yield
                emit(ACT, lambda: nc.scalar.activation(out=TM[ta_][:, 0:W], in_=TM[tr_][:, 0:W], func=AF.Exp,
                                                       scale=CA[:, lj:lj + 1]), (TM_b[tr_], VEC_b), (TM_b[ta_],))
                yield
                emit(ACT, lambda: nc.scalar.activation(out=TM[ts_][:, 0:W], in_=TM[tr_][:, 0:W], func=AF.Exp,
                                                       scale=CA2[:, lj:lj + 1]), (TM_b[tr_], VEC_b), (TM_b[ts_],))
                emit(DVE, lambda: nc.vector.tensor_scalar(TM[ts_][:, 0:W], TM[ts_][:, 0:W], 0.99999994, None, op0=ALU.min),
                     (TM_b[ts_],), (TM_b[ts_],))
                yield
                emit(ACT, lambda: nc.scalar.activation(out=TM[ts_][:, 0:W], in_=TM[ts_][:, 0:W], func=AF.Ln, scale=-1.0, bias=ONEC[:, 0:1]),
                     (TM_b[ts_], CONST_b), (TM_b[ts_],))
                yield
                emit(ACT, lambda: nc.scalar.activation(out=TM[ts_][:, 0:W], in_=TM[ts_][:, 0:W], func=AF.Exp, scale=0.5),
                     (TM_b[ts_],), (TM_b[ts_],))
                yield
                emit(DVE, lambda: nc.vector.tensor_tensor(TM[ti_][:, 0:W], TM[ti_][:, 0:W], TM[txc][:, 0:W], op=ALU.mult),
                     (TM_b[ti_], TM_b[txc]), (TM_b[ti_],))
                emit(DVE, lambda: nc.vector.tensor_tensor(TM[ti_][:, 0:W], TM[ti_][:, 0:W], TM[ts_][:, 0:W], op=ALU.mult),
                     (TM_b[ti_], TM_b[ts_]), (TM_b[ti_],))
                hb = si % 2
                for (a, b) in segs:
                    n = b - a
                    lo = a - c0
                    if si == 0 and a == 0:
                        init = SR0[:, lj:lj + 1]
                        ir = (VEC_b,)
                    elif si == 0:
                        init = 0.0
                        ir = ()
                    else:
                        pw = SPANS[si - 1][1] - SPANS[si - 1][0]
                        init = HH[1 - hb][:, pw - 1:pw]
                        ir = (HH_b[1 - hb],)
                    emit(DVE, lambda: nc.vector.tensor_tensor_scan(HH[hb][:, lo:lo + n], TM[ta_][:, lo:lo + n], TM[ti_][:, lo:lo + n],
                                                                   init, op0=ALU.mult, op1=ALU.add),
                         (TM_b[ta_], TM_b[ti_]) + tuple(ir), (HH_b[hb],))
                emit(DVE, lambda: nc.vector.tensor_tensor(YM[ym][:, c0:c0 + W], HH[hb][:, 0:W], TM[tg][:, 0:W], op=ALU.mult),
                     (HH_b[hb], TM_b[tg]), (YM_b[ym],))
                if si == 0:
                    emit_dma(SP, lambda: nc.sync.dma_start(out=rso[l:l + 1, j * 128:(j + 1) * 128].rearrange("o p -> p o"),
                                                           in_=HH[hb][:, 31:32], allow_slow_non_contiguous=True),
                             (HH_b[hb],), ())
                if si == 4:
                    emit_dma(SP, lambda: nc.sync.dma_start(out=rpo[l:l + 1, j * 128:(j + 1) * 128].rearrange("o p -> p o"),
                                                           in_=HH[hb][:, W - 1:W], allow_slow_non_contiguous=True),
                             (HH_b[hb],), ())

            for si in range(6):
                if si < 5:
                    yield from front(si)
                    yield
                if si >= 1:
                    yield from back(si - 1)
                    yield
            w_release(widx)
            emit_dma(SP, lambda: nc.sync.dma_start(out=cso[l][:, j * 128:(j + 1) * 128].rearrange("k p -> p k"),
                                                   in_=XA[:, 32:35], allow_slow_non_contiguous=True), (XA_b,), ())
            emit_dma(SP, lambda: nc.sync.dma_start(out=cpo[l][:, j * 128:(j + 1) * 128].rearrange("k p -> p k"),
                                                   in_=XA[:, NT + 3:NT + 6], allow_slow_non_contiguous=True), (XA_b,), ())
            emit_dma(SP, lambda: nc.sync.dma_start(out=ymscr[j], in_=YM[ym][:, :]), (YM_b[ym],), (ymscr_b[j],))
            if j + 1 < nun:
                halo(l, j + 1)

        def attend(l, h, q0, q1, blocks, ym, pend):
            W = q1 - q0
            nb = len(blocks)
            sl = {}
            LA = 3

            def scores(bi):
                bk = blocks[bi]
                nk, cs = bk["nk"], bk["cs"]
                p, hf = (bi // 2) % 2, bi % 2
                c0 = hf * 256
                for m in (0, 1):
                    emit(PE, lambda m=m: nc.tensor.matmul(PB[2 * p + m][0:nk, c0 + cs:c0 + W], lhsT=bk["kt"][64 * m:64 * m + 64, :],
                                                          rhs=QT[64 * m:64 * m + 64, q0 + cs:q1], start=True, stop=True),
                         tuple(bk["rd"]) + (QT_b,), (PB_b[2 * p + m],), mark=(m == 1))
                sl[bi] = (p, c0)

            def probs(bi):
                bk = blocks[bi]
                nk, cs = bk["nk"], bk["cs"]
                p, c0 = sl.pop(bi)
                pi = bi % NPT
                src = PS[0:nk, 2 * p:2 * p + 2, c0 + cs:c0 + W]
                dst = PT[pi][0:nk, :, cs:W]
                rb = (PB_b[2 * p], PB_b[2 * p + 1])
                if bk["e"][0] == 'c':
                    emit(ACT, lambda: nc.scalar.activation(out=dst, in_=src, func=AF.Exp, scale=0.125, bias=F15[0:nk, h:h + 1]),
                         rb + (VEC_b,), (PT_b[pi],))
                else:
                    emit(ACT, lambda: nc.scalar.activation(out=dst, in_=src, func=AF.Exp, scale=0.125), rb, (PT_b[pi],))
                    emit(DVE, lambda: nc.vector.tensor_tensor(dst, dst, bc2(bk["e"][1]), op=ALU.mult),
                         (PT_b[pi],) + tuple(bk["erd"]), (PT_b[pi],))

            def pv(bi):
                bk = blocks[bi]
                nk, cs = bk["nk"], bk["cs"]
                pi = bi % NPT
                st, sp = (bi == 0), (bi == nb - 1)
                rhs = PT[pi][0:nk, :, cs:W]
                o4 = PB[4].rearrange("p (m c) -> p m c", m=2)[:, :, cs:W]
                o5 = PB[5].rearrange("p (m c) -> p m c", m=2)[:, :, cs:W]
                emit(PE, lambda: nc.tensor.matmul(o4, lhsT=bk["v"], rhs=rhs, start=st, stop=sp),
                     tuple(bk["rd"]) + (PT_b[pi],), (PB_b[4],))
                emit(PE, lambda: nc.tensor.matmul(o5, lhsT=ones_b[0:nk, :], rhs=rhs, start=st, stop=sp),
                     (CONST_b, PT_b[pi]), (PB_b[5],), mark=True)

            def v2(ap512):
                return ap512.rearrange("p (m c) -> p m c", m=2)[:, :, 0:W]

            T1f, T3f, T2 = XT[0][:, 0:512], XT[0][:, 512:1024], XT[1][:, 0:W]
            B1a, B1, B2 = XT_b[0][0], XT_b[0][1], XT_b[1]

            def part1a():
                emit(DVE, lambda: nc.vector.tensor_copy(v2(T1f), v2(PB[4])), (PB_b[4],), (B1a,))
                emit(ACT, lambda: nc.scalar.activation(out=v2(T3f), in_=v2(PB[5]), func=AF.Ln), (PB_b[5],), (B1,))

            def part1():
                emit(ACT, lambda: nc.scalar.activation(out=v2(T3f), in_=v2(T3f), func=AF.Exp, scale=-1.0), (B1,), (B1,))
                emit(DVE, lambda: nc.vector.tensor_tensor(v2(T1f), v2(T1f), v2(T3f), op=ALU.mult), (B1a, B1), (B1a,))
                emit(DVE, lambda: nc.vector.scalar_tensor_tensor(out=T2, in0=T1f[:, 256:256 + W], scalar=NLAM[:, l:l + 1], in1=T1f[:, 0:W],
                                                                 op0=ALU.mult, op1=ALU.add), (B1a, VEC_b), (B2,))

            def part2():
                emit(ACT, lambda: nc.scalar.activation(out=T1f[:, 0:W], in_=T2, func=AF.Square), (B2,), (B1a,))
                p, c0 = 0, 0
                emit(PE, lambda: nc.tensor.matmul(PB[2 * p][:, c0:c0 + W], lhsT=ones_f[:, :], rhs=T1f[:, 0:W], start=True, stop=True),
                     (CONST_b, B1a), (PB_b[2 * p],), mark=True)
                emit(ACT, lambda: nc.scalar.activation(out=T3f[:, 0:W], in_=PB[2 * p][:, c0:c0 + W], func=AF.Ln, scale=1.0 / 128,
                                                       bias=EPSC[:, 0:1]), (PB_b[2 * p], CONST_b), (B1,))
                emit(ACT, lambda: nc.scalar.activation(out=T3f[:, 0:W], in_=T3f[:, 0:W], func=AF.Exp, scale=-0.5), (B1,), (B1,))
                emit(DVE, lambda: nc.vector.scalar_tensor_tensor(out=T2, in0=T2, scalar=SG[:, l:l + 1], in1=T3f[:, 0:W],
                                                                 op0=ALU.mult, op1=ALU.mult), (B1, B2, VEC_b), (B2,))
                emit(DVE, lambda: nc.vector.tensor_tensor(YM[ym][:, q0:q1], T2, SGB[:, q0:q1], op=ALU.mult),
                     (B2, SGB_b), (YM_b[ym],))

            ng = (nb + 1) // 2

            def sgroup(g):
                for bi in range(2 * g, min(2 * g + 2, nb)):
                    scores(bi)

            sgroup(0)
            if ng > 1:
                sgroup(1)
            for g in range(ng):
                vis = list(range(2 * g, min(2 * g + 2, nb)))
                if g == 0 and pend is not None:
                    pend[0]()
                for bi in vis:
                    probs(bi)
                if g == 0 and pend is not None:
                    pend[1]()
                for bi in vis:
                    pv(bi)
                if g == 0 and pend is not None:
                    pend[2]()
                if g + 2 < ng:
                    sgroup(g + 2)
                yield
            return (part1a, part1, part2)

        def e_load(h):
            ET, ET_b, EM, EM_b, EMM, ES, ES_b = ET2[h % 2], ET_b2[h % 2], EM2[h % 2], EM_b2[h % 2], EMM2[h % 2], ES2[h % 2], ES_b2[h % 2]
            for e in range(NE):
                delta = -640 + 128 * e
                emit_dma(SP, lambda e=e, delta=delta: nc.sync.dma_start(out=ET[:, e, :], in_=hankel(h, delta - 255 + OFF, 128, 256)),
                         (ev_b,), (ET_b[e],))
            for k in range(3):
                emit_dma(SP, lambda k=k: nc.sync.dma_start(out=EM[:, k, :], in_=hankel(h, -16 - 256 * k - 255 + OFF, 16, 256)),
                         (ev_b,), (EM_b,))
            emit_dma(SP, lambda: nc.sync.dma_start(out=EMM[:, :], in_=hankel(h, -15 + OFF, 16, 16)), (ev_b,), (EM_b,))
            for jb in range(11, 16):
                emit_dma(SP, lambda jb=jb: nc.sync.dma_start(out=ES[:, jb - 11, :], in_=hankel(h, 128 * jb - 2048 - 31 + OFF, 128, 32)),
                         (ev_b,), (ES_b,))
            emit_dma(SP, lambda: nc.sync.dma_start(out=ES[0:32, 5, :], in_=hankel(h, -31 + OFF, 32, 32)), (ev_b,), (ES_b,))

        def head(l, h):
            (sk, sv, sq_, sg), widx = w_get(l, h, ("k", "v", "q", "gb"))
            ym = 1
            ET, ET_b, EM, EM_b, EMM, ES, ES_b = ET2[h % 2], ET_b2[h % 2], EM2[h % 2], EM_b2[h % 2], EMM2[h % 2], ES2[h % 2], ES_b2[h % 2]
            emit(DVE, lambda: nc.vector.memset(ET[64:128, 5, 192:256], 0.0), (), (ET_b[5],))
            emit(DVE, lambda: nc.vector.memset(ET[64:128, 6, 64:128], 0.0), (), (ET_b[6],))
            hs = int(os.environ.get("MK_HSTOP", 99))
            if hs < 1:
                return
            def cache_load(hh):
                emit_dma(POOL, lambda: nc.gpsimd.dma_start(out=KCT[:], in_=ck[l][:, hh * 128:(hh + 1) * 128].rearrange("(b p) d -> p b d", p=128)),
                         (), tuple(KCT_b))
                emit_dma(POOL, lambda: nc.gpsimd.dma_start(out=VC[:], in_=cv[l][:, hh * 128:(hh + 1) * 128].rearrange("(b p) d -> p b d", p=128)),
                         (), (VC_b,))

            if h == 0:
                cache_load(0)
            if hs < 2:
                return
            for tb in range(int(os.environ.get("MK_TB0", 0)), int(os.environ.get("MK_TBMAX", NTB))):
                r0, R = tb_rows(tb)
                b = mmbank()
                rb = [WS_b[sk], WS_b[sv]] + [uT_b[i] for i in spans_of(r0, r0 + R)]
                for dc in range(16):
                    emit(PE, lambda dc=dc: nc.tensor.matmul(PB[b][0:R, 0:256], lhsT=uT[:, dc, r0:r0 + R], rhs=WS[:, sk:sk + 2, dc, :],
                                                            start=(dc == 0), stop=(dc == 15)), rb, (PB_b[b],), mark=(dc == 15))
                ks = tb % 2
                copy(ACT, KVS[ks][0:R, :], PB[b][0:R, 0:256], (PB_b[b],), (KVS_b[ks],))
                copy(DVE, VH[0:R, tb, :], KVS[ks][0:R, 0:256], (KVS_b[ks],), (VH_b,))
                cols = slice(h * 128, (h + 1) * 128)
                if os.environ.get("MK_NOKVDMA"):
                    continue
                if tb == 0:
                    emit_dma(ACT, lambda: nc.scalar.dma_start(out=kso[l][:, cols], in_=KVS[ks][0:32, 0:128]), (KVS_b[ks],), ())
                    emit_dma(ACT, lambda: nc.scalar.dma_start(out=vso[l][:, cols], in_=KVS[ks][0:32, 128:256]), (KVS_b[ks],), ())
                    emit_dma(ACT, lambda: nc.scalar.dma_start(out=kp[l][0:16, cols], in_=KVS[ks][32:48, 0:128]), (KVS_b[ks],), ())
                    emit_dma(ACT, lambda: nc.scalar.dma_start(out=vp[l][0:16, cols], in_=KVS[ks][32:48, 128:256]), (KVS_b[ks],), ())
                else:
                    p0 = 16 + 128 * (tb - 1)
                    emit_dma(ACT, lambda: nc.scalar.dma_start(out=kp[l][p0:p0 + 128, cols], in_=KVS[ks][:, 0:128]), (KVS_b[ks],), ())
                    emit_dma(ACT, lambda: nc.scalar.dma_start(out=vp[l][p0:p0 + 128, cols], in_=KVS[ks][:, 128:256]), (KVS_b[ks],), ())
                yield
            if hs < 3:
                return
            b = mmbank()
            for dc in range(16):
                emit(PE, lambda dc=dc: nc.tensor.matmul(PB[b][0:16, 0:128], lhsT=uT[:, dc, 32:48], rhs=WS[:, sv, dc, :],
                                                        start=(dc == 0), stop=(dc == 15)), (WS_b[sv], uT_b[0]), (PB_b[b],), mark=(dc == 15))
            copy(DVE, VM[:, :], PB[b][0:16, 0:128], (PB_b[b],), (VM_b,))
            for grp in ([0], [1, 2, 3, 4], [5, 6, 7, 8], [9, 10, 11, 12], [13, 14, 15, 16]):
                b = mmbank()
                pvw = PB[b][:].bitcast(BF16)
                off = 0
                for gi, tb in enumerate(grp):
                    r0, R = tb_rows(tb)
                    emit(PE, lambda tb=tb, R=R, off=off: nc.tensor.transpose(pvw[:, off:off + R], VH[0:R, tb, 0:128], ident[0:R, 0:R]),
                         (VH_b, CONST_b), (PB_b[b],), mark=(gi == len(grp) - 1))
                    off += R
                c0 = tb_rows(grp[0])[0]
                copy(evac_eng(), KT[:, c0:c0 + off], pvw[:, 0:off], (PB_b[b],), (KT_b,) + YMB_b[0] + YMB_b[1])
            yield
            for si, (c0, c1) in enumerate(SPANS):
                b = proj_fm(sq_, c0, c1)
                copy(evac_eng(), QT[:, c0:c1], PB[b][:, 0:c1 - c0], (PB_b[b],), (QT_b,) + YMB_b[0] + YMB_b[1])
                b = proj_fm(sg, c0, c1)
                emit(ACT, lambda b=b: nc.scalar.activation(out=SGB[:, c0:c1], in_=PB[b][:, 0:c1 - c0], func=AF.Exp, scale=-1.0),
                     (PB_b[b],), (SGB_b,))
                emit(ACT, lambda: nc.scalar.activation(out=SGB[:, c0:c1], in_=SGB[:, c0:c1], func=AF.Ln, bias=ONEC[:, 0:1]),
                     (SGB_b, CONST_b), (SGB_b,))
                emit(ACT, lambda: nc.scalar.activation(out=SGB[:, c0:c1], in_=SGB[:, c0:c1], func=AF.Exp, scale=-1.0),
                     (SGB_b,), (SGB_b,))
                emit(DVE, lambda b=b: nc.vector.tensor_tensor(SGB[:, c0:c1], PB[b][:, 0:c1 - c0], SGB[:, c0:c1], op=ALU.mult),
                     (PB_b[b], SGB_b), (SGB_b,))
                yield
            w_release(widx)
            if not (l == nl - 1 and h == nun - 1):
                e_load((h + 1) % nun)
            if hs < 4:
                return
            for g in range(4):
                slot = g % 2
                pv = PB[slot][:].bitcast(BF16)
                mmrr[0] = (slot + 1) % 4
                for i in range(4):
                    jb = 4 * g + i
                    emit(PE, lambda i=i, jb=jb: nc.tensor.transpose(pv[:, i * 128:(i + 1) * 128], KCT[:, jb, :], ident[:, :]),
                         tuple(KCT_b) + (CONST_b,), (PB_b[slot],), mark=(i == 3))
                copy(evac_eng(), KTC[:, 512 * g:512 * (g + 1)], pv[:, 0:512], (PB_b[slot],), (KTC_b,))
            if hs < 5:
                return
            blocks = []
            for jb in range(16):
                const = (128 * jb + 127 - 2048) <= R15
                blocks.append(dict(kt=KTC[:, 128 * jb:128 * (jb + 1)], v=VC[:, jb, :], nk=128, cs=0, rd=[KTC_b, VC_b],
                                   e=('c',) if const else ('h', ES[:, jb - 11, ::-1]), erd=[ES_b]))
            blocks.append(dict(kt=KT[:, 0:32], v=VH[0:32, 0, 128:256], nk=32, cs=0, rd=[KT_b, VH_b], e=('h', ES[0:32, 5, ::-1]), erd=[ES_b]))
            pend = yield from attend(l, h, 0, 32, blocks, ym, None)
            if h + 1 < nun:
                cache_load(h + 1)
            if hs < 6:
                return
            blocks = [dict(kt=KT[:, 32:48], v=VM[:, :], nk=16, cs=0, rd=[KT_b, VM_b], e=('h', EMM[:, ::-1]), erd=[EM_b])]
            pend = yield from attend(l, h, 32, 48, blocks, ym, pend)
            if hs < 7:
                return
            for k in range(8):
                q0 = 48 + 256 * k
                q1 = q0 + 256
                if -1 - 256 * k > R15:
                    blocks = [dict(kt=KT[:, 32:48], v=VM[:, :], nk=16, cs=0, rd=[KT_b, VM_b], e=('h', EM[:, k, ::-1]), erd=[EM_b])]
                else:
                    blocks = [dict(kt=KT[:, 32:48], v=VM[:, :], nk=16, cs=0, rd=[KT_b, VM_b], e=('c',), erd=[])]
                for jb in range(2 * k + 2):
                    delta = 128 * jb - 256 * k
                    kc = 48 + 128 * jb
                    d = dict(kt=KT[:, kc:kc + 128], v=VH[:, 1 + jb, 128:256], nk=128, cs=max(0, delta), rd=[KT_b, VH_b])
                    if delta + 127 <= R15:
                        d["e"] = ('c',)
                        d["erd"] = []
                    else:
                        e = (delta + 640) // 128
                        cs = d["cs"]
                        d["e"] = ('h', ET[:, e, 255 - cs::-1] if cs > 0 else ET[:, e, ::-1])
                        d["erd"] = [ET_b[e]]
                    blocks.append(d)
                pend = yield from attend(l, h, q0, q1, blocks, ym, pend)
            pend[0]()
            pend[1]()
            pend[2]()
            emit_dma(SP, lambda: nc.sync.dma_start(out=ymscr[8 + h], in_=YM[ym][:, :]), (YM_b[ym],), (ymscr_b[8 + h],))

        def load_gates(l):
            emit(DVE, lambda: nc.vector.memset(GW[:], 0.0), (), (GW_b,))
            for gi, gw in enumerate((grw, giw)):
                for half in range(2):
                    src = gw[l].rearrange("(j t) c d -> t c j d", t=2)[half]
                    emit_dma(POOL, lambda src=src, gi=gi, half=half: nc.gpsimd.dma_start(
                        out=GW[half * 64:(half + 1) * 64, gi * 8:(gi + 1) * 8, half * 64:(half + 1) * 64], in_=src), (), (GW_b,))

        def phaseC(l):
            allu = list(uT_b)
            for cc in range(16):
                emit_dma(POOL, lambda cc=cc: nc.gpsimd.dma_start(out=WO[:, cc, :], in_=w_out[l][cc * 128:(cc + 1) * 128, :]), (), allu)
            emit_dma(SP, lambda: nc.sync.dma_start(out=GB[:], in_=post_g[l].partition_broadcast(128)), (), (GB_b,))
            last = (l == nl - 1)
            def c_loads(tb):
                r0, R = tb_rows(tb)
                emit_dma(SP, lambda: nc.sync.dma_start(out=YMB[tb % 2][:, :, 0:R], in_=ymscr[:, :, r0:r0 + R].rearrange("c p n -> p c n")),
                         ymscr_b, YMB_b[tb % 2] + ((QT_b, KT_b) if tb < 2 else ()))
                load_x(l, tb, tb % 2)

            c_loads(0)
            for tb in range(NTB):
                r0, R = tb_rows(tb)
                xt = tb % 2
                yb = tb % 2
                if tb + 1 < NTB:
                    c_loads(tb + 1)
                ab = 4 if tb % 2 == 0 else 0
                SS_b, RSTD_b = SS_bb[xt], RSTD_bb[xt]
                sso = 8 * xt + 4
                rsc = RSTD[0:R, 2 * xt + 1:2 * xt + 2]
                for cc in range(16):
                    for g in range(4):
                        emit(PE, lambda cc=cc, g=g: nc.tensor.matmul(PB[ab + g][0:R, :], lhsT=YMB[yb][:, cc, 0:R],
                                                                     rhs=WO[:, cc, g * 512:(g + 1) * 512],
                                                                     start=(cc == 0), stop=(cc == 15)),
                             list(YMB_b[yb]) + allu, (PB_b[ab + g],), mark=(cc == 15))
                for g in range(4):
                    tc_ = g % 2
                    emit(ACT, lambda g=g, tc_=tc_: nc.scalar.activation(out=TMPC[tc_][0:R, :], in_=PB[ab + g][0:R, :], func=AF.Square,
                                                                        accum_out=SS[0:R, sso + g:sso + g + 1]),
                         (PB_b[ab + g],), (TMPC_b[tc_], SS_b))
                emit(DVE, lambda: nc.vector.tensor_reduce(out=rsc, in_=SS[0:R, sso:sso + 4], axis=AX.X, op=ALU.add),
                     (SS_b,), (RSTD_b,))
                emit(ACT, lambda: nc.scalar.activation(out=rsc, in_=rsc, func=AF.Ln, scale=1.0 / D, bias=EPSC[0:R, 0:1]),
                     (RSTD_b, CONST_b), (RSTD_b,))
                emit(ACT, lambda: nc.scalar.activation(out=rsc, in_=rsc, func=AF.Exp, scale=-0.5),
                     (RSTD_b,), (RSTD_b,))
                for g in range(4):
                    tc_ = g % 2
                    emit(DVE, lambda g=g, tc_=tc_: nc.vector.scalar_tensor_tensor(out=TMPC[tc_][0:R, :], in0=PB[ab + g][0:R, :],
                                                                                  scalar=rsc, in1=GB[0:R, g * 512:(g + 1) * 512],
                                                                                  op0=ALU.mult, op1=ALU.mult),
                         (PB_b[ab + g], RSTD_b, GB_b), (TMPC_b[tc_],))
                    emit(DVE, lambda g=g, tc_=tc_: nc.vector.tensor_tensor(XT[xt][0:R, g * 512:(g + 1) * 512], XT[xt][0:R, g * 512:(g + 1) * 512],
                                                                           TMPC[tc_][0:R, :], op=ALU.add),
                         (TMPC_b[tc_], XT_b[xt]), (XT_b[xt],))
                if last:
                    if tb == 0:
                        emit_dma(SP, lambda: nc.sync.dma_start(out=ys, in_=XT[xt][0:32, :]), (XT_b[xt],), ())
                    else:
                        emit_dma(SP, lambda: nc.sync.dma_start(out=yp[(tb - 1) * 128:tb * 128, :], in_=XT[xt][:, :]), (XT_b[xt],), ())
                else:
                    emit_dma(SP, lambda: nc.sync.dma_start(out=xscr[l % 2][r0:r0 + R, :], in_=XT[xt][0:R, :]),
                             (XT_b[xt],), (xscr_b[l % 2][tb],))

        for i in range(min(NSLOT, len(wq))):
            w_issue(i)
        e_load(0)
        stop = int(os.environ.get("MK_STOP", 9))
        KINT = int(os.environ.get("MK_KINT", 1))
        for l in range(nl):
            if stop >= 1:
                load_gates(l)
                halo(l, 0)
                phaseA(l)
            if stop >= 2:
                for u in range(nun):
                    gH = head(l, u)
                    gA = mixerA(l, u)
                    next(gH)
                    aliveA = True
                    cnt = 0
                    for _ in gH:
                        cnt += 1
                        if aliveA and cnt % KINT == 0:
                            for _k in range(2 if cnt <= 22 else 1):
                                try:
                                    next(gA)
                                except StopIteration:
                                    aliveA = False
                                    break
                    if aliveA:
                        for _ in gA:
                            pass
            if stop >= 4:
                phaseC(l)

        for e in (SP, POOL, ACT):
            for i, sem in enumerate(e.dsems):
                if e.dtot[i] > 0:
                    nc.sync.wait_ge(sem, e.dtot[i])
    return nc


_CACHE = {}


def _sel_matrix():
    s = np.zeros((32, NREL), np.float32)
    s[BUCKETS, np.arange(NREL)] = 1.0
    return s


def kernel(x_prompt, x_sample, cache_k, cache_v, state_conv, state_rglru, meta, rel_bias,
           pre_g, post_g, w_in, conv_w, conv_b, gate_r_w, gate_r_b, gate_i_w, gate_i_b,
           rglru_lam, lam_q1, lam_k1, lam_q2, lam_k2, subln_g, w_out):
    nl = int(os.environ.get("MK_LAYERS", DEPTH))
    ncores = int(os.environ.get("MK_CORES", NCORES))
    if nl not in _CACHE:
        _CACHE[nl] = build(nl)
    nc = _CACHE[nl]
    f = lambda a: np.ascontiguousarray(np.asarray(a, dtype=np.float32))
    shared = {
        "meta": f(meta), "rel_bias": f(rel_bias), "pre_g": f(pre_g), "post_g": f(post_g), "w_in": f(w_in),
        "conv_w": f(conv_w), "conv_b": f(conv_b), "gate_r_w": f(gate_r_w),
        "gate_r_b": f(gate_r_b).reshape(DEPTH, 1024), "gate_i_w": f(gate_i_w),
        "gate_i_b": f(gate_i_b).reshape(DEPTH, 1024), "rglru_lam": f(rglru_lam),
        "lam_q1": f(lam_q1), "lam_k1": f(lam_k1), "lam_q2": f(lam_q2), "lam_k2": f(lam_k2),
        "subln_g": f(subln_g), "w_out": f(w_out), "sel": _sel_matrix(),
    }
    in_maps = []
    for c in range(ncores):
        m = dict(shared)
        m["xp"] = f(x_prompt[c])
        m["xs"] = f(x_sample[c])
        m["ck"] = f(np.asarray(cache_k)[:, c].reshape(DEPTH, 2048, 1024))
        m["cv"] = f(np.asarray(cache_v)[:, c].reshape(DEPTH, 2048, 1024))
        m["sc"] = f(np.asarray(state_conv)[:, c])
        m["sr"] = f(np.asarray(state_rglru)[:, c])
        in_maps.append(m)
    res = run_bass_kernel_spmd(nc, in_maps, core_ids=list(range(ncores)))
    R = res.results
    st = lambda k, ax: np.stack([np.asarray(r[k]) for r in R], axis=ax)
    y_prompt = st("yp", 0)
    y_sample = st("ys", 0)
    k_prompt = st("kp", 1).reshape(DEPTH, ncores, 2064, 8, 128)
    v_prompt = st("vp", 1).reshape(DEPTH, ncores, 2064, 8, 128)
    conv_prompt = st("cp", 1)
    rglru_prompt = st("rp", 1)
    k_sample = st("ks", 1).reshape(DEPTH, ncores, 32, 8, 128)
    v_sample = st("vs", 1).reshape(DEPTH, ncores, 32, 8, 128)
    conv_sample = st("cs", 1)
    rglru_sample = st("rs", 1)
    return (y_prompt, y_sample, k_prompt, v_prompt, conv_prompt, rglru_prompt,
            k_sample, v_sample, conv_sample, rglru_sample)
```

```python
import os
import math
import bisect
import contextlib
import numpy as np
import concourse.bass as bass
import concourse.mybir as mybir
from concourse.bass_utils import run_bass_kernel_spmd

F32 = mybir.dt.float32
BF16 = mybir.dt.bfloat16
AF = mybir.ActivationFunctionType
ALU = mybir.AluOpType
AX = mybir.AxisListType

D = 2048
DEPTH = 4
NT = 2096
SPANS = [(0, 48), (48, 560), (560, 1072), (1072, 1584), (1584, 2096)]
NTB = 17
EPS = 1e-6
OFF = 2080
NREL = 2080 + 2064
NCORES = 8


def tb_rows(tb):
    return (0, 48) if tb == 0 else (48 + 128 * (tb - 1), 128)


def bucket_table():
    rel = np.arange(-OFF, NREL - OFF).astype(np.int64)
    half, max_exact = 16, 8
    ret = np.where(rel > 0, half, 0).astype(np.int32)
    n = np.abs(rel).astype(np.int32)
    nf = np.maximum(n, 1).astype(np.float32)
    lg = (np.log(nf / np.float32(max_exact)) / np.float32(math.log(1024 / max_exact))
          * np.float32(half - max_exact)).astype(np.float32)
    large = max_exact + lg.astype(np.int32)
    large = np.minimum(large, half - 1)
    return ret + np.where(n < max_exact, n, large)


BUCKETS = bucket_table()
_nb = np.nonzero(BUCKETS != 15)[0]
R15 = int(_nb[0]) - OFF - 1
assert -641 <= R15 < -513, R15


class Buf:
    __slots__ = ("w", "r")

    def __init__(self):
        self.w = {}
        self.r = {}


class Eng:
    def __init__(self, nc, es, h, name, nd=0, skip_self=False):
        self.h = h
        self.name = name
        self.sem = es.enter_context(nc.semaphore("s_" + name))
        self.seq = 0
        self.incs = []
        self.last = None
        self.seen = {}
        self.skip_self = skip_self
        self.eager = not skip_self
        self.dsems = [es.enter_context(nc.semaphore("d_%s%d" % (name, i))) for i in range(nd)]
        self.dtot = [0] * nd
        self.rr = 0


def _resolve(tok):
    if tok[0] == 'd':
        return tok[1], tok[2]
    e, seq = tok[1], tok[2]
    i = bisect.bisect_left(e.incs, seq)
    if i == len(e.incs):
        e.last.then_inc(e.sem, 1)
        e.incs.append(e.seq)
    return e.sem, i + 1


def _tmax(d, key, tok):
    o = d.get(key)
    if o is None or o[2] < tok[2]:
        d[key] = tok


def _wait_for(eng, toks):
    waits = {}
    for t in toks:
        if t[0] == 'c' and t[1] is eng and eng.skip_self:
            continue
        sem, val = _resolve(t)
        k = id(sem)
        if k not in waits or waits[k][1] < val:
            waits[k] = (sem, val)
    for k, (sem, val) in waits.items():
        if eng.seen.get(k, 0) < val:
            eng.h.wait_ge(sem, val)
            eng.seen[k] = val


def _flat(bs):
    out = []
    for b in bs:
        if isinstance(b, (tuple, list)):
            out.extend(_flat(b))
        else:
            out.append(b)
    return out


def _deps(reads, writes):
    toks = []
    for b in reads:
        toks.extend(b.w.values())
    for b in writes:
        toks.extend(b.w.values())
        toks.extend(b.r.values())
    return toks


def emit(eng, fn, reads=(), writes=(), mark=False):
    reads, writes = _flat(reads), _flat(writes)
    _wait_for(eng, _deps(reads, writes))
    ins = fn()
    eng.seq += 1
    eng.last = ins
    if eng.eager or mark:
        ins.then_inc(eng.sem, 1)
        eng.incs.append(eng.seq)
    tok = ('c', eng, eng.seq)
    for b in reads:
        _tmax(b.r, id(eng), tok)
    for b in writes:
        _tmax(b.w, id(eng), tok)
    return ins


def emit_dma(eng, fn, reads=(), writes=()):
    reads, writes = _flat(reads), _flat(writes)
    i = eng.rr
    eng.rr = (i + 1) % len(eng.dsems)
    sem = eng.dsems[i]
    _wait_for(eng, _deps(reads, writes))
    if eng.seen.get(id(sem), 0) < eng.dtot[i]:
        eng.h.wait_ge(sem, eng.dtot[i])
        eng.seen[id(sem)] = eng.dtot[i]
    ins = fn()
    eng.dtot[i] += 16
    ins.then_inc(sem, 16)
    tok = ('d', sem, eng.dtot[i])
    for b in reads:
        _tmax(b.r, id(sem), tok)
    for b in writes:
        _tmax(b.w, id(sem), tok)
    return ins


def build(nl):
    nc = bass.Bass("TRN2", target_bir_lowering=False)
    es = contextlib.ExitStack()

    def din(name, shape):
        return nc.dram_tensor(name, list(shape), F32, kind="ExternalInput")

    def dout(name, shape):
        return nc.dram_tensor(name, list(shape), F32, kind="ExternalOutput")

    xp = din("xp", [2048, D]).ap()
    xs = din("xs", [32, D]).ap()
    ck = din("ck", [DEPTH, 2048, 1024]).ap()
    cv = din("cv", [DEPTH, 2048, 1024]).ap()
    scv = din("sc", [DEPTH, 3, 1024]).ap()
    srg = din("sr", [DEPTH, 1024]).ap()
    meta = din("meta", [16, D]).ap()
    relb = din("rel_bias", [32, 8]).ap()
    pre_g = din("pre_g", [DEPTH, D]).ap()
    post_g = din("post_g", [DEPTH, D]).ap()
    w_in = din("w_in", [DEPTH, D, 6144]).ap()
    conv_w = din("conv_w", [DEPTH, 4, 1024]).ap()
    conv_b = din("conv_b", [DEPTH, 1024]).ap()
    grw = din("gate_r_w", [DEPTH, 16, 64, 64]).ap()
    grb = din("gate_r_b", [DEPTH, 1024]).ap()
    giw = din("gate_i_w", [DEPTH, 16, 64, 64]).ap()
    gib = din("gate_i_b", [DEPTH, 1024]).ap()
    rlam = din("rglru_lam", [DEPTH, 1024]).ap()
    lq1 = din("lam_q1", [DEPTH, 64]).ap()
    lk1 = din("lam_k1", [DEPTH, 64]).ap()
    lq2 = din("lam_q2", [DEPTH, 64]).ap()
    lk2 = din("lam_k2", [DEPTH, 64]).ap()
    subg = din("subln_g", [DEPTH, 128]).ap()
    w_out = din("w_out", [DEPTH, D, D]).ap()
    sel = din("sel", [32, NREL]).ap()

    yp = dout("yp", [2048, D]).ap()
    ys = dout("ys", [32, D]).ap()
    kp = dout("kp", [DEPTH, 2064, 1024]).ap()
    vp = dout("vp", [DEPTH, 2064, 1024]).ap()
    cpo = dout("cp", [DEPTH, 3, 1024]).ap()
    rpo = dout("rp", [DEPTH, 1024]).ap()
    kso = dout("ks", [DEPTH, 32, 1024]).ap()
    vso = dout("vs", [DEPTH, 32, 1024]).ap()
    cso = dout("cs", [DEPTH, 3, 1024]).ap()
    rso = dout("rs", [DEPTH, 1024]).ap()

    xscr = [nc.dram_tensor("xscr%d" % i, [NT, D], F32, kind="Internal").ap() for i in range(2)]
    ymscr = nc.dram_tensor("ymscr", [16, 128, NT], BF16, kind="Internal").ap()
    evscr_t = nc.dram_tensor("evscr", [8, NREL], BF16, kind="Internal")
    evscr = evscr_t.ap()

    with es:
        def sb(name, shape, dt):
            return es.enter_context(nc.sbuf_tensor(name, list(shape), dt))

        PE = Eng(nc, es, nc.tensor, "pe", skip_self=True)
        ACT = Eng(nc, es, nc.scalar, "act", nd=6)
        DVE = Eng(nc, es, nc.vector, "dve")
        POOL = Eng(nc, es, nc.gpsimd, "pool", nd=8)
        SP = Eng(nc, es, nc.sync, "sp", nd=12)

        R1 = sb("R1", [128, 16 * NT], BF16)
        uT = R1[:].rearrange("p (c n) -> p c n", c=16)
        WO = R1[:, 0:16 * 2048].rearrange("p (c n) -> p c n", c=16)
        uT_b = [Buf() for _ in SPANS]
        NSLOT = 6
        WS = sb("WS", [128, NSLOT, 16, 128], BF16)
        WS_b = [Buf() for _ in range(NSLOT)]
        XT = [sb("XT%d" % i, [128, D], F32) for i in range(2)]
        XT_b = [(Buf(), Buf()), Buf()]
        YM = [sb("YM%d" % i, [128, NT], BF16) for i in range(2)]
        YM_b = [Buf(), Buf()]
        TK = sb("TK", [128, 1024], F32)
        TMPC = [TK[:, 0:512], TK[:, 512:1024]]
        TMPC_b = [Buf(), Buf()]
        QK = sb("QK", [128, 2 * NT], BF16)
        QT = QK[:, 0:NT]
        QT_b = Buf()
        KT = QK[:, NT:2 * NT]
        KT_b = Buf()
        YMB = [QK[:, i * 2048:(i + 1) * 2048].rearrange("p (c n) -> p c n", c=16) for i in range(2)]
        YMB_b = [(Buf(),), (Buf(),)]
        VH = sb("VH", [128, NTB, 256], BF16)
        VH_b = Buf()
        VM = sb("VM", [16, 128], BF16)
        VM_b = Buf()
        SGB = sb("SGB", [128, NT], F32)
        SGB_b = Buf()
        GB = SGB[:, 0:D]
        GB_b = SGB_b
        KCT = TK[:].bitcast(BF16).rearrange("p (b d) -> p b d", b=16)
        KCT_b = TMPC_b
        KTC = sb("KTC", [128, 2048], BF16)
        KTC_b = Buf()
        VC = sb("VC", [128, 16, 128], BF16)
        VC_b = Buf()
        UB = [VC[:].rearrange("p b d -> p (b d)"), KTC[:, :]]
        UB_b = [VC_b, KTC_b]
        KVS = [sb("KVS%d" % i, [128, 256], F32) for i in range(2)]
        KVS_b = [Buf(), Buf()]
        NE = 7
        ET2 = [sb("ET%d" % i, [128, NE, 256], BF16) for i in range(2)]
        ET_b2 = [[Buf() for _ in range(NE)] for i in range(2)]
        EM2 = [sb("EM%d" % i, [16, 3, 256], BF16) for i in range(2)]
        EM_b2 = [Buf(), Buf()]
        EMM2 = [sb("EMM%d" % i, [16, 16], BF16) for i in range(2)]
        ES2 = [sb("ES%d" % i, [128, 6, 32], BF16) for i in range(2)]
        ES_b2 = [Buf(), Buf()]
        NPT = 4
        PT = [sb("PT%d" % i, [128, 2, 256], BF16) for i in range(NPT)]
        PT_b = [Buf() for _ in range(NPT)]
        NTMP = 10
        TMall = sb("TMall", [128, NTMP, 512], F32)
        TM = [TMall[:, i, :] for i in range(NTMP)]
        TM_b = [Buf() for _ in range(NTMP)]
        XA = sb("XA", [128, NT + 6], F32)
        XA_b = Buf()
        HH = [sb("HH%d" % i, [128, 512], F32) for i in range(2)]
        HH_b = [Buf(), Buf()]
        GW = sb("GW", [128, 16, 128], BF16)
        GW_b = Buf()
        ident = sb("ident", [128, 128], BF16)
        identf = sb("identf", [128, 128], F32)
        ones_b = sb("ones_b", [128, 128], BF16)
        ones_f = sb("ones_f", [128, 128], F32)
        CONST_b = Buf()
        EPSC = sb("EPSC", [128, 1], F32)
        ONEC = sb("ONEC", [128, 1], F32)
        CW = sb("CW", [128, DEPTH * 8, 4], F32)
        CB = sb("CB", [128, DEPTH * 8], F32)
        BR = sb("BR", [128, DEPTH * 8], F32)
        BI = sb("BI", [128, DEPTH * 8], F32)
        CA = sb("CA", [128, DEPTH * 8], F32)
        CA2 = sb("CA2", [128, DEPTH * 8], F32)
        SR0 = sb("SR0", [128, DEPTH * 8], F32)
        SG = sb("SG", [128, DEPTH], F32)
        NLAM = sb("NLAM", [128, DEPTH], F32)
        LT = XT[0][:, 0:4 * DEPTH * 64].rearrange("p (i k) -> p i k", i=4)
        LT2 = sb("LT2", [128, 2, DEPTH], F32)
        F15 = sb("F15", [128, 8], F32)
        RB = sb("RB", [32, 8], F32)
        SS = sb("SS", [128, 16], F32)
        SS_bb = [Buf(), Buf()]
        RSTD = sb("RSTD", [128, 4], F32)
        RSTD_bb = [Buf(), Buf()]
        VEC_b = Buf()

        PS = es.enter_context(nc.psum_tensor("PS", [128, 8, 512], F32))
        PB = [PS[:, i, :] for i in range(8)]
        PB_b = [Buf() for _ in range(8)]
        mmrr = [0]

        def mmbank():
            i = mmrr[0]
            mmrr[0] = (i + 1) % 4
            return i

        PAIRS = [(0, 1), (2, 3)]
        live = [None]
        fprr = [0]

        def free_pair():
            return (6, 7)

        slotc = [0]

        def next_slot():
            v = slotc[0]
            slotc[0] += 1
            return (v // 2) % 2, v % 2

        def bc2(a2):
            return bass.AP(a2.tensor, a2.offset, [list(a2.ap[0]), [0, 2], list(a2.ap[1])])

        tmrr = [0]

        def tmp():
            i = tmrr[0]
            tmrr[0] = (i + 1) % NTMP
            return i

        xscr_b = [[Buf() for _ in range(NTB)] for _ in range(2)]
        ymscr_b = [Buf() for _ in range(16)]
        ev_b = Buf()

        evac_rr = [0]

        def evac_eng():
            evac_rr[0] ^= 1
            return ACT if evac_rr[0] else DVE

        def copy(eng, out, in_, reads, writes):
            if eng is ACT:
                return emit(ACT, lambda: nc.scalar.copy(out, in_), reads, writes)
            return emit(eng, lambda: eng.h.tensor_copy(out, in_), reads, writes)

        emit(POOL, lambda: nc.gpsimd.memset(identf[:], 1.0), (), (CONST_b,))
        emit(POOL, lambda: nc.gpsimd.affine_select(out=identf[:], in_=identf[:], pattern=[[-1, 128]],
                                                   compare_op=ALU.is_equal, fill=0.0, base=0,
                                                   channel_multiplier=1), (), (CONST_b,))
        emit(POOL, lambda: nc.gpsimd.memset(ones_f[:], 1.0), (), (CONST_b,))
        emit(POOL, lambda: nc.gpsimd.memset(EPSC[:], EPS), (), (CONST_b,))
        emit(POOL, lambda: nc.gpsimd.memset(ONEC[:], 1.0), (), (CONST_b,))
        emit(DVE, lambda: nc.vector.tensor_copy(ident[:], identf[:]), (CONST_b,), (CONST_b,))
        emit(DVE, lambda: nc.vector.tensor_copy(ones_b[:], ones_f[:]), (CONST_b,), (CONST_b,))
        emit(DVE, lambda: nc.vector.memset(XA[:], 0.0), (), (XA_b,))

        def small(dst, src):
            emit_dma(SP, lambda: nc.sync.dma_start(out=dst, in_=src, allow_slow_non_contiguous=True),
                     (), (VEC_b,))

        for l in range(nl):
            for k in range(4):
                small(CW[:, l * 8:(l + 1) * 8, k], conv_w[l][k].rearrange("(c p) -> p c", p=128))
        small(CB[:, 0:nl * 8], conv_b[0:nl].rearrange("l (c p) -> p (l c)", p=128))
        small(BI[:, 0:nl * 8], gib[0:nl].rearrange("l (c p) -> p (l c)", p=128))
        small(CA[:, 0:nl * 8], rlam[0:nl].rearrange("l (c p) -> p (l c)", p=128))
        small(SR0[:, 0:nl * 8], srg[0:nl].rearrange("l (c p) -> p (l c)", p=128))
        small(SG[:, :], subg.rearrange("l p -> p l"))
        for i, t in enumerate((lq1, lk1, lq2, lk2)):
            emit_dma(SP, lambda i=i, t=t: nc.sync.dma_start(out=LT[:, i, :], in_=t.rearrange("l k -> (l k)").partition_broadcast(128)), (), (XT_b[0], VEC_b))
        small(F15[:, :], relb[15:16, :].rearrange("o h -> (o h)").partition_broadcast(128))
        small(RB[:, :], relb)

        emit(ACT, lambda: nc.scalar.activation(out=CA[:], in_=CA[:], func=AF.Exp, scale=-1.0), (VEC_b,), (VEC_b,))
        V = (VEC_b,)
        emit(DVE, lambda: nc.vector.tensor_scalar(CA2[:], CA[:], 2.0, None, op0=ALU.add), V, V)
        emit(DVE, lambda: nc.vector.reciprocal(CA2[:], CA2[:]), V, V)
        emit(DVE, lambda: nc.vector.tensor_tensor(CA[:], CA[:], CA2[:], op=ALU.mult), V, V)
        emit(DVE, lambda: nc.vector.tensor_tensor(CA2[:], CA[:], CA[:], op=ALU.mult), V, V)
        emit(DVE, lambda: nc.vector.memset(BR[:], 1.0 / 15), V, V)
        for cf in (1.0 / 13, 1.0 / 11, 1.0 / 9, 1.0 / 7, 1.0 / 5, 1.0 / 3, 1.0):
            emit(DVE, lambda: nc.vector.tensor_tensor(BR[:], BR[:], CA2[:], op=ALU.mult), V, V)
            emit(DVE, lambda cf=cf: nc.vector.tensor_scalar(BR[:], BR[:], cf, None, op0=ALU.add), V, V)
        emit(DVE, lambda: nc.vector.tensor_tensor(CA[:], CA[:], BR[:], op=ALU.mult), V, V)
        emit(DVE, lambda: nc.vector.tensor_scalar(CA[:], CA[:], 2.0, None, op0=ALU.mult), V, V)
        emit(DVE, lambda: nc.vector.tensor_scalar(CA2[:], CA[:], -16.0, None, op0=ALU.mult), (VEC_b,), (VEC_b,))
        emit(DVE, lambda: nc.vector.tensor_scalar(CA[:], CA[:], -8.0, None, op0=ALU.mult), (VEC_b,), (VEC_b,))
        small(BR[:, 0:nl * 8], grb[0:nl].rearrange("l (c p) -> p (l c)", p=128))
        emit(DVE, lambda: nc.vector.tensor_scalar(BR[:], BR[:], -1.0, None, op0=ALU.mult), (VEC_b,), (VEC_b,))
        emit(DVE, lambda: nc.vector.tensor_scalar(BI[:], BI[:], -1.0, None, op0=ALU.mult), (VEC_b,), (VEC_b,))
        emit(DVE, lambda: nc.vector.tensor_tensor(LT[:, 0, :], LT[:, 0, :], LT[:, 1, :], op=ALU.mult), (VEC_b, XT_b[0]), (VEC_b, XT_b[0]))
        emit(DVE, lambda: nc.vector.tensor_tensor(LT[:, 2, :], LT[:, 2, :], LT[:, 3, :], op=ALU.mult), (VEC_b, XT_b[0]), (VEC_b, XT_b[0]))
        emit(DVE, lambda: nc.vector.tensor_reduce(out=LT2[:, 0, :], in_=LT[:, 0, :].rearrange("p (l k) -> p l k", k=64),
                                                  axis=AX.X, op=ALU.add), (VEC_b, XT_b[0]), (VEC_b,))
        emit(DVE, lambda: nc.vector.tensor_reduce(out=LT2[:, 1, :], in_=LT[:, 2, :].rearrange("p (l k) -> p l k", k=64),
                                                  axis=AX.X, op=ALU.add), (VEC_b, XT_b[0]), (VEC_b,))
        emit(ACT, lambda: nc.scalar.activation(out=LT2[:], in_=LT2[:], func=AF.Exp), (VEC_b,), (VEC_b,))
        emit(DVE, lambda: nc.vector.tensor_tensor(NLAM[:], LT2[:, 1, :], LT2[:, 0, :], op=ALU.subtract), (VEC_b,), (VEC_b,))
        for l in range(DEPTH):
            li = 0.8 - 0.6 * math.exp(-0.3 * l)
            emit(DVE, lambda l=l, li=li: nc.vector.tensor_scalar(NLAM[:, l:l + 1], NLAM[:, l:l + 1], -li, None, op0=ALU.add),
                 (VEC_b,), (VEC_b,))
            emit(DVE, lambda l=l, li=li: nc.vector.tensor_scalar(SG[:, l:l + 1], SG[:, l:l + 1], 1.0 - li, None, op0=ALU.mult),
                 (VEC_b,), (VEC_b,))
        for ci, c0 in enumerate(range(0, NREL, 512)):
            c1 = min(NREL, c0 + 512)
            b = mmbank()
            ts_ = tmp()
            pp = ci % 2
            emit_dma(SP, lambda: nc.sync.dma_start(out=TM[ts_][0:32, 0:c1 - c0], in_=sel[:, c0:c1]), (), (TM_b[ts_],))
            emit(PE, lambda: nc.tensor.matmul(PB[b][0:8, 0:c1 - c0], lhsT=RB[:, :], rhs=TM[ts_][0:32, 0:c1 - c0],
                                              start=True, stop=True), (VEC_b, TM_b[ts_]), (PB_b[b],), mark=True)
            stg = PT[pp][:].rearrange("p m c -> p (m c)")
            emit(ACT, lambda: nc.scalar.activation(out=stg[0:8, 0:c1 - c0], in_=PB[b][0:8, 0:c1 - c0], func=AF.Exp),
                 (PB_b[b],), (PT_b[pp],))
            emit_dma(SP, lambda: nc.sync.dma_start(out=evscr[:, c0:c1], in_=stg[0:8, 0:c1 - c0]), (PT_b[pp],), (ev_b,))

        def hankel(h, base, npart, ncol):
            return bass.AP(evscr_t, h * NREL + base, [[1, npart], [1, ncol]])

        wq = []
        nun = int(os.environ.get("MK_UNITS", 8))
        WPOS = [{"k": 0, "v": 1, "x": 2, "g": 3, "q": 4, "gb": 5}, {"k": 0, "v": 1, "q": 2, "gb": 3, "x": 4, "g": 5}]
        for l in range(nl):
            for u in range(nun):
                pos = WPOS[(l * nun + u) % 2]
                blk = {"k": 24 + u, "v": 32 + u, "q": 16 + u, "gb": 40 + u, "x": u, "g": 8 + u}
                for role in sorted(pos, key=lambda r: pos[r]):
                    wq.append((l, blk[role]))
        wissued = [False] * len(wq)
        wdone = [False] * len(wq)

        def w_issue(i):
            l, c = wq[i]
            s = i % NSLOT
            emit_dma(POOL, lambda: nc.gpsimd.dma_start(
                out=WS[:, s, :, :], in_=w_in[l][:, c * 128:(c + 1) * 128].rearrange("(c p) n -> p c n", p=128)),
                (), (WS_b[s],))
            wissued[i] = True

        def w_get(l, u, roles):
            pair = l * nun + u
            idx = [NSLOT * pair + WPOS[pair % 2][r] for r in roles]
            assert all(wissued[i] for i in idx), (l, u, roles)
            return [i % NSLOT for i in idx], idx

        def w_release(idxs):
            for i in idxs:
                wdone[i] = True
            lo = 0
            while lo < len(wq) and wissued[lo]:
                lo += 1
            for i in range(lo, min(len(wq), lo + 2 * NSLOT)):
                if not wissued[i] and (i < NSLOT or wdone[i - NSLOT]):
                    w_issue(i)

        def spans_of(c0, c1):
            return [i for i, (a, b) in enumerate(SPANS) if a < c1 and b > c0]

        def proj_fm(slot, c0, c1, b=None):
            if b is None:
                b = mmbank()
            rb = [WS_b[slot]] + [uT_b[i] for i in spans_of(c0, c1)]
            for dc in range(16):
                emit(PE, lambda dc=dc: nc.tensor.matmul(PB[b][:, 0:c1 - c0], lhsT=WS[:, slot, dc, :], rhs=uT[:, dc, c0:c1],
                                                        start=(dc == 0), stop=(dc == 15)), rb, (PB_b[b],), mark=(dc == 15))
            return b

        def xa_idx(c):
            return c + 3 if c < 32 else c + 6

        def load_x(l, tb, xt):
            r0, R = tb_rows(tb)
            if l == 0:
                if tb == 0:
                    emit_dma(SP, lambda: nc.sync.dma_start(out=XT[xt][0:32, :], in_=xs), (), (XT_b[xt],))
                    emit_dma(SP, lambda: nc.sync.dma_start(out=XT[xt][32:48, :], in_=meta), (), (XT_b[xt],))
                else:
                    emit_dma(SP, lambda: nc.sync.dma_start(out=XT[xt][:, :], in_=xp[(tb - 1) * 128:tb * 128, :]),
                             (), (XT_b[xt],))
            else:
                src = xscr[(l - 1) % 2]
                emit_dma(SP, lambda: nc.sync.dma_start(out=XT[xt][0:R, :], in_=src[r0:r0 + R, :]),
                         (xscr_b[(l - 1) % 2][tb],), (XT_b[xt],))

        def phaseA(l):
            emit_dma(SP, lambda: nc.sync.dma_start(out=GB[:], in_=pre_g[l].partition_broadcast(128)), (), (GB_b,))

            def s1(tb):
                r0, R = tb_rows(tb)
                xt = tb % 2
                SS_b, RSTD_b = SS_bb[xt], RSTD_bb[xt]
                ssc = SS[0:R, 8 * xt:8 * xt + 1]
                rsc = RSTD[0:R, 2 * xt:2 * xt + 1]
                load_x(l, tb, xt)
                emit(ACT, lambda: nc.scalar.activation(out=UB[xt][0:R, :], in_=XT[xt][0:R, :], func=AF.Square,
                                                       accum_out=ssc), (XT_b[xt],), (UB_b[xt], SS_b))
                emit(ACT, lambda: nc.scalar.activation(out=rsc, in_=ssc, func=AF.Ln, scale=1.0 / D, bias=EPSC[0:R, 0:1]),
                     (SS_b, CONST_b), (RSTD_b,))
                emit(ACT, lambda: nc.scalar.activation(out=rsc, in_=rsc, func=AF.Exp, scale=-0.5),
                     (RSTD_b,), (RSTD_b,))
                emit(DVE, lambda: nc.vector.scalar_tensor_tensor(out=UB[xt][0:R, :], in0=XT[xt][0:R, :], scalar=rsc,
                                                                 in1=GB[0:R, :], op0=ALU.mult, op1=ALU.mult),
                     (XT_b[xt], RSTD_b, GB_b), (UB_b[xt],))

            def s2(tb):
                r0, R = tb_rows(tb)
                xt = tb % 2
                for g in range(4):
                    slot = g % 2
                    pv = PB[4 + slot][:].bitcast(BF16)
                    for i in range(4):
                        dc = 4 * g + i
                        emit(PE, lambda i=i, dc=dc: nc.tensor.transpose(pv[:, i * 128:i * 128 + R], UB[xt][0:R, dc * 128:(dc + 1) * 128],
                                                                        ident[0:R, 0:R]),
                             (UB_b[xt], CONST_b), (PB_b[4 + slot],), mark=(i == 3))
                    src = pv[:, 0:512].rearrange("p (i n) -> p i n", i=4)[:, :, 0:R]
                    dst = uT[:, 4 * g:4 * g + 4, r0:r0 + R]
                    copy(evac_eng(), dst, src, (PB_b[4 + slot],), [uT_b[i] for i in spans_of(r0, r0 + R)])

            s1(0)
            for tb in range(NTB):
                if tb + 1 < NTB:
                    s1(tb + 1)
                s2(tb)

        def halo(l, j):
            emit_dma(SP, lambda: nc.sync.dma_start(out=XA[:, 0:3], in_=scv[l][:, j * 128:(j + 1) * 128].rearrange("k p -> p k"),
                                                   allow_slow_non_contiguous=True), (), (XA_b,))

        def mixerA(l, j):
            (sx, sg), widx = w_get(l, j, ("x", "g"))
            lj = l * 8 + j
            ym = 0
            stage = {}

            def front(si):
                c0, c1 = SPANS[si]
                W = c1 - c0
                bx, bg = free_pair()
                proj_fm(sx, c0, c1, bx)
                proj_fm(sg, c0, c1, bg)
                segs = [(0, 32), (32, 48)] if si == 0 else [(c0, c1)]
                for (a, b) in segs:
                    copy(DVE, XA[:, xa_idx(a):xa_idx(a) + (b - a)], PB[bx][:, a - c0:b - c0], (PB_b[bx],), (XA_b,))
                tg, txc, txb = [3 * (si % 2) + k for k in range(3)]
                emit(ACT, lambda: nc.scalar.activation(out=TM[tg][:, 0:W], in_=PB[bg][:, 0:W], func=AF.Exp, scale=-1.0),
                     (PB_b[bg],), (TM_b[tg],))
                yield
                emit(ACT, lambda: nc.scalar.activation(out=TM[tg][:, 0:W], in_=TM[tg][:, 0:W], func=AF.Ln, bias=ONEC[:, 0:1]),
                     (TM_b[tg], CONST_b), (TM_b[tg],))
                yield
                emit(ACT, lambda: nc.scalar.activation(out=TM[tg][:, 0:W], in_=TM[tg][:, 0:W], func=AF.Exp, scale=-1.0),
                     (TM_b[tg],), (TM_b[tg],))
                yield
                emit(DVE, lambda: nc.vector.tensor_tensor(TM[tg][:, 0:W], PB[bg][:, 0:W], TM[tg][:, 0:W], op=ALU.mult),
                     (PB_b[bg], TM_b[tg]), (TM_b[tg],))
                for (a, b) in segs:
                    i0 = xa_idx(a)
                    n = b - a
                    o = TM[txc][:, a - c0:b - c0]
                    emit(DVE, lambda: nc.vector.tensor_scalar(o, XA[:, i0 - 3:i0 - 3 + n], CW[:, lj, 0:1], CB[:, lj:lj + 1],
                                                              op0=ALU.mult, op1=ALU.add), (XA_b, VEC_b), (TM_b[txc],))
                    for k in (1, 2, 3):
                        emit(DVE, lambda k=k: nc.vector.scalar_tensor_tensor(out=o, in0=XA[:, i0 - 3 + k:i0 - 3 + k + n],
                                                                             scalar=CW[:, lj, k:k + 1], in1=o,
                                                                             op0=ALU.mult, op1=ALU.add),
                             (XA_b, VEC_b, TM_b[txc]), (TM_b[txc],))
                xcb = TM[txb][:].bitcast(BF16)
                yield
                emit(ACT, lambda: nc.scalar.copy(xcb[:, 0:W], TM[txc][:, 0:W]), (TM_b[txc],), (TM_b[txb],))
                stage[si] = (tg, txc, txb, xcb, segs, W, c0)

            def back(si):
                tg, txc, txb, xcb, segs, W, c0 = stage.pop(si)
                br_, bi_ = free_pair()
                emit(PE, lambda: nc.tensor.matmul(PB[br_][:, 0:W], lhsT=GW[:, j, :], rhs=xcb[:, 0:W], start=True, stop=True),
                     (GW_b, TM_b[txb]), (PB_b[br_],), mark=True)
                emit(PE, lambda: nc.tensor.matmul(PB[bi_][:, 0:W], lhsT=GW[:, 8 + j, :], rhs=xcb[:, 0:W], start=True, stop=True),
                     (GW_b, TM_b[txb]), (PB_b[bi_],), mark=True)
                tr_, ti_, ta_, ts_ = 6, 7, 8, 9
                yield
                emit(ACT, lambda: nc.scalar.activation(out=TM[tr_][:, 0:W], in_=PB[br_][:, 0:W], func=AF.Exp, scale=-1.0,
                                                       bias=BR[:, lj:lj + 1]), (PB_b[br_], VEC_b), (TM_b[tr_],))
                yield
                emit(ACT, lambda: nc.scalar.activation(out=TM[ti_][:, 0:W], in_=PB[bi_][:, 0:W], func=AF.Exp, scale=-1.0,
                                                       bias=BI[:, lj:lj + 1]), (PB_b[bi_], VEC_b), (TM_b[ti_],))
                yield
                ri = TMall[:, tr_:tr_ + 2, 0:W]
                emit(ACT, lambda: nc.scalar.activation(out=ri, in_=ri, func=AF.Ln, bias=ONEC[:, 0:1]),
                     (TM_b[tr_], TM_b[ti_], CONST_b), (TM_b[tr_], TM_b[ti_]))
                yield
                emit(ACT, lambda: nc.scalar.activation(out=ri, in_=ri, func=AF.Exp, scale=-1.0),
                     (TM_b[tr_], TM_b[ti_]), (TM_b[tr_], TM_b[ti_]))
                yield
                emit(ACT, lambda: nc.scalar.activation(out=TM[ta_][:, 0:W], in_=TM[tr_][:, 0:W], func=AF.Exp,
                                                       scale=CA[:, lj:lj + 1]), (TM_b[tr_], VEC_b), (TM_b[ta_],))
                yield
                emit(ACT, lambda: nc.scalar.activation(out=TM[ts_][:, 0:W], in_=TM[tr_][:, 0:W], func=AF.Exp,
                                                       scale=CA2[:, lj:lj + 1]), (TM_b[tr_], VEC_b), (TM_b[ts_],))
                emit(DVE, lambda: nc.vector.tensor_scalar(TM[ts_][:, 0:W], TM[ts_][:, 0:W], 0.99999994, None, op0=ALU.min),
                     (TM_b[ts_],), (TM_b[ts_],))
                yield
                emit(ACT, lambda: nc.scalar.activation(out=TM[ts_][:, 0:W], in_=TM[ts_][:, 0:W], func=AF.Ln, scale=-1.0, bias=ONEC[:, 0:1]),
                     (TM_b[ts_], CONST_b), (TM_b[ts_],))
                yield
                emit(ACT, lambda: nc.scalar.activation(out=TM[ts_][:, 0:W], in_=TM[ts_][:, 0:W], func=AF.Exp, scale=0.5),
                     (TM_b[ts_],), (TM_b[ts_],))
                yield
                emit(DVE, lambda: nc.vector.tensor_tensor(TM[ti_][:, 0:W], TM[ti_][:, 0:W], TM[txc][:, 0:W], op=ALU.mult),
                     (TM_b[ti_], TM_b[txc]), (TM_b[ti_],))
                emit(DVE, lambda: nc.vector.tensor_tensor(TM[ti_][:, 0:W], TM[ti_][:, 0:W], TM[ts_][:, 0:W], op=ALU.mult),
                     (TM_b[ti_], TM_b[ts_]), (TM_b[ti_],))
                hb = si % 2
                for (a, b) in segs:
                    n = b - a
                    lo = a - c0
                    if si == 0 and a == 0:
                        init = SR0[:, lj:lj + 1]
                        ir = (VEC_b,)
                    elif si == 0:
                        init = 0.0
                        ir = ()
                    else:
                        pw = SPANS[si - 1][1] - SPANS[si - 1][0]
                        init = HH[1 - hb][:, pw - 1:pw]
                        ir = (HH_b[1 - hb],)
                    emit(DVE, lambda: nc.vector.tensor_tensor_scan(HH[hb][:, lo:lo + n], TM[ta_][:, lo:lo + n], TM[ti_][:, lo:lo + n],
                                                                   init, op0=ALU.mult, op1=ALU.add),
                         (TM_b[ta_], TM_b[ti_]) + tuple(ir), (HH_b[hb],))
                emit(DVE, lambda: nc.vector.tensor_tensor(YM[ym][:, c0:c0 + W], HH[hb][:, 0:W], TM[tg][:, 0:W], op=ALU.mult),
                     (HH_b[hb], TM_b[tg]), (YM_b[ym],))
                if si == 0:
                    emit_dma(SP, lambda: nc.sync.dma_start(out=rso[l:l + 1, j * 128:(j + 1) * 128].rearrange("o p -> p o"),
                                                           in_=HH[hb][:, 31:32], allow_slow_non_contiguous=True),
                             (HH_b[hb],), ())
                if si == 4:
                    emit_dma(SP, lambda: nc.sync.dma_start(out=rpo[l:l + 1, j * 128:(j + 1) * 128].rearrange("o p -> p o"),
                                                           in_=HH[hb][:, W - 1:W], allow_slow_non_contiguous=True),
                             (HH_b[hb],), ())

            for si in range(6):
                if si < 5:
                    yield from front(si)
                    yield
                if si >= 1:
                    yield from back(si - 1)
                    yield
            w_release(widx)
            emit_dma(SP, lambda: nc.sync.dma_start(out=cso[l][:, j * 128:(j + 1) * 128].rearrange("k p -> p k"),
                                                   in_=XA[:, 32:35], allow_slow_non_contiguous=True), (XA_b,), ())
            emit_dma(SP, lambda: nc.sync.dma_start(out=cpo[l][:, j * 128:(j + 1) * 128].rearrange("k p -> p k"),
                                                   in_=XA[:, NT + 3:NT + 6], allow_slow_non_contiguous=True), (XA_b,), ())
            emit_dma(SP, lambda: nc.sync.dma_start(out=ymscr[j], in_=YM[ym][:, :]), (YM_b[ym],), (ymscr_b[j],))
            if j + 1 < nun:
                halo(l, j + 1)

        def attend(l, h, q0, q1, blocks, ym, pend):
            W = q1 - q0
            nb = len(blocks)
            sl = {}
            LA = 3

            def scores(bi):
                bk = blocks[bi]
                nk, cs = bk["nk"], bk["cs"]
                p, hf = (bi // 2) % 2, bi % 2
                c0 = hf * 256
                for m in (0, 1):
                    emit(PE, lambda m=m: nc.tensor.matmul(PB[2 * p + m][0:nk, c0 + cs:c0 + W], lhsT=bk["kt"][64 * m:64 * m + 64, :],
                                                          rhs=QT[64 * m:64 * m + 64, q0 + cs:q1], start=True, stop=True),
                         tuple(bk["rd"]) + (QT_b,), (PB_b[2 * p + m],), mark=(m == 1))
                sl[bi] = (p, c0)

            def probs(bi):
                bk = blocks[bi]
                nk, cs = bk["nk"], bk["cs"]
                p, c0 = sl.pop(bi)
                pi = bi % NPT
                src = PS[0:nk, 2 * p:2 * p + 2, c0 + cs:c0 + W]
                dst = PT[pi][0:nk, :, cs:W]
                rb = (PB_b[2 * p], PB_b[2 * p + 1])
                if bk["e"][0] == 'c':
                    emit(ACT, lambda: nc.scalar.activation(out=dst, in_=src, func=AF.Exp, scale=0.125, bias=F15[0:nk, h:h + 1]),
                         rb + (VEC_b,), (PT_b[pi],))
                else:
                    emit(ACT, lambda: nc.scalar.activation(out=dst, in_=src, func=AF.Exp, scale=0.125), rb, (PT_b[pi],))
                    emit(DVE, lambda: nc.vector.tensor_tensor(dst, dst, bc2(bk["e"][1]), op=ALU.mult),
                         (PT_b[pi],) + tuple(bk["erd"]), (PT_b[pi],))

            def pv(bi):
                bk = blocks[bi]
                nk, cs = bk["nk"], bk["cs"]
                pi = bi % NPT
                st, sp = (bi == 0), (bi == nb - 1)
                rhs = PT[pi][0:nk, :, cs:W]
                o4 = PB[4].rearrange("p (m c) -> p m c", m=2)[:, :, cs:W]
                o5 = PB[5].rearrange("p (m c) -> p m c", m=2)[:, :, cs:W]
                emit(PE, lambda: nc.tensor.matmul(o4, lhsT=bk["v"], rhs=rhs, start=st, stop=sp),
                     tuple(bk["rd"]) + (PT_b[pi],), (PB_b[4],))
                emit(PE, lambda: nc.tensor.matmul(o5, lhsT=ones_b[0:nk, :], rhs=rhs, start=st, stop=sp),
                     (CONST_b, PT_b[pi]), (PB_b[5],), mark=True)

            def v2(ap512):
                return ap512.rearrange("p (m c) -> p m c", m=2)[:, :, 0:W]

            T1f, T3f, T2 = XT[0][:, 0:512], XT[0][:, 512:1024], XT[1][:, 0:W]
            B1a, B1, B2 = XT_b[0][0], XT_b[0][1], XT_b[1]

            def part1a():
                emit(DVE, lambda: nc.vector.tensor_copy(v2(T1f), v2(PB[4])), (PB_b[4],), (B1a,))
                emit(ACT, lambda: nc.scalar.activation(out=v2(T3f), in_=v2(PB[5]), func=AF.Ln), (PB_b[5],), (B1,))

            def part1():
                emit(ACT, lambda: nc.scalar.activation(out=v2(T3f), in_=v2(T3f), func=AF.Exp, scale=-1.0), (B1,), (B1,))
                emit(DVE, lambda: nc.vector.tensor_tensor(v2(T1f), v2(T1f), v2(T3f), op=ALU.mult), (B1a, B1), (B1a,))
                emit(DVE, lambda: nc.vector.scalar_tensor_tensor(out=T2, in0=T1f[:, 256:256 + W], scalar=NLAM[:, l:l + 1], in1=T1f[:, 0:W],
                                                                 op0=ALU.mult, op1=ALU.add), (B1a, VEC_b), (B2,))

            def part2():
                emit(ACT, lambda: nc.scalar.activation(out=T1f[:, 0:W], in_=T2, func=AF.Square), (B2,), (B1a,))
                p, c0 = 0, 0
                emit(PE, lambda: nc.tensor.matmul(PB[2 * p][:, c0:c0 + W], lhsT=ones_f[:, :], rhs=T1f[:, 0:W], start=True, stop=True),
                     (CONST_b, B1a), (PB_b[2 * p],), mark=True)
                emit(ACT, lambda: nc.scalar.activation(out=T3f[:, 0:W], in_=PB[2 * p][:, c0:c0 + W], func=AF.Ln, scale=1.0 / 128,
                                                       bias=EPSC[:, 0:1]), (PB_b[2 * p], CONST_b), (B1,))
                emit(ACT, lambda: nc.scalar.activation(out=T3f[:, 0:W], in_=T3f[:, 0:W], func=AF.Exp, scale=-0.5), (B1,), (B1,))
                emit(DVE, lambda: nc.vector.scalar_tensor_tensor(out=T2, in0=T2, scalar=SG[:, l:l + 1], in1=T3f[:, 0:W],
                                                                 op0=ALU.mult, op1=ALU.mult), (B1, B2, VEC_b), (B2,))
                emit(DVE, lambda: nc.vector.tensor_tensor(YM[ym][:, q0:q1], T2, SGB[:, q0:q1], op=ALU.mult),
                     (B2, SGB_b), (YM_b[ym],))

            ng = (nb + 1) // 2

            def sgroup(g):
                for bi in range(2 * g, min(2 * g + 2, nb)):
                    scores(bi)

            sgroup(0)
            if ng > 1:
                sgroup(1)
            for g in range(ng):
                vis = list(range(2 * g, min(2 * g + 2, nb)))
                if g == 0 and pend is not None:
                    pend[0]()
                for bi in vis:
                    probs(bi)
                if g == 0 and pend is not None:
                    pend[1]()
                for bi in vis:
                    pv(bi)
                if g == 0 and pend is not None:
                    pend[2]()
                if g + 2 < ng:
                    sgroup(g + 2)
                yield
            return (part1a, part1, part2)

        def e_load(h):
            ET, ET_b, EM, EM_b, EMM, ES, ES_b = ET2[h % 2], ET_b2[h % 2], EM2[h % 2], EM_b2[h % 2], EMM2[h % 2], ES2[h % 2], ES_b2[h % 2]
            for e in range(NE):
                delta = -640 + 128 * e
                emit_dma(SP, lambda e=e, delta=delta: nc.sync.dma_start(out=ET[:, e, :], in_=hankel(h, delta - 255 + OFF, 128, 256)),
                         (ev_b,), (ET_b[e],))
            for k in range(3):
                emit_dma(SP, lambda k=k: nc.sync.dma_start(out=EM[:, k, :], in_=hankel(h, -16 - 256 * k - 255 + OFF, 16, 256)),
                         (ev_b,), (EM_b,))
            emit_dma(SP, lambda: nc.sync.dma_start(out=EMM[:, :], in_=hankel(h, -15 + OFF, 16, 16)), (ev_b,), (EM_b,))
            for jb in range(11, 16):
                emit_dma(SP, lambda jb=jb: nc.sync.dma_start(out=ES[:, jb - 11, :], in_=hankel(h, 128 * jb - 2048 - 31 + OFF, 128, 32)),
                         (ev_b,), (ES_b,))
            emit_dma(SP, lambda: nc.sync.dma_start(out=ES[0:32, 5, :], in_=hankel(h, -31 + OFF, 32, 32)), (ev_b,), (ES_b,))

        def head(l, h):
            (sk, sv, sq_, sg), widx = w_get(l, h, ("k", "v", "q", "gb"))
            ym = 1
            ET, ET_b, EM, EM_b, EMM, ES, ES_b = ET2[h % 2], ET_b2[h % 2], EM2[h % 2], EM_b2[h % 2], EMM2[h % 2], ES2[h % 2], ES_b2[h % 2]
            emit(DVE, lambda: nc.vector.memset(ET[64:128, 5, 192:256], 0.0), (), (ET_b[5],))
            emit(DVE, lambda: nc.vector.memset(ET[64:128, 6, 64:128], 0.0), (), (ET_b[6],))
            hs = int(os.environ.get("MK_HSTOP", 99))
            if hs < 1:
                return
            def cache_load(hh):
                emit_dma(POOL, lambda: nc.gpsimd.dma_start(out=KCT[:], in_=ck[l][:, hh * 128:(hh + 1) * 128].rearrange("(b p) d -> p b d", p=128)),
                         (), tuple(KCT_b))
                emit_dma(POOL, lambda: nc.gpsimd.dma_start(out=VC[:], in_=cv[l][:, hh * 128:(hh + 1) * 128].rearrange("(b p) d -> p b d", p=128)),
                         (), (VC_b,))

            if h == 0:
                cache_load(0)
            if hs < 2:
                return
            for tb in range(int(os.environ.get("MK_TB0", 0)), int(os.environ.get("MK_TBMAX", NTB))):
                r0, R = tb_rows(tb)
                b = mmbank()
                rb = [WS_b[sk], WS_b[sv]] + [uT_b[i] for i in spans_of(r0, r0 + R)]
                for dc in range(16):
                    emit(PE, lambda dc=dc: nc.tensor.matmul(PB[b][0:R, 0:256], lhsT=uT[:, dc, r0:r0 + R], rhs=WS[:, sk:sk + 2, dc, :],
                                                            start=(dc == 0), stop=(dc == 15)), rb, (PB_b[b],), mark=(dc == 15))
                ks = tb % 2
                copy(ACT, KVS[ks][0:R, :], PB[b][0:R, 0:256], (PB_b[b],), (KVS_b[ks],))
                copy(DVE, VH[0:R, tb, :], KVS[ks][0:R, 0:256], (KVS_b[ks],), (VH_b,))
                cols = slice(h * 128, (h + 1) * 128)
                if os.environ.get("MK_NOKVDMA"):
                    continue
                if tb == 0:
                    emit_dma(ACT, lambda: nc.scalar.dma_start(out=kso[l][:, cols], in_=KVS[ks][0:32, 0:128]), (KVS_b[ks],), ())
                    emit_dma(ACT, lambda: nc.scalar.dma_start(out=vso[l][:, cols], in_=KVS[ks][0:32, 128:256]), (KVS_b[ks],), ())
                    emit_dma(ACT, lambda: nc.scalar.dma_start(out=kp[l][0:16, cols], in_=KVS[ks][32:48, 0:128]), (KVS_b[ks],), ())
                    emit_dma(ACT, lambda: nc.scalar.dma_start(out=vp[l][0:16, cols], in_=KVS[ks][32:48, 128:256]), (KVS_b[ks],), ())
                else:
                    p0 = 16 + 128 * (tb - 1)
                    emit_dma(ACT, lambda: nc.scalar.dma_start(out=kp[l][p0:p0 + 128, cols], in_=KVS[ks][:, 0:128]), (KVS_b[ks],), ())
                    emit_dma(ACT, lambda: nc.scalar.dma_start(out=vp[l][p0:p0 + 128, cols], in_=KVS[ks][:, 128:256]), (KVS_b[ks],), ())
                yield
            if hs < 3:
                return
            b = mmbank()
            for dc in range(16):
                emit(PE, lambda dc=dc: nc.tensor.matmul(PB[b][0:16, 0:128], lhsT=uT[:, dc, 32:48], rhs=WS[:, sv, dc, :],
                                                        start=(dc == 0), stop=(dc == 15)), (WS_b[sv], uT_b[0]), (PB_b[b],), mark=(dc == 15))
            copy(DVE, VM[:, :], PB[b][0:16, 0:128], (PB_b[b],), (VM_b,))
            for grp in ([0], [1, 2, 3, 4], [5, 6, 7, 8], [9, 10, 11, 12], [13, 14, 15, 16]):
                b = mmbank()
                pvw = PB[b][:].bitcast(BF16)
                off = 0
                for gi, tb in enumerate(grp):
                    r0, R = tb_rows(tb)
                    emit(PE, lambda tb=tb, R=R, off=off: nc.tensor.transpose(pvw[:, off:off + R], VH[0:R, tb, 0:128], ident[0:R, 0:R]),
                         (VH_b, CONST_b), (PB_b[b],), mark=(gi == len(grp) - 1))
                    off += R
                c0 = tb_rows(grp[0])[0]
                copy(evac_eng(), KT[:, c0:c0 + off], pvw[:, 0:off], (PB_b[b],), (KT_b,) + YMB_b[0] + YMB_b[1])
            yield
            for si, (c0, c1) in enumerate(SPANS):
                b = proj_fm(sq_, c0, c1)
                copy(evac_eng(), QT[:, c0:c1], PB[b][:, 0:c1 - c0], (PB_b[b],), (QT_b,) + YMB_b[0] + YMB_b[1])
                b = proj_fm(sg, c0, c1)
                emit(ACT, lambda b=b: nc.scalar.activation(out=SGB[:, c0:c1], in_=PB[b][:, 0:c1 - c0], func=AF.Exp, scale=-1.0),
                     (PB_b[b],), (SGB_b,))
                emit(ACT, lambda: nc.scalar.activation(out=SGB[:, c0:c1], in_=SGB[:, c0:c1], func=AF.Ln, bias=ONEC[:, 0:1]),
                     (SGB_b, CONST_b), (SGB_b,))
                emit(ACT, lambda: nc.scalar.activation(out=SGB[:, c0:c1], in_=SGB[:, c0:c1], func=AF.Exp, scale=-1.0),
                     (SGB_b,), (SGB_b,))
                emit(DVE, lambda b=b: nc.vector.tensor_tensor(SGB[:, c0:c1], PB[b][:, 0:c1 - c0], SGB[:, c0:c1], op=ALU.mult),
                     (PB_b[b], SGB_b), (SGB_b,))
                yield
            w_release(widx)
            if not (l == nl - 1 and h == nun - 1):
                e_load((h + 1) % nun)
            if hs < 4:
                return
            for g in range(4):
                slot = g % 2
                pv = PB[slot][:].bitcast(BF16)
                mmrr[0] = (slot + 1) % 4
                for i in range(4):
                    jb = 4 * g + i
                    emit(PE, lambda i=i, jb=jb: nc.tensor.transpose(pv[:, i * 128:(i + 1) * 128], KCT[:, jb, :], ident[:, :]),
                         tuple(KCT_b) + (CONST_b,), (PB_b[slot],), mark=(i == 3))
                copy(evac_eng(), KTC[:, 512 * g:512 * (g + 1)], pv[:, 0:512], (PB_b[slot],), (KTC_b,))
            if hs < 5:
                return
            blocks = []
            for jb in range(16):
                const = (128 * jb + 127 - 2048) <= R15
                blocks.append(dict(kt=KTC[:, 128 * jb:128 * (jb + 1)], v=VC[:, jb, :], nk=128, cs=0, rd=[KTC_b, VC_b],
                                   e=('c',) if const else ('h', ES[:, jb - 11, ::-1]), erd=[ES_b]))
            blocks.append(dict(kt=KT[:, 0:32], v=VH[0:32, 0, 128:256], nk=32, cs=0, rd=[KT_b, VH_b], e=('h', ES[0:32, 5, ::-1]), erd=[ES_b]))
            pend = yield from attend(l, h, 0, 32, blocks, ym, None)
            if h + 1 < nun:
                cache_load(h + 1)
            if hs < 6:
                return
            blocks = [dict(kt=KT[:, 32:48], v=VM[:, :], nk=16, cs=0, rd=[KT_b, VM_b], e=('h', EMM[:, ::-1]), erd=[EM_b])]
            pend = yield from attend(l, h, 32, 48, blocks, ym, pend)
            if hs < 7:
                return
            for k in range(8):
                q0 = 48 + 256 * k
                q1 = q0 + 256
                if -1 - 256 * k > R15:
                    blocks = [dict(kt=KT[:, 32:48], v=VM[:, :], nk=16, cs=0, rd=[KT_b, VM_b], e=('h', EM[:, k, ::-1]), erd=[EM_b])]
                else:
                    blocks = [dict(kt=KT[:, 32:48], v=VM[:, :], nk=16, cs=0, rd=[KT_b, VM_b], e=('c',), erd=[])]
                for jb in range(2 * k + 2):
                    delta = 128 * jb - 256 * k
                    kc = 48 + 128 * jb
                    d = dict(kt=KT[:, kc:kc + 128], v=VH[:, 1 + jb, 128:256], nk=128, cs=max(0, delta), rd=[KT_b, VH_b])
                    if delta + 127 <= R15:
                        d["e"] = ('c',)
                        d["erd"] = []
                    else:
                        e = (delta + 640) // 128
                        cs = d["cs"]
                        d["e"] = ('h', ET[:, e, 255 - cs::-1] if cs > 0 else ET[:, e, ::-1])
                        d["erd"] = [ET_b[e]]
                    blocks.append(d)
                pend = yield from attend(l, h, q0, q1, blocks, ym, pend)
            pend[0]()
            pend[1]()
            pend[2]()
            emit_dma(SP, lambda: nc.sync.dma_start(out=ymscr[8 + h], in_=YM[ym][:, :]), (YM_b[ym],), (ymscr_b[8 + h],))

        def load_gates(l):
            emit(DVE, lambda: nc.vector.memset(GW[:], 0.0), (), (GW_b,))
            for gi, gw in enumerate((grw, giw)):
                for half in range(2):
                    src = gw[l].rearrange("(j t) c d -> t c j d", t=2)[half]
                    emit_dma(POOL, lambda src=src, gi=gi, half=half: nc.gpsimd.dma_start(
                        out=GW[half * 64:(half + 1) * 64, gi * 8:(gi + 1) * 8, half * 64:(half + 1) * 64], in_=src), (), (GW_b,))

        def phaseC(l):
            allu = list(uT_b)
            for cc in range(16):
                emit_dma(POOL, lambda cc=cc: nc.gpsimd.dma_start(out=WO[:, cc, :], in_=w_out[l][cc * 128:(cc + 1) * 128, :]), (), allu)
            emit_dma(SP, lambda: nc.sync.dma_start(out=GB[:], in_=post_g[l].partition_broadcast(128)), (), (GB_b,))
            last = (l == nl - 1)
            def c_loads(tb):
                r0, R = tb_rows(tb)
                emit_dma(SP, lambda: nc.sync.dma_start(out=YMB[tb % 2][:, :, 0:R], in_=ymscr[:, :, r0:r0 + R].rearrange("c p n -> p c n")),
                         ymscr_b, YMB_b[tb % 2] + ((QT_b, KT_b) if tb < 2 else ()))
                load_x(l, tb, tb % 2)

            c_loads(0)
            for tb in range(NTB):
                r0, R = tb_rows(tb)
                xt = tb % 2
                yb = tb % 2
                if tb + 1 < NTB:
                    c_loads(tb + 1)
                ab = 4 if tb % 2 == 0 else 0
                SS_b, RSTD_b = SS_bb[xt], RSTD_bb[xt]
                sso = 8 * xt + 4
                rsc = RSTD[0:R, 2 * xt + 1:2 * xt + 2]
                for cc in range(16):
                    for g in range(4):
                        emit(PE, lambda cc=cc, g=g: nc.tensor.matmul(PB[ab + g][0:R, :], lhsT=YMB[yb][:, cc, 0:R],
                                                                     rhs=WO[:, cc, g * 512:(g + 1) * 512],
                                                                     start=(cc == 0), stop=(cc == 15)),
                             list(YMB_b[yb]) + allu, (PB_b[ab + g],), mark=(cc == 15))
                for g in range(4):
                    tc_ = g % 2
                    emit(ACT, lambda g=g, tc_=tc_: nc.scalar.activation(out=TMPC[tc_][0:R, :], in_=PB[ab + g][0:R, :], func=AF.Square,
                                                                        accum_out=SS[0:R, sso + g:sso + g + 1]),
                         (PB_b[ab + g],), (TMPC_b[tc_], SS_b))
                emit(DVE, lambda: nc.vector.tensor_reduce(out=rsc, in_=SS[0:R, sso:sso + 4], axis=AX.X, op=ALU.add),
                     (SS_b,), (RSTD_b,))
                emit(ACT, lambda: nc.scalar.activation(out=rsc, in_=rsc, func=AF.Ln, scale=1.0 / D, bias=EPSC[0:R, 0:1]),
                     (RSTD_b, CONST_b), (RSTD_b,))
                emit(ACT, lambda: nc.scalar.activation(out=rsc, in_=rsc, func=AF.Exp, scale=-0.5),
                     (RSTD_b,), (RSTD_b,))
                for g in range(4):
                    tc_ = g % 2
                    emit(DVE, lambda g=g, tc_=tc_: nc.vector.scalar_tensor_tensor(out=TMPC[tc_][0:R, :], in0=PB[ab + g][0:R, :],
                                                                                  scalar=rsc, in1=GB[0:R, g * 512:(g + 1) * 512],
                                                                                  op0=ALU.mult, op1=ALU.mult),
                         (PB_b[ab + g], RSTD_b, GB_b), (TMPC_b[tc_],))
                    emit(DVE, lambda g=g, tc_=tc_: nc.vector.tensor_tensor(XT[xt][0:R, g * 512:(g + 1) * 512], XT[xt][0:R, g * 512:(g + 1) * 512],
                                                                           TMPC[tc_][0:R, :], op=ALU.add),
                         (TMPC_b[tc_], XT_b[xt]), (XT_b[xt],))
                if last:
                    if tb == 0:
                        emit_dma(SP, lambda: nc.sync.dma_start(out=ys, in_=XT[xt][0:32, :]), (XT_b[xt],), ())
                    else:
                        emit_dma(SP, lambda: nc.sync.dma_start(out=yp[(tb - 1) * 128:tb * 128, :], in_=XT[xt][:, :]), (XT_b[xt],), ())
                else:
                    emit_dma(SP, lambda: nc.sync.dma_start(out=xscr[l % 2][r0:r0 + R, :], in_=XT[xt][0:R, :]),
                             (XT_b[xt],), (xscr_b[l % 2][tb],))

        for i in range(min(NSLOT, len(wq))):
            w_issue(i)
        e_load(0)
        stop = int(os.environ.get("MK_STOP", 9))
        KINT = int(os.environ.get("MK_KINT", 1))
        for l in range(nl):
            if stop >= 1:
                load_gates(l)
                halo(l, 0)
                phaseA(l)
            if stop >= 2:
                for u in range(nun):
                    gH = head(l, u)
                    gA = mixerA(l, u)
                    next(gH)
                    aliveA = True
                    cnt = 0
                    for _ in gH:
                        cnt += 1
                        if aliveA and cnt % KINT == 0:
                            for _k in range(3 if cnt <= 22 else 1):
                                try:
                                    next(gA)
                                except StopIteration:
                                    aliveA = False
                                    break
                    if aliveA:
                        for _ in gA:
                            pass
            if stop >= 4:
                phaseC(l)

        for e in (SP, POOL, ACT):
            for i, sem in enumerate(e.dsems):
                if e.dtot[i] > 0:
                    nc.sync.wait_ge(sem, e.dtot[i])
    return nc


_CACHE = {}


def _sel_matrix():
    s = np.zeros((32, NREL), np.float32)
    s[BUCKETS, np.arange(NREL)] = 1.0
    return s


def kernel(x_prompt, x_sample, cache_k, cache_v, state_conv, state_rglru, meta, rel_bias,
           pre_g, post_g, w_in, conv_w, conv_b, gate_r_w, gate_r_b, gate_i_w, gate_i_b,
           rglru_lam, lam_q1, lam_k1, lam_q2, lam_k2, subln_g, w_out):
    nl = int(os.environ.get("MK_LAYERS", DEPTH))
    ncores = int(os.environ.get("MK_CORES", NCORES))
    if nl not in _CACHE:
        _CACHE[nl] = build(nl)
    nc = _CACHE[nl]
    f = lambda a: np.ascontiguousarray(np.asarray(a, dtype=np.float32))
    shared = {
        "meta": f(meta), "rel_bias": f(rel_bias), "pre_g": f(pre_g), "post_g": f(post_g), "w_in": f(w_in),
        "conv_w": f(conv_w), "conv_b": f(conv_b), "gate_r_w": f(gate_r_w),
        "gate_r_b": f(gate_r_b).reshape(DEPTH, 1024), "gate_i_w": f(gate_i_w),
        "gate_i_b": f(gate_i_b).reshape(DEPTH, 1024), "rglru_lam": f(rglru_lam),
        "lam_q1": f(lam_q1), "lam_k1": f(lam_k1), "lam_q2": f(lam_q2), "lam_k2": f(lam_k2),
        "subln_g": f(subln_g), "w_out": f(w_out), "sel": _sel_matrix(),
    }
    in_maps = []
    for c in range(ncores):
        m = dict(shared)
        m["xp"] = f(x_prompt[c])
        m["xs"] = f(x_sample[c])
        m["ck"] = f(np.asarray(cache_k)[:, c].reshape(DEPTH, 2048, 1024))
        m["cv"] = f(np.asarray(cache_v)[:, c].reshape(DEPTH, 2048, 1024))
        m["sc"] = f(np.asarray(state_conv)[:, c])
        m["sr"] = f(np.asarray(state_rglru)[:, c])
        in_maps.append(m)
    res = run_bass_kernel_spmd(nc, in_maps, core_ids=list(range(ncores)))
    R = res.results
    st = lambda k, ax: np.stack([np.asarray(r[k]) for r in R], axis=ax)
    y_prompt = st("yp", 0)
    y_sample = st("ys", 0)
    k_prompt = st("kp", 1).reshape(DEPTH, ncores, 2064, 8, 128)
    v_prompt = st("vp", 1).reshape(DEPTH, ncores, 2064, 8, 128)
    conv_prompt = st("cp", 1)
    rglru_prompt = st("rp", 1)
    k_sample = st("ks", 1).reshape(DEPTH, ncores, 32, 8, 128)
    v_sample = st("vs", 1).reshape(DEPTH, ncores, 32, 8, 128)
    conv_sample = st("cs", 1)
    rglru_sample = st("rs", 1)
    return (y_prompt, y_sample, k_prompt, v_prompt, conv_prompt, rglru_prompt,
            k_sample, v_sample, conv_sample, rglru_sample)
```
